# Optimizing a Trainium2 kernel written in Bass

```python
import jax, jax.numpy as jnp
from jax import lax
import numpy as np

D_MODEL = 1024
BATCH = 4
SEQ = 8192
DEPTH = 2

BLOCK = 128
HEAD_DIM = 64
N_HEADS_FOX = 16
N_HEADS_DIL = 16
WIDTH_FOX = N_HEADS_FOX * HEAD_DIM
WIDTH_DIL = N_HEADS_DIL * HEAD_DIM
DIL_PATTERNS = ((128, 1), (512, 4), (2048, 16))
N_HEADS_RET = 4
RET_QK_DIM = 256
RET_V_DIM = 512
RET_CHUNK = 128
ROT_BASE = 10000.0
EPS = 1e-6
NEG = -1e30
N_EVEN = (DEPTH + 1) // 2
N_ODD = DEPTH // 2
EVEN_IN = 4 * WIDTH_FOX + N_HEADS_FOX + 4 * WIDTH_DIL
EVEN_MIX = WIDTH_FOX + WIDTH_DIL
ODD_IN = 2 * N_HEADS_RET * RET_QK_DIM + 2 * N_HEADS_RET * RET_V_DIM
ODD_MIX = N_HEADS_RET * RET_V_DIM

kernel_name = "hybrid_fox_dilated_retention_trunk"

f32 = jnp.float32


def rms_norm(x, g):
    xf = x.astype(f32)
    y = xf * lax.rsqrt(jnp.mean(xf * xf, axis=-1, keepdims=True) + EPS)
    return (y * g.astype(f32)).astype(x.dtype)


def forgetting_attention(q, k, v, log_f):
    Bsz, S, H, hd = q.shape
    scale = hd ** -0.5
    c = jnp.cumsum(log_f, axis=1).transpose(0, 2, 1)
    outs = []
    for i in range(S // BLOCK):
        q0, e = i * BLOCK, (i + 1) * BLOCK
        s = jnp.einsum('bqhd,bkhd->bhqk', q[:, q0:e], k[:, :e]).astype(f32) * scale
        s = s + c[:, :, q0:e, None] - c[:, :, None, :e]
        causal = jnp.arange(e)[None, :] <= (q0 + jnp.arange(BLOCK))[:, None]
        p = jax.nn.softmax(jnp.where(causal, s, NEG), axis=-1)
        outs.append(jnp.einsum('bhqk,bkhd->bqhd', p.astype(v.dtype), v[:, :e]))
    return jnp.concatenate(outs, axis=1)


def dilated_pattern(q, k, v, window, dilation):
    Bsz, S, H, hd = q.shape
    n_keys = window // dilation
    span = dilation * BLOCK
    Sp = -(-S // span) * span
    L = Sp // dilation
    nb = L // BLOCK

    def streams(t):
        t = jnp.pad(t, ((0, 0), (0, Sp - S), (0, 0), (0, 0))).reshape(Bsz, L, dilation, H, hd)
        return t.transpose(0, 2, 3, 1, 4).reshape(Bsz, dilation, H, nb, BLOCK, hd)

    def with_prev(t):
        prev = jnp.pad(t, ((0, 0), (0, 0), (0, 0), (1, 0), (0, 0), (0, 0)))[:, :, :, :-1]
        return jnp.concatenate([prev, t], axis=4)

    qs = streams(q)
    kb, vb = with_prev(streams(k)), with_prev(streams(v))
    s = jnp.einsum('brhnqd,brhnkd->brhnqk', qs, kb).astype(f32) * (hd ** -0.5)
    qi = jnp.arange(BLOCK)[:, None]
    kj = jnp.arange(2 * BLOCK)[None, :]
    dist = BLOCK + qi - kj
    band = (dist >= 0) & (dist <= n_keys)
    has_prev = (jnp.arange(nb) > 0)[:, None, None] | (kj >= BLOCK)[None]
    mask = band[None] & has_prev
    s = jnp.where(mask, s, NEG)
    lse = jax.nn.logsumexp(s, axis=-1)
    p = jnp.exp(s - lse[..., None])
    o = jnp.einsum('brhnqk,brhnkd->brhnqd', p.astype(v.dtype), vb)
    o = o.reshape(Bsz, dilation, H, L, hd).transpose(0, 3, 1, 2, 4).reshape(Bsz, Sp, H, hd)[:, :S]
    lse = lse.reshape(Bsz, dilation, H, L).transpose(0, 3, 1, 2).reshape(Bsz, Sp, H)[:, :S]
    return o, lse


def dilated_attention(q, k, v):
    outs, lses = [], []
    for window, dilation in DIL_PATTERNS:
        o, lse = dilated_pattern(q, k, v, window, dilation)
        outs.append(o)
        lses.append(lse)
    wts = jax.nn.softmax(jnp.stack(lses), axis=0)
    return jnp.einsum('pbsh,pbshd->bshd', wts.astype(q.dtype), jnp.stack(outs))


def rotate(x):
    S, half = x.shape[1], x.shape[-1] // 2
    inv = 1.0 / (ROT_BASE ** jnp.linspace(0.0, 1.0, half, dtype=f32))
    ang = jnp.arange(S, dtype=f32)[:, None] * inv[None, :]
    cos, sin = jnp.cos(ang)[None, :, None, :], jnp.sin(ang)[None, :, None, :]
    x1, x2 = x[..., 0::2].astype(f32), x[..., 1::2].astype(f32)
    y = jnp.stack([x1 * cos - x2 * sin, x1 * sin + x2 * cos], axis=-1)
    return y.reshape(x.shape).astype(x.dtype)


def retention(q, k, v):
    Bsz, S, H, dk = q.shape
    dv = v.shape[-1]
    C = RET_CHUNK
    nC = S // C
    log_gamma = jnp.log1p(-jnp.power(2.0, -5.0 - jnp.arange(H, dtype=f32)))
    pos = jnp.arange(C, dtype=f32)
    rel = pos[:, None] - pos[None, :]
    intra = jnp.where(rel >= 0, jnp.exp(log_gamma[:, None, None] * jnp.maximum(rel, 0.0)), 0.0)
    chunks = lambda t: t.reshape(Bsz, nC, C, H, t.shape[-1]).transpose(1, 0, 3, 2, 4)
    qc, kc, vc = chunks(q), chunks(k), chunks(v)
    s = jnp.einsum('nbhqd,nbhkd->nbhqk', qc, kc).astype(f32) * intra
    inner = jnp.einsum('nbhqk,nbhkd->nbhqd', s.astype(v.dtype), vc).astype(f32)
    q_decay = jnp.exp(log_gamma[:, None] * (pos + 1.0))[None, :, :, None]
    k_decay = jnp.exp(log_gamma[:, None] * (C - 1.0 - pos))[None, :, :, None]
    chunk_decay = jnp.exp(log_gamma * C)[None, :, None, None]

    def step(state, inp):
        q_i, k_i, v_i = inp
        cross = jnp.einsum('bhqd,bhde->bhqe', q_i.astype(f32), state) * q_decay
        state = state * chunk_decay + jnp.einsum('bhkd,bhke->bhde', k_i.astype(f32) * k_decay, v_i.astype(f32))
        return state, cross

    _, cross = lax.scan(step, jnp.zeros((Bsz, H, dk, dv), f32), (qc, kc, vc))
    out = inner + cross
    return out.transpose(1, 0, 3, 2, 4).reshape(Bsz, S, H, dv)


def head_group_norm(y):
    mu = jnp.mean(y, axis=-1, keepdims=True)
    var = jnp.mean(jnp.square(y - mu), axis=-1, keepdims=True)
    return (y - mu) * lax.rsqrt(var + EPS)


def even_layer(x, g_norm, w_in, b_f, w_out):
    Bsz, S, _ = x.shape
    z = rms_norm(x, g_norm) @ w_in
    cuts = [WIDTH_FOX, 2 * WIDTH_FOX, 3 * WIDTH_FOX, 4 * WIDTH_FOX,
            4 * WIDTH_FOX + N_HEADS_FOX,
            4 * WIDTH_FOX + N_HEADS_FOX + WIDTH_DIL,
            4 * WIDTH_FOX + N_HEADS_FOX + 2 * WIDTH_DIL,
            4 * WIDTH_FOX + N_HEADS_FOX + 3 * WIDTH_DIL]
    qa, ka, va, ga, fa, qb, kb, vb, gb = jnp.split(z, cuts, axis=-1)
    hA = lambda t: t.reshape(Bsz, S, N_HEADS_FOX, HEAD_DIM)
    hB = lambda t: t.reshape(Bsz, S, N_HEADS_DIL, HEAD_DIM)
    log_f = jax.nn.log_sigmoid(fa.astype(f32) + b_f.astype(f32))
    ya = forgetting_attention(hA(qa), hA(ka), hA(va), log_f).reshape(Bsz, S, WIDTH_FOX)
    yb = dilated_attention(hB(qb), hB(kb), hB(vb)).reshape(Bsz, S, WIDTH_DIL)
    y = jnp.concatenate([ya * jax.nn.silu(ga), yb * jax.nn.silu(gb)], axis=-1)
    return x + y @ w_out


def odd_layer(x, g_norm, w_in, w_out):
    Bsz, S, _ = x.shape
    qk_w, v_w = N_HEADS_RET * RET_QK_DIM, N_HEADS_RET * RET_V_DIM
    z = rms_norm(x, g_norm) @ w_in
    q, k, v, g = jnp.split(z, [qk_w, 2 * qk_w, 2 * qk_w + v_w], axis=-1)
    q = rotate(q.reshape(Bsz, S, N_HEADS_RET, RET_QK_DIM))
    k = rotate(k.reshape(Bsz, S, N_HEADS_RET, RET_QK_DIM)) * (RET_QK_DIM ** -0.5)
    y = retention(q, k, v.reshape(Bsz, S, N_HEADS_RET, RET_V_DIM))
    y = head_group_norm(y).reshape(Bsz, S, v_w).astype(x.dtype)
    return x + (y * jax.nn.silu(g)) @ w_out


def setup_inputs(seed: int = 0) -> dict:
    key = jax.random.key(seed)
    ks = jax.random.split(key, 12)
    x = jax.random.normal(ks[0], (BATCH, SEQ, D_MODEL), f32)
    even_norm = 1.0 + 0.02 * jax.random.normal(ks[1], (N_EVEN, D_MODEL), f32)
    even_w_in = jax.random.normal(ks[2], (N_EVEN, D_MODEL, EVEN_IN), f32) * D_MODEL ** -0.5
    even_b_f = jax.random.uniform(ks[3], (N_EVEN, N_HEADS_FOX), f32, 1.0, 5.0)
    even_w_out = jax.random.normal(ks[4], (N_EVEN, EVEN_MIX, D_MODEL), f32) * EVEN_MIX ** -0.5
    odd_norm = 1.0 + 0.02 * jax.random.normal(ks[5], (N_ODD, D_MODEL), f32)
    odd_w_in = jax.random.normal(ks[6], (N_ODD, D_MODEL, ODD_IN), f32) * D_MODEL ** -0.5
    odd_w_out = jax.random.normal(ks[7], (N_ODD, ODD_MIX, D_MODEL), f32) * ODD_MIX ** -0.5
    final_norm = 1.0 + 0.02 * jax.random.normal(ks[8], (D_MODEL,), f32)
    return {"x": x, "even_norm": even_norm, "even_w_in": even_w_in, "even_b_f": even_b_f,
            "even_w_out": even_w_out, "odd_norm": odd_norm, "odd_w_in": odd_w_in,
            "odd_w_out": odd_w_out, "final_norm": final_norm}


def reference(x, even_norm, even_w_in, even_b_f, even_w_out, odd_norm, odd_w_in, odd_w_out, final_norm):
    for layer in range(DEPTH):
        j = layer // 2
        if layer % 2 == 0:
            x = even_layer(x, even_norm[j], even_w_in[j], even_b_f[j], even_w_out[j])
        else:
            x = odd_layer(x, odd_norm[j], odd_w_in[j], odd_w_out[j])
    return rms_norm(x, final_norm)
```

```python
import concourse.bass as bass
import concourse.mybir as mybir

ENGS = ("pe", "act", "dve", "pool", "sp")
DMA_POOL = 12


class Sched:
    def __init__(self, nc, same_engine_sync=True):
        self.nc = nc
        self.q = {e: [] for e in ENGS}
        self.cnt = {e: 0 for e in ENGS}
        self.sem = {e: nc.alloc_semaphore(f"sq_{e}") for e in ENGS if e != "sp"}
        self.dsem = {}
        for e in ("sp", "pool", "act"):
            self.dsem[e] = [nc.alloc_semaphore(f"sd_{e}{i}") for i in range(DMA_POOL)]
        self.dcnt = {e: [0] * DMA_POOL for e in self.dsem}
        self.drr = {e: 0 for e in self.dsem}
        self.waited = {e: {} for e in ENGS}
        self.res = {}
        self.semobj = {}
        self.same = same_engine_sync
        self.nwaits = 0
        self.clear_sems()

    def all_sems(self):
        return list(self.sem.values()) + [s for e in self.dsem for s in self.dsem[e]]

    def clear_sems(self):
        nc = self.nc
        sems = self.all_sems()
        with nc.Block() as block:
            def body(g):
                for s in sems:
                    g.sem_clear(s)
            block.gpsimd(body)

    def _deps(self, eng, reads, writes):
        deps = {}

        def add(tok):
            if tok is None:
                return
            k, v = tok
            if deps.get(k, 0) < v:
                deps[k] = v

        for r in reads:
            st = self.res.get(r)
            if st is not None:
                add(st["w"])
        for w in writes:
            st = self.res.get(w)
            if st is not None:
                add(st["w"])
                for k, v in st["r"].items():
                    add((k, v))
        return deps

    def _commit(self, tok, reads, writes):
        for r in reads:
            st = self.res.setdefault(r, {"w": None, "r": {}})
            k, v = tok
            if st["r"].get(k, 0) < v:
                st["r"][k] = v
        for w in writes:
            self.res[w] = {"w": tok, "r": {}}

    def _waits(self, eng, deps, own_key):
        ws = []
        for k, v in deps.items():
            if k == own_key and (eng == "pe" or not self.same):
                continue
            if self.waited[eng].get(k, 0) >= v:
                continue
            self.waited[eng][k] = v
            ws.append((k, v))
        self.nwaits += len(ws)
        return ws

    def op(self, eng, fn, reads=(), writes=()):
        deps = self._deps(eng, reads, writes)
        sem = self.sem[eng]
        key = "q_" + eng
        self.semobj[key] = sem
        ws = self._waits(eng, deps, key)
        self.cnt[eng] += 1
        tok = (key, self.cnt[eng])
        self.q[eng].append((fn, ws, (sem, 1, self.cnt[eng])))
        self._commit(tok, reads, writes)
        return tok

    def dma(self, eng, fn, reads=(), writes=()):
        deps = self._deps(eng, reads, writes)
        i = self.drr[eng]
        self.drr[eng] = (i + 1) % DMA_POOL
        sem = self.dsem[eng][i]
        key = f"d_{eng}{i}"
        self.semobj[key] = sem
        prev = self.dcnt[eng][i]
        if prev > 0:
            deps[key] = max(deps.get(key, 0), 16 * prev)
        ws = self._waits(eng, deps, None)
        self.dcnt[eng][i] += 1
        tok = (key, 16 * self.dcnt[eng][i])
        self.q[eng].append((fn, ws, (sem, 16)))
        self._commit(tok, reads, writes)
        return tok

    def coll(self, src_ap, dst_ap, groups, reads=(), writes=()):
        nc = self.nc
        if not hasattr(self, "cc_toks"):
            self.cc_toks = []
        n = len(self.cc_toks)
        sem = nc.alloc_semaphore(f"ccs{n}")
        key = f"cc{n}"
        self.semobj[key] = sem
        deps = self._deps("pool", reads, writes)
        if self.cc_toks:
            pk, pv = self.cc_toks[-1]
            deps[pk] = max(deps.get(pk, 0), pv)
        ws = self._waits("pool", deps, None)
        tok = (key, 1)
        fn = lambda g: g.collective_compute("AllGather", mybir.AluOpType.bypass, replica_groups=groups,
                                            ins=[src_ap.opt()], outs=[dst_ap.opt()])
        self.q["pool"].append((fn, ws, (sem, None)))
        self._commit(tok, reads, writes)
        self.cc_toks.append(tok)
        return tok

    def barrier(self):
        toks = {}
        for e in ENGS:
            if e != "sp" and self.cnt[e] > 0:
                toks["q_" + e] = self.cnt[e]
        for e in self.dsem:
            for i in range(DMA_POOL):
                if self.dcnt[e][i] > 0:
                    toks[f"d_{e}{i}"] = 16 * self.dcnt[e][i]
        for k, v in getattr(self, "cc_toks", []):
            toks[k] = v
        for e in ENGS:
            ws = []
            for k, v in toks.items():
                if self.waited[e].get(k, 0) >= v:
                    continue
                self.waited[e][k] = v
                ws.append((k, v))
            if ws:
                self.q[e].append((None, ws, None))

    def emit(self):
        nc = self.nc
        self.barrier()
        if not any(self.q[e] for e in ENGS):
            return
        handles = {"pe": "tensor", "act": "scalar", "dve": "vector", "pool": "gpsimd", "sp": "sync"}
        if not hasattr(self, "base"):
            self.base = {}
        needed = {}
        for e in ENGS:
            for fn, ws, inc in self.q[e]:
                for k, v in ws:
                    if k.startswith("q_"):
                        needed.setdefault(k, set()).add(v)
        valmap = {}
        for k, idxs in needed.items():
            b = self.base.get(k, 0)
            for r, idx in enumerate(sorted(idxs)):
                valmap[(k, idx)] = b + r + 1
            self.base[k] = b + len(idxs)
        with nc.Block() as block:
            for e in ENGS:
                items = self.q[e]
                if not items:
                    continue

                def body(engh, items=items, e=e):
                    for fn, ws, inc in items:
                        for k, v in ws:
                            if k.startswith("q_"):
                                engh.wait_ge(self.semobj[k], valmap[(k, v)])
                            else:
                                engh.wait_ge(self.semobj[k], v)
                        if fn is not None:
                            inst = fn(engh)
                            if len(inc) == 3:
                                if ("q_" + e, inc[2]) in valmap:
                                    inst.then_inc(inc[0], 1)
                            elif inc[1] is None:
                                inst.then_inc(inc[0])
                            else:
                                inst.then_inc(inc[0], inc[1])

                getattr(block, handles[e])(body)
        self.q = {e: [] for e in ENGS}
        self.res = {}
import numpy as np
import concourse.bass as bass
import concourse.mybir as mybir
from contextlib import ExitStack

F32 = mybir.dt.float32
BF = mybir.dt.bfloat16
AF = mybir.ActivationFunctionType
ALU = mybir.AluOpType
AX = mybir.AxisListType

S = 8192
_UID = [0]
NTB = S // 512
NFEAT = 3072
NV = 1024
NW = NFEAT + NV + 8
MASKNEG = -30000.0


def build_proj_phase(nc, sc, x, w, gn, bfv, ident, featT, vtok, caug, ntb=NTB):
    with ExitStack() as es:
        Wb = es.enter_context(nc.sbuf_tensor("Wb_pa", [128, 8, NW], BF))
        gn_t = es.enter_context(nc.sbuf_tensor("gn_t_pa", [128, 8], F32))
        id_t = es.enter_context(nc.sbuf_tensor("id_t_pa", [128, 128], BF))
        eps_t = es.enter_context(nc.sbuf_tensor("eps_t_pa", [128, 1], F32))
        one_t = es.enter_context(nc.sbuf_tensor("one_t_pa", [128, 1], F32))
        nb_t = es.enter_context(nc.sbuf_tensor("nb_t_pa", [8, 1], F32))
        ones8 = es.enter_context(nc.sbuf_tensor("ones8_pa", [8, 512], F32))
        xb0 = es.enter_context(nc.sbuf_tensor("xb0_pa", [128, NW], F32))
        xb1 = es.enter_context(nc.sbuf_tensor("xb1_pa", [128, NW], F32))
        junk = es.enter_context(nc.sbuf_tensor("junk_pa", [128, 1024], F32))
        ss = es.enter_context(nc.sbuf_tensor("ss_pa", [128, 8], F32))
        xn0 = es.enter_context(nc.sbuf_tensor("xn0_pa", [128, 1024], BF))
        xn1 = es.enter_context(nc.sbuf_tensor("xn1_pa", [128, 1024], BF))
        xT0 = es.enter_context(nc.sbuf_tensor("xT0_pa", [128, 8, 512], BF))
        xT1 = es.enter_context(nc.sbuf_tensor("xT1_pa", [128, 8, 512], BF))
        stF = es.enter_context(nc.sbuf_tensor("stF_pa", [128, 8, 512], BF))
        stV = es.enter_context(nc.sbuf_tensor("stV_pa", [128, 4, 512], BF))
        fw = es.enter_context(nc.sbuf_tensor("fw_pa", [8, 2, 6, 512], F32))
        cs = es.enter_context(nc.sbuf_tensor("cs_pa", [8, 2, 6, 512], BF))
        psT0 = es.enter_context(nc.psum_tensor("psT0_pa", [128, 8, 128], BF))
        psT1 = es.enter_context(nc.psum_tensor("psT1_pa", [128, 8, 128], BF))
        psF0 = es.enter_context(nc.psum_tensor("psF0_pa", [128, 512], F32))
        psF1 = es.enter_context(nc.psum_tensor("psF1_pa", [128, 512], F32))
        psF2 = es.enter_context(nc.psum_tensor("psF2_pa", [128, 512], F32))
        psV0 = es.enter_context(nc.psum_tensor("psV0_pa", [128, 512], F32))
        psV1 = es.enter_context(nc.psum_tensor("psV1_pa", [128, 512], F32))
        psf = es.enter_context(nc.psum_tensor("psf_pa", [8, 512], F32))
        xb = [xb0, xb1]
        xn = [xn0, xn1]
        xT = [xT0, xT1]
        psT = [psT0, psT1]
        psF = [psF0, psF1, psF2]
        psV = [psV0, psV1]
        sc.dma("sp", lambda e: e.dma_start(out=gn_t[:], in_=gn), writes=["gn_t"])
        sc.dma("sp", lambda e: e.dma_start(out=id_t[:], in_=ident), writes=["id_t"])
        sc.dma("sp", lambda e: e.dma_start(out=nb_t[:], in_=bfv), writes=["nb_t"])
        sc.op("dve", lambda e: e.memset(eps_t[:], 1e-6), writes=["eps_t"])
        sc.op("dve", lambda e: e.memset(one_t[:], 1.0), writes=["one_t"])
        sc.op("dve", lambda e: e.memset(ones8[:], 1.0), writes=["ones8"])
        sc.op("dve", lambda e: e.tensor_scalar(out=nb_t[:], in0=nb_t[:], scalar1=-1.0, scalar2=None, op0=ALU.mult),
              reads=["nb_t"], writes=["nb_t"])
        for kc in range(8):
            s = kc % 2
            sc.dma("sp" if s == 0 else "pool",
                   lambda e, kc=kc, s=s: e.dma_start(out=xb[s][:, :], in_=w[kc * 128:(kc + 1) * 128, :]),
                   writes=[("xb", s)])
            eng = "dve" if s == 0 else "pool"
            sc.op(eng, lambda e, kc=kc, s=s: e.tensor_scalar(
                out=Wb[:, kc, :], in0=xb[s][:, :], scalar1=gn_t[:, kc:kc + 1], scalar2=None, op0=ALU.mult),
                reads=[("xb", s), "gn_t"], writes=[("Wb", kc)])
        WbR = [("Wb", kc) for kc in range(8)]
        evac_box = [0]

        def load_x(tb):
            s = tb % 2
            xv = xb[s][:, 0:4096].rearrange("p (j f) -> p j f", j=4)
            src = x[tb * 512:(tb + 1) * 512, :].rearrange("(j p) f -> p j f", p=128)
            sc.dma("sp", lambda e, xv=xv, src=src: e.dma_start(out=xv, in_=src), writes=[("xb", s)])

        def norm_tile(tb, j):
            s = tb % 2
            xv = xb[s][:, 0:4096].rearrange("p (j f) -> p j f", j=4)
            xs = (tb * 4 + j) % 2
            col = (tb * 4 + j) % 8
            sc.op("act", lambda e: e.activation(out=junk[:], in_=xv[:, j, :], func=AF.Square),
                  reads=[("xb", s)], writes=["junk"])
            sc.op("dve", lambda e: e.reduce_sum(out=ss[:, col:col + 1], in_=junk[:], axis=AX.X),
                  reads=["junk"], writes=[("ss", col)])
            sc.op("act", lambda e: e.activation(out=ss[:, col:col + 1], in_=ss[:, col:col + 1], func=AF.Sqrt,
                                                bias=eps_t[:, 0:1], scale=1.0 / 1024.0),
                  reads=[("ss", col), "eps_t"], writes=[("ss", col)])
            sc.op("dve", lambda e: e.reciprocal(out=ss[:, col:col + 1], in_=ss[:, col:col + 1]),
                  reads=[("ss", col)], writes=[("ss", col)])
            sc.op("dve", lambda e: e.tensor_scalar(out=xn[xs][:], in0=xv[:, j, :], scalar1=ss[:, col:col + 1], scalar2=None, op0=ALU.mult),
                  reads=[("xb", s), ("ss", col)], writes=[("xn", xs)])

        def transp_tile(tb, j):
            s = tb % 2
            xs = (tb * 4 + j) % 2
            for kc in range(8):
                sc.op("pe", lambda e, kc=kc: e.transpose(out=psT[xs][:, kc, :], in_=xn[xs][:, kc * 128:(kc + 1) * 128], identity=id_t[:]),
                      reads=[("xn", xs), "id_t"], writes=[("psT", xs)])
            sc.op("act", lambda e: e.copy(out=xT[s][:, :, j * 128:(j + 1) * 128], in_=psT[xs][:]),
                  reads=[("psT", xs)], writes=[("xT", s, j)])

        def feat_group(tb, cg):
            s = tb % 2
            pb = cg % 3
            for kc in range(8):
                sc.op("pe", lambda e, kc=kc: e.matmul(
                    psF[pb][:], lhsT=Wb[:, kc, cg * 128:(cg + 1) * 128], rhs=xT[s][:, kc, :],
                    start=(kc == 0), stop=(kc == 7)),
                    reads=[("xT", s, jj) for jj in range(4)] + [("Wb", kc)], writes=[("psF", pb)])
            sl = evac_box[0] % 8
            evac_box[0] += 1
            kind = (cg // 4) % 3
            if kind == 0:
                sc.op("dve", lambda e: e.tensor_scalar(out=stF[:, sl, :], in0=psF[pb][:], scalar1=0.125, scalar2=None, op0=ALU.mult),
                      reads=[("psF", pb)], writes=[("stF", sl)])
            elif kind == 1:
                sc.op("dve", lambda e: e.tensor_copy(out=stF[:, sl, :], in_=psF[pb][:]),
                      reads=[("psF", pb)], writes=[("stF", sl)])
            else:
                sc.op("act", lambda e: e.activation(out=stF[:, sl, :], in_=psF[pb][:], func=AF.Silu),
                      reads=[("psF", pb)], writes=[("stF", sl)])
            sc.dma("sp" if sl % 2 == 0 else "act", lambda e: e.dma_start(out=featT[cg * 128:(cg + 1) * 128, tb * 512:(tb + 1) * 512], in_=stF[:, sl, :]),
                   reads=[("stF", sl)])

        def v_group(tb, j, half):
            s = tb % 2
            pv = (j * 2 + half) % 2
            for kc in range(8):
                sc.op("pe", lambda e, kc=kc: e.matmul(
                    psV[pv][:], lhsT=xT[s][:, kc, j * 128:(j + 1) * 128],
                    rhs=Wb[:, kc, NFEAT + half * 512:NFEAT + (half + 1) * 512],
                    start=(kc == 0), stop=(kc == 7)),
                    reads=[("xT", s, j), ("Wb", kc)], writes=[("psV", pv)])
            sl = (j * 2 + half) % 4
            sc.op("act" if half == 1 else "dve", (lambda e: e.copy(out=stV[:, sl, :], in_=psV[pv][:])) if half == 1 else
                  (lambda e: e.tensor_copy(out=stV[:, sl, :], in_=psV[pv][:])),
                  reads=[("psV", pv)], writes=[("stV", sl)])
            r0 = tb * 512 + j * 128
            sc.dma("pool", lambda e: e.dma_start(out=vtok[r0:r0 + 128, half * 512:(half + 1) * 512], in_=stV[:, sl, :]),
                   reads=[("stV", sl)])

        load_x(0)
        if ntb > 1:
            load_x(1)
        for j in range(4):
            norm_tile(0, j)
            transp_tile(0, j)
        for tb in range(ntb):
            s = tb % 2
            nxt = tb + 1 < ntb
            if tb + 2 < ntb:
                load_x(tb + 2)
            groups = [(lambda cg=cg: feat_group(tb, cg)) for cg in range(NFEAT // 128)]
            groups += [(lambda j=j, half=half: v_group(tb, j, half)) for j in range(4) for half in range(2)]
            for gi, g in enumerate(groups):
                g()
                if nxt and gi % 8 == 1:
                    norm_tile(tb + 1, gi // 8)
                if nxt and gi % 8 == 5:
                    transp_tile(tb + 1, gi // 8)
            for kc in range(8):
                sc.op("pe", lambda e, kc=kc, s=s: e.matmul(
                    psf[:], lhsT=Wb[:, kc, NFEAT + NV:NFEAT + NV + 8], rhs=xT[s][:, kc, :],
                    start=(kc == 0), stop=(kc == 7)),
                    reads=[("xT", s, jj) for jj in range(4)] + [("Wb", kc)], writes=["psf"])
            fs = tb % 2
            fwv = fw[:, fs]
            csv = cs[:, fs]
            R_fw = ("fw", fs)
            sc.op("act", lambda e, fwv=fwv: e.activation(out=fwv[:, 0, :], in_=psf[:], func=AF.Exp, bias=nb_t[:, 0:1], scale=-1.0),
                  reads=["psf", "nb_t"], writes=[R_fw])
            sc.op("act", lambda e, fwv=fwv: e.activation(out=fwv[:, 1, :], in_=fwv[:, 0, :], func=AF.Ln, bias=one_t[0:8, 0:1], scale=1.0),
                  reads=[R_fw, "one_t"], writes=[R_fw])
            if tb == 0:
                init = 0.0
                rd = [R_fw, "ones8"]
            else:
                init = fw[:, 1 - fs, 2, 511:512]
                rd = [R_fw, "ones8", ("fw", 1 - fs)]
            sc.op("dve", lambda e, fwv=fwv, init=init: e.tensor_tensor_scan(
                out=fwv[:, 2, :], data0=ones8[:], data1=fwv[:, 1, :], initial=init, op0=ALU.mult, op1=ALU.subtract),
                reads=rd, writes=[R_fw])
            R_cs = ("cs", fs)
            sc.op("dve", lambda e, fwv=fwv, csv=csv: e.tensor_copy(out=csv[:, 0, :], in_=fwv[:, 2, :]), reads=[R_fw], writes=[R_cs])
            sc.op("dve", lambda e, fwv=fwv, csv=csv: e.tensor_tensor(out=fwv[:, 3, :], in0=fwv[:, 2, :], in1=csv[:, 0, :], op=ALU.subtract),
                  reads=[R_fw, R_cs], writes=[R_fw])
            sc.op("dve", lambda e, fwv=fwv, csv=csv: e.tensor_copy(out=csv[:, 1, :], in_=fwv[:, 3, :]), reads=[R_fw], writes=[R_cs])
            sc.op("dve", lambda e, fwv=fwv, csv=csv: e.tensor_tensor(out=fwv[:, 4, :], in0=fwv[:, 3, :], in1=csv[:, 1, :], op=ALU.subtract),
                  reads=[R_fw, R_cs], writes=[R_fw])
            sc.op("dve", lambda e, fwv=fwv, csv=csv: e.tensor_copy(out=csv[:, 2, :], in_=fwv[:, 4, :]), reads=[R_fw], writes=[R_cs])
            sc.op("dve", lambda e, csv=csv: e.tensor_scalar(out=csv[:, 3:6, :], in0=csv[:, 0:3, :], scalar1=-1.0, scalar2=None, op0=ALU.mult),
                  reads=[R_cs], writes=[R_cs])
            sc.dma("sp", lambda e, csv=csv, tb=tb: e.dma_start(out=caug[:, :, tb * 512:(tb + 1) * 512], in_=csv),
                   reads=[R_cs])
        sc.emit()


def dram_ap(ap, offset, dims):
    return bass.AP(ap.tensor, offset, [list(d) for d in dims])


def build_attn_phase(nc, sc, featT, vtok, caug, ident, maskD, yT, fox_heads=range(8), dil_heads=range(8), nqc=16, on_job_done=None):
    with ExitStack() as es:
        _UID[0] += 1
        _u = _UID[0]
        sb = lambda n, s, d: es.enter_context(nc.sbuf_tensor(f"{n}_u{_u}", s, d))
        ps = lambda n, s, d: es.enter_context(nc.psum_tensor(f"{n}_u{_u}", s, d))
        Qa = [sb(f"Qa{i}", [70, S], BF) for i in range(2)]
        Ka = [sb(f"Ka{i}", [70, S], BF) for i in range(2)]
        Vt = [sb(f"Vt{i}", [128, 3, 64, 65], BF) for i in range(2)]
        acc = sb("acc", [65, S], F32)
        pT2 = [sb(f"pT{i}", [128, 1024], BF) for i in range(2)]
        pT = [pT2[0][:, 0:512], pT2[0][:, 512:1024], pT2[1][:, 0:512], pT2[1][:, 512:1024]]
        Gc = [sb(f"Gc{i}", [64, 512], BF) for i in range(3)]
        rec = [sb(f"rec{i}", [65, 512], F32) for i in range(3)]
        bcs = [sb(f"bcs{i}", [64, 512], F32) for i in range(3)]
        obt = [sb(f"obt{i}", [64, 512], F32) for i in range(3)]
        ysb = [sb(f"ysb{i}", [64, 512], BF) for i in range(3)]
        ones_b = sb("ones_b", [65, 64], BF)
        rhi = [sb(f"rhi{i}", [65, 512], BF) for i in range(3)]
        rlo = [sb(f"rlo{i}", [65, 512], BF) for i in range(3)]
        id_t = sb("id_t2", [128, 128], BF)
        mk_t = sb("mk_t", [128, 256], BF)
        mk_m = sb("mk_m", [128, 256], BF)
        psS2 = [ps(f"psS{i}", [128, 1024], F32) for i in range(2)]
        psS = [psS2[0][:, 0:512], psS2[0][:, 512:1024], psS2[1][:, 0:512], psS2[1][:, 512:1024]]
        psO = [ps(f"psO{i}", [128, 512], F32) for i in range(2)]
        psB = ps("psB", [64, 512], F32)

        sc.dma("sp", lambda e: e.dma_start(out=id_t[:], in_=ident), writes=["id_t"])
        sc.dma("sp", lambda e: e.dma_start(out=mk_t[:], in_=maskD), writes=["mk_t"])
        sc.op("dve", lambda e: e.memset(ones_b[:], 1.0), writes=["ones_b"])
        sc.op("dve", lambda e: e.tensor_scalar(out=mk_m[:], in0=mk_t[:], scalar1=0.0, scalar2=None, op0=ALU.is_equal),
              reads=["mk_t"], writes=["mk_m"])
        for i in range(2):
            sc.op("dve", lambda e, i=i: e.memset(Qa[i][64:70, :], 1.0), writes=[("Qa", i)])
            sc.op("dve", lambda e, i=i: e.memset(Ka[i][64:70, :], 1.0), writes=[("Ka", i)])
            sc.op("pool", lambda e, i=i: e.memset(Vt[i][:, :, :, 64:65], 1.0), writes=[("Vt", i)])

        jobs = [("fox", h) for h in fox_heads] + [("dil", h) for h in dil_heads]

        def load(ji):
            kind, h = jobs[ji]
            s = ji % 2
            if kind == "fox":
                qrow, krow, vcol = h * 64, 512 + h * 64, h * 64
            else:
                qrow, krow, vcol = 1536 + h * 64, 2048 + h * 64, 512 + h * 64
            sc.dma("sp", lambda e: e.dma_start(out=Qa[s][0:64, :], in_=featT[qrow:qrow + 64, :]), writes=[("Qa", s)])
            sc.dma("sp", lambda e: e.dma_start(out=Ka[s][0:64, :], in_=featT[krow:krow + 64, :]), writes=[("Ka", s)])
            if kind == "fox":
                sc.dma("sp", lambda e: e.dma_start(out=Qa[s][64:67, :], in_=caug[h, 0:3, :]), writes=[("Qa", s)])
                sc.dma("sp", lambda e: e.dma_start(out=Ka[s][67:70, :], in_=caug[h, 3:6, :]), writes=[("Ka", s)])
                pats = [(0, 1)]
            else:
                pats = [(0, 1), (1, 4), (2, 16)]
            for pi, r in pats:
                nblk = S // r // 128
                for s_ in range(r):
                    step = 16 if nblk >= 16 else nblk
                    for j0 in range(0, nblk, step):
                        src = dram_ap(vtok, (s_ + r * 128 * j0) * 1024 + vcol,
                                      [[r * 1024, 128], [r * 128 * 1024, step], [1, 64]])
                        t0 = s_ * nblk + j0
                        sc.dma("sp", lambda e, src=src, pi=pi, t0=t0, step=step: e.dma_start(
                            out=Vt[s][:, pi, t0:t0 + step, 0:64], in_=src), writes=[("Vt", s)])

        NS = 3
        pending = []
        slot_ctr = [0]
        tick = [0]

        def normalize(kind, h, qc, src_ps, ob, job=0):
            pending.append({"state": 0, "kind": kind, "h": h, "qc": qc, "src_ps": src_ps, "ob": ob, "job": job})

        def _srcs(t):
            cols = slice(t["qc"] * 512, (t["qc"] + 1) * 512)
            if t["src_ps"]:
                return psO[t["ob"]][64:65, :], psO[t["ob"]][0:64, :], [("psO", t["ob"])], cols
            return acc[64:65, cols], acc[0:64, cols], [("acc", t["qc"])], cols

        def stage1(t):
            rs = slot_ctr[0] % NS
            slot_ctr[0] += 1
            t["rs"] = rs
            den, num, rd, cols = _srcs(t)
            grow = (1024 if t["kind"] == "fox" else 2560) + t["h"] * 64
            sc.dma("sp", lambda e: e.dma_start(out=Gc[rs][:], in_=featT[grow:grow + 64, cols]), writes=[("Gc", rs)])
            sc.op("dve", lambda e: e.reciprocal(out=rec[rs][64:65, :], in_=den), reads=rd, writes=[("rec", rs)])
            sc.op("dve", lambda e: e.tensor_copy(out=rhi[rs][64:65, :], in_=rec[rs][64:65, :]), reads=[("rec", rs)], writes=[("rhi", rs)])
            sc.op("dve", lambda e: e.tensor_tensor(out=rlo[rs][64:65, :], in0=rec[rs][64:65, :], in1=rhi[rs][64:65, :], op=ALU.subtract),
                  reads=[("rec", rs), ("rhi", rs)], writes=[("rlo", rs)])
            t["state"] = 1
            t["t_issue"] = tick[0]

        def stage2(t):
            rs = t["rs"]
            den, num, rd, cols = _srcs(t)
            yrow = (0 if t["kind"] == "fox" else 512) + t["h"] * 64
            job = t["job"]
            sc.op("pe", lambda e: e.matmul(psB[:], lhsT=ones_b[64:65, 0:64], rhs=rhi[rs][64:65, :], start=True, stop=False),
                  reads=[("rhi", rs), "ones_b"], writes=["psB"])
            sc.op("pe", lambda e: e.matmul(psB[:], lhsT=ones_b[64:65, 0:64], rhs=rlo[rs][64:65, :], start=False, stop=True),
                  reads=[("rlo", rs), "ones_b"], writes=["psB"])
            if t["src_ps"]:
                sc.op("dve", lambda e: e.tensor_copy(out=bcs[rs][:], in_=psB[:]), reads=["psB"], writes=[("bcs", rs)])
                sc.op("dve", lambda e: e.tensor_tensor(out=obt[rs][:], in0=num, in1=bcs[rs][:], op=ALU.mult),
                      reads=rd + [("bcs", rs)], writes=[("obt", rs)])
            else:
                sc.op("dve", lambda e: e.tensor_tensor(out=obt[rs][:], in0=num, in1=psB[:], op=ALU.mult),
                      reads=rd + ["psB"], writes=[("obt", rs)])
            sc.op("pool", lambda e: e.tensor_tensor(out=ysb[rs][:], in0=obt[rs][:], in1=Gc[rs][:], op=ALU.mult),
                  reads=[("obt", rs), ("Gc", rs)], writes=[("ysb", rs)])
            sc.dma("pool", lambda e: e.dma_start(out=yT[yrow:yrow + 64, cols], in_=ysb[rs][:]), reads=[("ysb", rs)], writes=[("yTd", job)])
            pending.remove(t)

        def pump():
            tick[0] += 1
            for t in list(pending):
                if t["state"] == 1:
                    if tick[0] - t["t_issue"] >= 6:
                        stage2(t)
                    break
            if sum(1 for t in pending if t["state"] == 1) < NS:
                for t in pending:
                    if t["state"] == 0:
                        stage1(t)
                        break

        def flush(pred=lambda t: True):
            for t in list(pending):
                if pred(t):
                    for u in list(pending):
                        if u is t:
                            break
                        if u["state"] == 1:
                            stage2(u)
                    if t["state"] == 0:
                        stage1(t)
                    stage2(t)

        tile_ctr = [0]

        def run_tiles(tiles):
            LA = 3
            base = tile_ctr[0]
            n = len(tiles)
            for i in range(min(LA, n)):
                tiles[i]["s_fn"]((base + i) % 4)
            for i in range(n):
                if i + LA < n:
                    tiles[i + LA]["s_fn"]((base + i + LA) % 4)
                b = (base + i) % 4
                c0, c1 = tiles[i]["cr"]
                sc.op("act", lambda e, b=b, c0=c0, c1=c1: e.activation(out=pT[b][:, c0:c1], in_=psS[b][:, c0:c1], func=AF.Exp),
                      reads=[("psS", b)], writes=[("pT", b)])
                if "mask" in tiles[i]:
                    m0 = tiles[i]["mask"]
                    sc.op("dve", lambda e, b=b, c0=c0, c1=c1, m0=m0: e.tensor_tensor(
                        out=pT[b][:, c0:c1], in0=pT[b][:, c0:c1], in1=mk_m[:, m0:m0 + (c1 - c0)], op=ALU.mult),
                        reads=[("pT", b), "mk_m"], writes=[("pT", b)])
                tiles[i]["pv_fn"](b)
                if tiles[i].get("post"):
                    tiles[i]["post"]()
                pump()
            tile_ctr[0] = base + n

        ob_box = [0]

        def do_job(ji, kind, h):
            s = ji % 2
            ob_ctr = ob_box[0]
            RQ, RK, RV = ("Qa", s), ("Ka", s), ("Vt", s)
            tiles = []
            if kind == "fox":
                for qc in range(nqc):
                    ob = ob_ctr % 2
                    ob_ctr += 1
                    nk = 4 * qc + 4
                    for kt in range(nk):
                        j = kt - 4 * qc
                        c0 = 128 * max(j, 0)

                        def s_fn(b, kt=kt, qc=qc, j=j, c0=c0):
                            kap = Ka[s][0:70, kt * 128:(kt + 1) * 128]
                            if j < 0:
                                sc.op("pe", lambda e: e.matmul(psS[b][:, 0:512], lhsT=kap, rhs=Qa[s][0:70, qc * 512:(qc + 1) * 512],
                                                               start=True, stop=True),
                                      reads=[RQ, RK], writes=[("psS", b)])
                            else:
                                q0 = qc * 512 + c0
                                sc.op("pe", lambda e: e.matmul(psS[b][:, c0:c0 + 128], lhsT=id_t[:], rhs=mk_t[:, 0:128],
                                                               start=True, stop=False),
                                      reads=["id_t", "mk_t"], writes=[("psS", b)])
                                sc.op("pe", lambda e: e.matmul(psS[b][:, c0:c0 + 128], lhsT=kap, rhs=Qa[s][0:70, q0:q0 + 128],
                                                               start=False, stop=True),
                                      reads=[RQ, RK], writes=[("psS", b)])
                                if c0 + 128 < 512:
                                    sc.op("pe", lambda e: e.matmul(psS[b][:, c0 + 128:512], lhsT=kap,
                                                                   rhs=Qa[s][0:70, q0 + 128:(qc + 1) * 512], start=True, stop=True),
                                          reads=[RQ, RK], writes=[("psS", b)])

                        def pv_fn(b, kt=kt, c0=c0, ob=ob, nk=nk):
                            if kt == 0:
                                flush(lambda t: t["src_ps"] and t["ob"] == ob)
                            sc.op("pe", lambda e: e.matmul(psO[ob][0:65, c0:512], lhsT=Vt[s][:, 0, kt, 0:65], rhs=pT[b][:, c0:512],
                                                           start=(kt == 0), stop=(kt == nk - 1), skip_group_check=True),
                                  reads=[("pT", b), RV], writes=[("psO", ob)])

                        t = {"s_fn": s_fn, "cr": (c0, 512), "pv_fn": pv_fn}
                        if kt == nk - 1:
                            t["post"] = (lambda qc=qc, ob=ob: normalize("fox", h, qc, True, ob, ji))
                        tiles.append(t)
                run_tiles_pairs(tiles)
            else:
                for pi, r in [(0, 1), (1, 4), (2, 16)]:
                    nblk = S // r // 128
                    for s_ in range(r):
                        for c in range(nblk // 4):
                            n0 = 4 * c
                            ob = ob_ctr % 2
                            ob_ctr += 1
                            qbase = s_ + r * 128 * n0
                            js = ([n0 - 1] if n0 > 0 else []) + list(range(n0, n0 + 4))
                            for j in js:
                                b_lo, b_hi = max(j, n0), min(j + 1, n0 + 3)
                                c0, c1 = (b_lo - n0) * 128, (b_hi - n0 + 1) * 128
                                m0 = 0 if b_lo == j else 128

                                def s_fn(b, j=j, c0=c0, c1=c1, m0=m0, r=r, s_=s_, qbase=qbase):
                                    kb = s_ + r * 128 * j
                                    kap = Ka[s][0:64, kb:kb + r * 127 + 1:r]
                                    qap = Qa[s][0:64, qbase + r * c0:qbase + r * (c1 - 1) + 1:r]
                                    sc.op("pe", lambda e: e.matmul(psS[b][:, c0:c1], lhsT=id_t[:], rhs=mk_t[:, m0:m0 + (c1 - c0)],
                                                                   start=True, stop=False),
                                          reads=["id_t", "mk_t"], writes=[("psS", b)])
                                    sc.op("pe", lambda e: e.matmul(psS[b][:, c0:c1], lhsT=kap, rhs=qap, start=False, stop=True),
                                          reads=[RQ, RK], writes=[("psS", b)])

                                def pv_fn(b, j=j, b_lo=b_lo, b_hi=b_hi, n0=n0, ob=ob, pi=pi, s_=s_, nblk=nblk):
                                    flush(lambda t: t["src_ps"] and t["ob"] == ob)
                                    for bb in range(b_lo, b_hi + 1):
                                        cb = (bb - n0) * 128
                                        st = (j == bb - 1) or (bb == 0 and j == 0)
                                        sc.op("pe", lambda e, cb=cb, st=st, bb=bb: e.matmul(
                                            psO[ob][0:65, cb:cb + 128], lhsT=Vt[s][:, pi, s_ * nblk + j, 0:65], rhs=pT[b][:, cb:cb + 128],
                                            start=st, stop=(j == bb), skip_group_check=True),
                                            reads=[("pT", b), RV], writes=[("psO", ob)])

                                t = {"s_fn": s_fn, "cr": (c0, c1), "pv_fn": pv_fn}
                                if j == js[-1]:
                                    def post(ob=ob, pi=pi, r=r, qbase=qbase, c=c):
                                        av = acc[0:65, qbase:qbase + r * 511 + 1:r]
                                        ares = [("acc", k) for k in range(r * c, r * c + r)]
                                        flush(lambda t: (not t["src_ps"]) and t["qc"] in range(r * c, r * c + r))
                                        if pi == 0:
                                            sc.op("dve", lambda e: e.tensor_copy(out=av, in_=psO[ob][0:65, :]),
                                                  reads=[("psO", ob)], writes=ares)
                                        else:
                                            sc.op("dve", lambda e: e.tensor_tensor(out=av, in0=av, in1=psO[ob][0:65, :], op=ALU.add),
                                                  reads=[("psO", ob)] + ares, writes=ares)
                                    t["post"] = post
                                tiles.append(t)
                run_tiles(tiles)
                for qc in range(nqc):
                    normalize("dil", h, qc, False, 0, ji)
            ob_box[0] = ob_ctr

        def run_tiles_pairs(tiles):
            if tile_ctr[0] % 2:
                tile_ctr[0] += 1
            base = tile_ctr[0]
            n = len(tiles)
            pairs = [list(range(i, min(i + 2, n))) for i in range(0, n, 2)]

            def issue_s(p):
                for i in pairs[p]:
                    tiles[i]["s_fn"]((base + i) % 4)
            issue_s(0)
            for p in range(len(pairs)):
                if p + 1 < len(pairs):
                    issue_s(p + 1)
                idx = pairs[p]
                b2 = ((base + idx[0]) % 4) // 2
                lo = tiles[idx[0]]["cr"][0]
                hi = 512 * (len(idx) - 1) + tiles[idx[-1]]["cr"][1]
                bs = [(base + i) % 4 for i in idx]
                if len(idx) == 2 and tiles[idx[1]]["cr"][0] > 0:
                    for i, b in zip(idx, bs):
                        c0, c1 = tiles[i]["cr"]
                        sc.op("act", lambda e, b=b, c0=c0, c1=c1: e.activation(out=pT[b][:, c0:c1], in_=psS[b][:, c0:c1], func=AF.Exp),
                              reads=[("psS", b)], writes=[("pT", b)])
                else:
                    sc.op("act", lambda e, b2=b2, lo=lo, hi=hi: e.activation(out=pT2[b2][:, lo:hi], in_=psS2[b2][:, lo:hi], func=AF.Exp),
                          reads=[("psS", b) for b in bs], writes=[("pT", b) for b in bs])
                for i in idx:
                    tiles[i]["pv_fn"]((base + i) % 4)
                    if tiles[i].get("post"):
                        tiles[i]["post"]()
                    pump()
            tile_ctr[0] = base + n

        done_box = [0]

        def notify_done():
            while done_box[0] < len(jobs) and done_box[0] < cur_job[0] + 0 and not any(t["job"] == done_box[0] for t in pending):
                if on_job_done is not None:
                    on_job_done(done_box[0])
                done_box[0] += 1

        cur_job = [0]
        _pump0 = pump

        def pump():
            _pump0()
            notify_done()

        load(0)
        for ji, (kind, h) in enumerate(jobs):
            cur_job[0] = ji
            if ji + 1 < len(jobs):
                load(ji + 1)
            do_job(ji, kind, h)
        cur_job[0] = len(jobs)
        flush()
        notify_done()
        sc.emit()
import numpy as np
import concourse.bass as bass
import concourse.mybir as mybir
from contextlib import ExitStack

F32 = mybir.dt.float32
BF = mybir.dt.bfloat16
AF = mybir.ActivationFunctionType
ALU = mybir.AluOpType
AX = mybir.AxisListType
S = 8192
_UID = [0]


def build_outproj(nc, sc, yT, wout, xres, ident, gvec, out_main, out_xT, final, ntok=4096):
    with ExitStack() as es:
        _UID[0] += 1
        _u = _UID[0]
        sb = lambda n, s, d: es.enter_context(nc.sbuf_tensor(f"{n}_u{_u}", s, d))
        ps = lambda n, s, d: es.enter_context(nc.psum_tensor(f"{n}_u{_u}", s, d))
        Wo = sb("Wo", [128, 16, 1024], BF)
        wst = [sb(f"wst{i}", [128, 1024], F32) for i in range(2)]
        yt = [sb(f"yt{i}", [128, 16, 512], BF) for i in range(2)]
        xt = [sb(f"xt{i}", [128, 1024], F32) for i in range(2)]
        x1 = [sb(f"x1{i}", [128, 1024], F32) for i in range(2)]
        junk = sb("junkb", [128, 1024], F32)
        ss = sb("ssb", [128, 8], F32)
        eps_t = sb("epsb", [128, 1], F32)
        id_t = sb("idb", [128, 128], BF)
        xn = [sb(f"xnb{i}", [128, 1024], BF) for i in range(2)]
        xTs = [sb(f"xTs{i}", [128, 8, 512], BF) for i in range(2)]
        gft = sb("gft", [128, 1024], F32)
        psA = [ps(f"psA{i}", [128, 512], F32) for i in range(4)]
        psT = [ps(f"psTb{i}", [128, 8, 128], BF) for i in range(2)]
        sc.dma("sp", lambda e: e.dma_start(out=id_t[:], in_=ident), writes=["id_t"])
        sc.op("dve", lambda e: e.memset(eps_t[:], 1e-6), writes=["eps_t"])
        if final:
            gsrc = bass.AP(gvec.tensor, 0, [[0, 128], [1, 1024]])
            sc.dma("sp", lambda e: e.dma_start(out=gft[:], in_=gsrc), writes=["gft"])
        for kc in range(16):
            s = kc % 2
            sc.dma("sp" if s == 0 else "pool", lambda e, kc=kc, s=s: e.dma_start(out=wst[s][:], in_=wout[kc * 128:(kc + 1) * 128, :]),
                   writes=[("wst", s)])
            sc.op("dve" if s == 0 else "pool", lambda e, kc=kc, s=s: e.tensor_copy(out=Wo[:, kc, :], in_=wst[s][:]),
                  reads=[("wst", s)], writes=[("Wo", kc)])
        ti = 0
        for tb in range(ntok // 512):
            s = tb % 2
            ysrc = yT[:, tb * 512:(tb + 1) * 512].rearrange("(kc p) t -> p kc t", p=128)
            sc.dma("sp", lambda e, s=s, ysrc=ysrc: e.dma_start(out=yt[s][:], in_=ysrc), writes=[("yt", s)])
            for j in range(4):
                xs = ti % 2
                col = ti % 8
                ti += 1
                r0 = tb * 512 + j * 128
                sc.dma("pool", lambda e, xs=xs, r0=r0: e.dma_start(out=xt[xs][:], in_=xres[r0:r0 + 128, :]), writes=[("xt", xs)])
                for half in range(2):
                    pb = (xs * 2 + half)
                    for kc in range(16):
                        sc.op("pe", lambda e, pb=pb, kc=kc, s=s, j=j, half=half: e.matmul(
                            psA[pb][:], lhsT=yt[s][:, kc, j * 128:(j + 1) * 128], rhs=Wo[:, kc, half * 512:(half + 1) * 512],
                            start=(kc == 0), stop=(kc == 15)),
                            reads=[("yt", s), ("Wo", kc)], writes=[("psA", pb)])
                    sc.op("dve", lambda e, pb=pb, xs=xs, half=half: e.tensor_tensor(
                        out=x1[xs][:, half * 512:(half + 1) * 512], in0=xt[xs][:, half * 512:(half + 1) * 512], in1=psA[pb][:], op=ALU.add),
                        reads=[("psA", pb), ("xt", xs)], writes=[("x1", xs)])
                if not final:
                    sc.dma("sp", lambda e, xs=xs, r0=r0: e.dma_start(out=out_main[r0:r0 + 128, :], in_=x1[xs][:]), reads=[("x1", xs)])
                sc.op("act", lambda e, xs=xs: e.activation(out=junk[:], in_=x1[xs][:], func=AF.Square), reads=[("x1", xs)], writes=["junk"])
                sc.op("dve", lambda e, col=col: e.reduce_sum(out=ss[:, col:col + 1], in_=junk[:], axis=AX.X), reads=["junk"], writes=[("ss", col)])
                sc.op("act", lambda e, col=col: e.activation(out=ss[:, col:col + 1], in_=ss[:, col:col + 1], func=AF.Sqrt,
                                                             bias=eps_t[:, 0:1], scale=1.0 / 1024.0),
                      reads=[("ss", col), "eps_t"], writes=[("ss", col)])
                sc.op("dve", lambda e, col=col: e.reciprocal(out=ss[:, col:col + 1], in_=ss[:, col:col + 1]), reads=[("ss", col)], writes=[("ss", col)])
                if final:
                    sc.op("dve", lambda e, xs=xs, col=col: e.scalar_tensor_tensor(
                        out=xt[xs][:], in0=x1[xs][:], scalar=ss[:, col:col + 1], in1=gft[:], op0=ALU.mult, op1=ALU.mult),
                        reads=[("x1", xs), ("ss", col), "gft"], writes=[("xt", xs)])
                    sc.dma("sp", lambda e, xs=xs, r0=r0: e.dma_start(out=out_main[r0:r0 + 128, :], in_=xt[xs][:]), reads=[("xt", xs)])
                else:
                    sc.op("dve", lambda e, xs=xs, col=col: e.tensor_scalar(out=xn[xs][:], in0=x1[xs][:], scalar1=ss[:, col:col + 1],
                                                                         scalar2=None, op0=ALU.mult),
                          reads=[("x1", xs), ("ss", col)], writes=[("xn", xs)])
                    for kc in range(8):
                        sc.op("pe", lambda e, xs=xs, kc=kc: e.transpose(out=psT[xs][:, kc, :], in_=xn[xs][:, kc * 128:(kc + 1) * 128], identity=id_t[:]),
                              reads=[("xn", xs), "id_t"], writes=[("psT", xs)])
                    sc.op("act", lambda e, xs=xs, s=s, j=j: e.copy(out=xTs[s][:, :, j * 128:(j + 1) * 128], in_=psT[xs][:]),
                          reads=[("psT", xs)], writes=[("xTs", s)])
            if not final:
                dst = out_xT[:, tb * 512:(tb + 1) * 512].rearrange("(kc p) t -> p kc t", p=128)
                sc.dma("sp", lambda e, s=s, dst=dst: e.dma_start(out=dst, in_=xTs[s][:]), reads=[("xTs", s)])
        sc.emit()


NW1 = 3072


def build_retention(nc, sc, xsrc_fn, w1, gn, ident, cosT, sinT, maskR, qdT, kdec, cd, ydst_fn, ntb=16, on_block_done=None):
    with ExitStack() as es:
        _UID[0] += 1
        _u = _UID[0]
        sb = lambda n, s, d: es.enter_context(nc.sbuf_tensor(f"{n}_u{_u}", s, d))
        ps = lambda n, s, d: es.enter_context(nc.psum_tensor(f"{n}_u{_u}", s, d))
        Wb = sb("W1b", [128, 8, NW1], BF)
        wst = [sb(f"w1st{i}", [128, NW1], F32) for i in range(2)]
        gn_t = sb("gn1", [128, 8], F32)
        gn16 = sb("gn16", [128, 8], F32)
        id_t = sb("idc", [128, 128], BF)
        eps_t = sb("epsc", [128, 1], F32)
        mk = sb("mkR", [128, 2, 128], F32)
        qd = sb("qd_sb", [128, 2, 128], F32)
        kd = sb("kd_sb", [128, 2], F32)
        xT = [sb(f"xTc{i}", [128, 8, 512], BF) for i in range(2)]
        cs_t = [sb(f"cos{i}", [128, 512], F32) for i in range(2)]
        sn_t = [sb(f"sin{i}", [128, 512], F32) for i in range(2)]
        tm = [sb(f"tm{i}", [128, 4, 512], F32) for i in range(2)]
        QT = [sb(f"QT{i}", [128, 2, 2, 512], BF) for i in range(2)]
        QdT = [sb(f"QdT{i}", [128, 2, 2, 512], BF) for i in range(2)]
        KT = [sb(f"KT{i}", [128, 2, 2, 512], BF) for i in range(2)]
        Ktok = [sb(f"Ktok{i}", [128, 4, 2, 256], BF) for i in range(2)]
        Vt = [sb(f"Vc{i}", [128, 4, 2, 512], BF) for i in range(2)]
        Gt = [sb(f"Gc{i}", [128, 4, 2, 512], BF) for i in range(2)]
        St = sb("St", [128, 2, 2, 512], F32)
        Stb = [sb(f"Stb{i}", [128, 2, 2, 512], BF) for i in range(2)]
        Sm = [sb(f"Sm{i}", [128, 128], BF) for i in range(2)]
        stats = sb("stats", [128, 2, 2, 6], F32)
        mv = sb("mv", [128, 2, 2, 2], F32)
        yn = [sb(f"yn{i}", [128, 512], F32) for i in range(2)]
        y2 = [sb(f"y2{i}", [128, 1024], BF) for i in range(2)]
        y2s = [sb(f"y2s{i}", [128, 8, 128], BF) for i in range(2)]
        psF = [ps(f"pcF{i}", [128, 512], F32) for i in range(2)]
        psS = ps("pcS", [128, 2, 128], F32)
        psO = [ps(f"pcO{i}", [128, 512], F32) for i in range(2)]
        psU = [ps(f"pcU{i}", [128, 512], F32) for i in range(2)]
        psT = ps("pcT", [128, 8, 128], BF)

        for (t, src, nm) in ((gn_t, gn, "gn_t"), (id_t, ident, "id_t"), (mk, maskR, "mk"), (qd, qdT, "qd"), (kd, kdec, "kd")):
            sc.dma("sp", lambda e, t=t, src=src: e.dma_start(out=t[:], in_=src), writes=[nm])
        cdt = sb("cdt", [128, 2], F32)
        cdsrc = bass.AP(cd.tensor, 0, [[0, 128], [1, 2]])
        sc.dma("sp", lambda e: e.dma_start(out=cdt[:], in_=cdsrc), writes=["cdt"])
        sc.op("dve", lambda e: e.memset(eps_t[:], 1e-6), writes=["eps_t"])
        sc.op("dve", lambda e: e.memset(St[:], 0.0), writes=[("St", a, b) for a in range(2) for b in range(2)])
        sc.op("dve", lambda e: e.memset(Stb[0][:], 0.0), writes=[("Stb", 0, a, b) for a in range(2) for b in range(2)])
        sc.op("dve", lambda e: e.tensor_scalar(out=gn16[:], in0=gn_t[:], scalar1=1.0 / 16.0, scalar2=None, op0=ALU.mult),
              reads=["gn_t"], writes=["gn16"])
        for kc in range(8):
            s = kc % 2
            sc.dma("sp" if s == 0 else "pool", lambda e, kc=kc, s=s: e.dma_start(out=wst[s][:], in_=w1[kc * 128:(kc + 1) * 128, :]),
                   writes=[("wst", s)])
            eng = "dve" if s == 0 else "pool"
            for (c0, c1, gt, gname) in ((0, 512, gn_t, "gn_t"), (512, 1024, gn16, "gn16"), (1024, NW1, gn_t, "gn_t")):
                sc.op(eng, lambda e, kc=kc, s=s, c0=c0, c1=c1, gt=gt: e.tensor_scalar(
                    out=Wb[:, kc, c0:c1], in0=wst[s][:, c0:c1], scalar1=gt[:, kc:kc + 1], scalar2=None, op0=ALU.mult),
                    reads=[("wst", s), gname], writes=[("Wb", kc, c0)])
        WR = lambda kc: [("Wb", kc, 0), ("Wb", kc, 512), ("Wb", kc, 1024)]
        chunk_i = 0
        deferred = [None]
        kdefer = []
        def load_blk(tb):
            s = tb % 2
            t0 = tb * 512
            src = xsrc_fn(tb).rearrange("(kc p) t -> p kc t", p=128)
            sc.dma("sp", lambda e: e.dma_start(out=xT[s][:, 0:4, :], in_=src[:, 0:4, :]), writes=[("xT", s)])
            sc.dma("act", lambda e: e.dma_start(out=xT[s][:, 4:8, :], in_=src[:, 4:8, :]), writes=[("xT", s)])
            sc.dma("pool", lambda e: e.dma_start(out=cs_t[s][:], in_=cosT[:, t0:t0 + 512]), writes=[("cos", s)])
            sc.dma("pool", lambda e: e.dma_start(out=sn_t[s][:], in_=sinT[:, t0:t0 + 512]), writes=[("sin", s)])

        load_blk(0)
        for tb in range(ntb):
            s = tb % 2
            t0 = tb * 512
            if tb + 1 < ntb:
                load_blk(tb + 1)
            for gi in range(4):
                isk, h = gi // 2, gi % 2
                fb = [psF[0], psF[1]] if gi % 2 == 0 else [psO[0], psO[1]]
                fr = [("psF", 0), ("psF", 1)] if gi % 2 == 0 else [("psO", 0), ("psO", 1)]
                for eo in range(2):
                    cg = gi * 2 + eo
                    for kc in range(8):
                        sc.op("pe", lambda e, eo=eo, kc=kc, cg=cg, s=s, fb=fb: e.matmul(
                            fb[eo][:], lhsT=Wb[:, kc, cg * 128:(cg + 1) * 128], rhs=xT[s][:, kc, :], start=(kc == 0), stop=(kc == 7)),
                            reads=[("xT", s)] + WR(kc), writes=[fr[eo]])
                ts = gi % 2
                RT = ("tm", ts)
                sc.op("dve", lambda e, ts=ts, s=s, fb=fb: e.tensor_tensor(out=tm[ts][:, 0, :], in0=fb[0][:], in1=cs_t[s][:], op=ALU.mult),
                      reads=[fr[0], ("cos", s)], writes=[RT])
                sc.op("dve", lambda e, ts=ts, s=s, fb=fb: e.tensor_tensor(out=tm[ts][:, 1, :], in0=fb[1][:], in1=sn_t[s][:], op=ALU.mult),
                      reads=[fr[1], ("sin", s)], writes=[RT])
                sc.op("dve", lambda e, ts=ts, s=s, fb=fb: e.tensor_tensor(out=tm[ts][:, 2, :], in0=fb[0][:], in1=sn_t[s][:], op=ALU.mult),
                      reads=[fr[0], ("sin", s)], writes=[RT])
                sc.op("dve", lambda e, ts=ts, s=s, fb=fb: e.tensor_tensor(out=tm[ts][:, 3, :], in0=fb[1][:], in1=cs_t[s][:], op=ALU.mult),
                      reads=[fr[1], ("cos", s)], writes=[RT])
                dst = (KT if isk else QT)[s]
                RD = ("KT" if isk else "QT", s)
                sc.op("pool", lambda e, ts=ts, dst=dst, h=h: e.tensor_tensor(out=dst[:, h, 0, :], in0=tm[ts][:, 0, :], in1=tm[ts][:, 1, :], op=ALU.subtract),
                      reads=[RT], writes=[RD])
                sc.op("pool", lambda e, ts=ts, dst=dst, h=h: e.tensor_tensor(out=dst[:, h, 1, :], in0=tm[ts][:, 2, :], in1=tm[ts][:, 3, :], op=ALU.add),
                      reads=[RT], writes=[RD])
                if not isk:
                    for eo in range(2):
                        qv = QT[s][:, h, eo, :].rearrange("p (c t) -> p c t", c=4)
                        ov = QdT[s][:, h, eo, :].rearrange("p (c t) -> p c t", c=4)
                        base = qd[:, h, :]
                        dv_ = bass.AP(base.tensor, base.offset, [list(base.ap[0]), [0, 4], [1, 128]])
                        sc.op("dve", lambda e, qv=qv, ov=ov, dv_=dv_: e.tensor_tensor(out=ov, in0=qv, in1=dv_, op=ALU.mult),
                              reads=[RD, "qd"], writes=[("QdT", s)])
                else:
                    def ktrans(s=s, h=h, RD=RD):
                        for c in range(4):
                            for eo in range(2):
                                sc.op("pe", lambda e, c=c, eo=eo: e.transpose(
                                    out=psT[:, c * 2 + eo, :], in_=KT[s][:, h, eo, c * 128:(c + 1) * 128], identity=id_t[:]),
                                    reads=[RD, "id_t"], writes=["psT"])
                        kv = Ktok[s][:, :, h, :].rearrange("p c (eo i) -> p c eo i", eo=2)
                        pv = psT[:].rearrange("p (c eo) i -> p c eo i", eo=2)
                        sc.op("act", lambda e: e.activation(out=kv, in_=pv, func=AF.Copy, scale=kd[:, h:h + 1]),
                              reads=["psT", "kd"], writes=[("Ktok", s)])
                    kdefer.append(ktrans)
            for j in range(4):
                for grp in range(4):
                    pb = grp % 2
                    for kc in range(8):
                        sc.op("pe", lambda e, pb=pb, kc=kc, j=j, grp=grp, s=s: e.matmul(
                            psU[pb][:], lhsT=xT[s][:, kc, j * 128:(j + 1) * 128], rhs=Wb[:, kc, 1024 + grp * 512:1024 + (grp + 1) * 512],
                            start=(kc == 0), stop=(kc == 7)),
                            reads=[("xT", s)] + WR(kc), writes=[("psU", pb)])
                    if grp < 2:
                        sc.op("act", lambda e, pb=pb, j=j, grp=grp, s=s: e.copy(out=Vt[s][:, j, grp, :], in_=psU[pb][:]),
                              reads=[("psU", pb)], writes=[("Vt", s)])
                    else:
                        sc.op("act", lambda e, pb=pb, j=j, grp=grp, s=s: e.activation(out=Gt[s][:, j, grp - 2, :], in_=psU[pb][:], func=AF.Silu),
                              reads=[("psU", pb)], writes=[("Gt", s)])
            for f in kdefer:
                f()
            kdefer.clear()
            for c in range(4):
                cur = chunk_i % 2
                nxt = 1 - cur
                ys = chunk_i % 2
                cc = slice(c * 128, (c + 1) * 128)
                oset = chunk_i % 2
                ob = [psO[0], psO[1]] if oset == 0 else [psF[0], psF[1]]
                orr = [("psO", 0), ("psO", 1)] if oset == 0 else [("psF", 0), ("psF", 1)]
                for h in range(2):
                    for eo in range(2):
                        sc.op("pe", lambda e, h=h, eo=eo, s=s, cc=cc: e.matmul(
                            psS[:, h, :], lhsT=KT[s][:, h, eo, cc], rhs=QT[s][:, h, eo, cc], start=(eo == 0), stop=(eo == 1)),
                            reads=[("KT", s), ("QT", s)], writes=[("psS", h)])
                    sc.op("dve", lambda e, h=h: e.tensor_tensor(out=Sm[h][:], in0=psS[:, h, :], in1=mk[:, h, :], op=ALU.mult),
                          reads=[("psS", h), "mk"], writes=[("Sm", h)])
                    sc.op("pe", lambda e, h=h, s=s, c=c, ob=ob: e.matmul(ob[h][:], lhsT=Sm[h][:], rhs=Vt[s][:, c, h, :], start=True, stop=False),
                          reads=[("Sm", h), ("Vt", s)], writes=[orr[h]])
                    for eo in range(2):
                        sc.op("pe", lambda e, h=h, eo=eo, s=s, cc=cc, cur=cur, ob=ob: e.matmul(
                            ob[h][:], lhsT=QdT[s][:, h, eo, cc], rhs=Stb[cur][:, eo, h, :], start=False, stop=(eo == 1)),
                            reads=[("QdT", s), ("Stb", cur, eo, h)], writes=[orr[h]])
                    for half in range(2):
                        sc.op("pe", lambda e, h=h, half=half, s=s, c=c: e.matmul(
                            psU[half][:], lhsT=Ktok[s][:, c, h, half * 128:(half + 1) * 128], rhs=Vt[s][:, c, h, :], start=True, stop=True),
                            reads=[("Ktok", s), ("Vt", s)], writes=[("psU", half)])
                        sc.op("dve", lambda e, h=h, half=half: e.scalar_tensor_tensor(
                            out=St[:, half, h, :], in0=St[:, half, h, :], scalar=cdt[:, h:h + 1], in1=psU[half][:], op0=ALU.mult, op1=ALU.add),
                            reads=[("St", half, h), ("psU", half), "cdt"], writes=[("St", half, h)])
                        sc.op("act", lambda e, h=h, half=half, nxt=nxt: e.copy(out=Stb[nxt][:, half, h, :], in_=St[:, half, h, :]),
                              reads=[("St", half, h)], writes=[("Stb", nxt, half, h)])
                    sr = ("stats", ys, h)
                    sc.op("dve", lambda e, h=h, ob=ob, ys=ys: e.bn_stats(out=stats[:, ys, h, :], in_=ob[h][:]), reads=[orr[h]], writes=[sr])
                    sc.op("dve", lambda e, h=h, ys=ys: e.bn_aggr(out=mv[:, ys, h, :], in_=stats[:, ys, h, :]), reads=[sr], writes=[("mv", ys, h)])
                    sc.op("act", lambda e, h=h, ys=ys: e.activation(out=mv[:, ys, h, 1:2], in_=mv[:, ys, h, 1:2], func=AF.Sqrt, bias=eps_t[:, 0:1], scale=1.0),
                          reads=[("mv", ys, h), "eps_t"], writes=[("mv", ys, h)])
                    sc.op("dve", lambda e, h=h, ys=ys: e.reciprocal(out=mv[:, ys, h, 1:2], in_=mv[:, ys, h, 1:2]), reads=[("mv", ys, h)], writes=[("mv", ys, h)])
                    sc.op("dve", lambda e, h=h, ob=ob, ys=ys: e.tensor_scalar(out=yn[h][:], in0=ob[h][:], scalar1=mv[:, ys, h, 0:1], scalar2=mv[:, ys, h, 1:2],
                                                                        op0=ALU.subtract, op1=ALU.mult),
                          reads=[orr[h], ("mv", ys, h)], writes=[("yn", h)])
                    sc.op("pool", lambda e, h=h, ys=ys, s=s, c=c: e.tensor_tensor(out=y2[ys][:, h * 512:(h + 1) * 512], in0=yn[h][:], in1=Gt[s][:, c, h, :], op=ALU.mult),
                          reads=[("yn", h), ("Gt", s)], writes=[("y2", ys, h)])

                def epilogue(ys=ys, tb=tb, c=c):
                    for kc in range(8):
                        sc.op("pe", lambda e, kc=kc: e.transpose(out=psT[:, kc, :], in_=y2[ys][:, kc * 128:(kc + 1) * 128], identity=id_t[:]),
                              reads=[("y2", ys, kc // 4), "id_t"], writes=["psT"])
                    sc.op("act", lambda e: e.copy(out=y2s[ys][:], in_=psT[:]), reads=["psT"], writes=[("y2s", ys)])
                    dst = ydst_fn(tb, c).rearrange("(kc p) t -> p kc t", p=128)
                    sc.dma("sp", lambda e: e.dma_start(out=dst, in_=y2s[ys][:]), reads=[("y2s", ys)], writes=[("y2d", tb)])
                    if c == 3 and on_block_done is not None:
                        on_block_done(tb)

                if deferred[0] is not None:
                    deferred[0]()
                deferred[0] = epilogue
                chunk_i += 1

        if deferred[0] is not None:
            deferred[0]()
        sc.emit()
import numpy as np
import concourse.bass as bass
import concourse.mybir as mybir
from contextlib import ExitStack

F32 = mybir.dt.float32
BF = mybir.dt.bfloat16
AF = mybir.ActivationFunctionType
ALU = mybir.AluOpType
AX = mybir.AxisListType
S = 8192
_UID = [0]


def build_outproj_p1(nc, sc, ysrc_fn, wout, xres, xout, ssloc):
    with ExitStack() as es:
        _UID[0] += 1
        _u = _UID[0]
        sb = lambda n, s, d: es.enter_context(nc.sbuf_tensor(f"{n}_u{_u}", s, d))
        ps = lambda n, s, d: es.enter_context(nc.psum_tensor(f"{n}_u{_u}", s, d))
        Wo = sb("Wo", [128, 16, 512], BF)
        wst = [sb(f"wst{i}", [128, 512], F32) for i in range(2)]
        yt = [sb(f"yt{i}", [128, 16, 512], BF) for i in range(2)]
        xt = [sb(f"xt{i}", [128, 512], F32) for i in range(2)]
        x1 = [sb(f"x1{i}", [128, 512], F32) for i in range(2)]
        junk = sb("junkb", [128, 512], F32)
        ssp = sb("ssp", [128, 64], F32)
        psA = [ps(f"psA{i}", [128, 512], F32) for i in range(4)]
        for kc in range(16):
            s = kc % 2
            sc.dma("sp" if s == 0 else "pool", lambda e, kc=kc, s=s: e.dma_start(out=wst[s][:], in_=wout[kc * 128:(kc + 1) * 128, :]),
                   writes=[("wst", s)])
            sc.op("dve" if s == 0 else "pool", lambda e, kc=kc, s=s: e.tensor_copy(out=Wo[:, kc, :], in_=wst[s][:]),
                  reads=[("wst", s)], writes=[("Wo", kc)])
        ti = 0

        def load_y(tb):
            s = tb % 2
            ysrc = ysrc_fn(tb).rearrange("(kc p) t -> p kc t", p=128)
            for q4, qn in enumerate(("sp", "act", "sp", "act")):
                sc.dma(qn, lambda e, q4=q4: e.dma_start(out=yt[s][:, q4 * 4:(q4 + 1) * 4, :], in_=ysrc[:, q4 * 4:(q4 + 1) * 4, :]),
                       writes=[("yt", s, q4)])

        load_y(0)
        for tb in range(S // 512):
            s = tb % 2
            if tb + 1 < S // 512:
                load_y(tb + 1)
            for j in range(4):
                xs = ti % 2
                pb = ti % 4
                tile = ti
                ti += 1
                r0 = tb * 512 + j * 128
                sc.dma("pool", lambda e, xs=xs, r0=r0: e.dma_start(out=xt[xs][:], in_=xres[r0:r0 + 128, :]), writes=[("xt", xs)])
                for kc in range(16):
                    sc.op("pe", lambda e, pb=pb, kc=kc, s=s, j=j: e.matmul(
                        psA[pb][:], lhsT=yt[s][:, kc, j * 128:(j + 1) * 128], rhs=Wo[:, kc, :], start=(kc == 0), stop=(kc == 15)),
                        reads=[("yt", s, kc // 4), ("Wo", kc)], writes=[("psA", pb)])
                sc.op("dve", lambda e, pb=pb, xs=xs: e.tensor_tensor(out=x1[xs][:], in0=xt[xs][:], in1=psA[pb][:], op=ALU.add),
                      reads=[("psA", pb), ("xt", xs)], writes=[("x1", xs)])
                sc.dma("sp", lambda e, xs=xs, r0=r0: e.dma_start(out=xout[r0:r0 + 128, :], in_=x1[xs][:]), reads=[("x1", xs)])
                sc.op("act", lambda e, xs=xs: e.activation(out=junk[:], in_=x1[xs][:], func=AF.Square), reads=[("x1", xs)], writes=["junk"])
                sc.op("dve", lambda e, tile=tile: e.reduce_sum(out=ssp[:, tile:tile + 1], in_=junk[:], axis=AX.X),
                      reads=["junk"], writes=["ssp"])
        sc.dma("sp", lambda e: e.dma_start(out=ssloc, in_=ssp[:]), reads=["ssp"])
        sc.emit()


def build_outproj_p2(nc, sc, xin, ssall, ident, gvec, out_main, xdst_fn, final, on_block_done=None):
    with ExitStack() as es:
        _UID[0] += 1
        _u = _UID[0]
        sb = lambda n, s, d: es.enter_context(nc.sbuf_tensor(f"{n}_u{_u}", s, d))
        ps = lambda n, s, d: es.enter_context(nc.psum_tensor(f"{n}_u{_u}", s, d))
        ssa = sb("ssa", [128, 2, 64], F32)
        rstd = sb("rstd", [128, 64], F32)
        eps_t = sb("epsb", [128, 1], F32)
        id_t = sb("idb", [128, 128], BF)
        gft = sb("gft", [128, 512], F32)
        xt = [sb(f"xq{i}", [128, 512], F32) for i in range(4)]
        xo = [sb(f"xo{i}", [128, 512], F32) for i in range(4)]
        xn = [sb(f"xnb{i}", [128, 512], BF) for i in range(4)]
        xTs = [sb(f"xTs{i}", [128, 4, 512], BF) for i in range(2)]
        psT = [ps(f"psTb{i}", [128, 4, 128], BF) for i in range(4)]
        sc.dma("sp", lambda e: e.dma_start(out=id_t[:], in_=ident), writes=["id_t"])
        sc.op("dve", lambda e: e.memset(eps_t[:], 1e-6), writes=["eps_t"])
        sc.dma("sp", lambda e: e.dma_start(out=ssa[:], in_=ssall.rearrange("(r p) t -> p r t", p=128)), reads=["ssall"], writes=["ssa"])
        if final:
            gsrc = bass.AP(gvec.tensor, 0, [[0, 128], [1, 512]])
            sc.dma("sp", lambda e: e.dma_start(out=gft[:], in_=gsrc), writes=["gft"])
        sc.op("dve", lambda e: e.tensor_tensor(out=rstd[:], in0=ssa[:, 0, :], in1=ssa[:, 1, :], op=ALU.add), reads=["ssa"], writes=["rstd"])
        sc.op("act", lambda e: e.activation(out=rstd[:], in_=rstd[:], func=AF.Sqrt, bias=eps_t[:, 0:1], scale=1.0 / 1024.0),
              reads=["rstd", "eps_t"], writes=["rstd"])
        sc.op("dve", lambda e: e.reciprocal(out=rstd[:], in_=rstd[:]), reads=["rstd"], writes=["rstd"])
        ti = 0
        for tb in range(S // 512):
            s = tb % 2
            for j in range(4):
                xs = ti % 4
                tile = ti
                ti += 1
                r0 = tb * 512 + j * 128
                sc.dma(("sp", "pool", "act")[ti % 3], lambda e, xs=xs, r0=r0: e.dma_start(out=xt[xs][:], in_=xin[r0:r0 + 128, :]), writes=[("xt", xs)])
                if final:
                    sc.op("dve", lambda e, xs=xs, tile=tile: e.scalar_tensor_tensor(
                        out=xo[xs][:], in0=xt[xs][:], scalar=rstd[:, tile:tile + 1], in1=gft[:], op0=ALU.mult, op1=ALU.mult),
                        reads=[("xt", xs), "rstd", "gft"], writes=[("xo", xs)])
                    sc.dma("sp", lambda e, xs=xs, r0=r0: e.dma_start(out=out_main[r0:r0 + 128, :], in_=xo[xs][:]), reads=[("xo", xs)])
                else:
                    sc.op("dve", lambda e, xs=xs, tile=tile: e.tensor_scalar(out=xn[xs][:], in0=xt[xs][:], scalar1=rstd[:, tile:tile + 1],
                                                                           scalar2=None, op0=ALU.mult),
                          reads=[("xt", xs), "rstd"], writes=[("xn", xs)])
                    for kc in range(4):
                        sc.op("pe", lambda e, xs=xs, kc=kc: e.transpose(out=psT[xs][:, kc, :], in_=xn[xs][:, kc * 128:(kc + 1) * 128], identity=id_t[:]),
                              reads=[("xn", xs), "id_t"], writes=[("psT", xs)])
                    sc.op("act", lambda e, xs=xs, s=s, j=j: e.copy(out=xTs[s][:, :, j * 128:(j + 1) * 128], in_=psT[xs][:]),
                          reads=[("psT", xs)], writes=[("xTs", s)])
            if not final:
                dst = xdst_fn(tb).rearrange("(kc p) t -> p kc t", p=128)
                sc.dma("sp", lambda e, s=s, dst=dst: e.dma_start(out=dst, in_=xTs[s][:]), reads=[("xTs", s)], writes=[("xnd", tb)])
                if on_block_done is not None:
                    on_block_done(tb)
        sc.emit()
import ml_dtypes
import numpy as np, ml_dtypes
bf = ml_dtypes.bfloat16
S = 8192
def consts_common():
    kk = np.arange(128)[:, None]; qq = np.arange(128)[None, :]
    maskD = np.concatenate([np.where(qq >= kk, 0.0, -30000.0), np.where(kk >= qq, 0.0, -30000.0)], axis=1).astype(bf)
    return {"ident": np.eye(128, dtype=bf), "maskD": maskD}
def rot_tables():
    inv = (1.0 / (np.float32(10000.0) ** np.linspace(0.0, 1.0, 128, dtype=np.float32))).astype(np.float32)
    ang = (np.arange(S, dtype=np.float32)[:, None] * inv[None, :]).astype(np.float32)
    return np.ascontiguousarray(np.cos(ang.astype(np.float64)).T.astype(np.float32)), np.ascontiguousarray(np.sin(ang.astype(np.float64)).T.astype(np.float32))
def decay_tables(hp):
    Hs = [2 * hp, 2 * hp + 1]
    lg = [float(np.log1p(-np.float32(2.0) ** np.float32(-5.0 - H))) for H in Hs]
    pos = np.arange(128, dtype=np.float64)
    maskR = np.zeros((128, 2, 128), np.float32); qdT = np.zeros((128, 2, 128), np.float32); kdec = np.zeros((128, 2), np.float32); cd = []
    for i, l in enumerate(lg):
        rel = pos[None, :] - pos[:, None]
        maskR[:, i, :] = np.where(rel >= 0, np.exp(l * np.maximum(rel, 0)), 0.0)
        qdT[:, i, :] = np.exp(l * (pos + 1.0))[None, :]
        kdec[:, i] = np.exp(l * (127.0 - pos))
        cd.append(float(np.exp(l * 128.0)))
    return maskR, qdT, kdec, cd
def w1_core(Wodd, hp):
    Hs = [2 * hp, 2 * hp + 1]
    cols = []
    for base in (0, 1024):
        for H in Hs:
            cols.append(Wodd[:, base + H * 256: base + (H + 1) * 256][:, 0::2])
            cols.append(Wodd[:, base + H * 256: base + (H + 1) * 256][:, 1::2])
    for base in (2048, 4096):
        for H in Hs:
            cols.append(Wodd[:, base + H * 512: base + (H + 1) * 512])
    return np.ascontiguousarray(np.concatenate(cols, axis=1))


from concourse.bass_utils import run_bass_kernel_spmd

NCORES = 8
GROUPS = [[0, 1], [2, 3], [4, 5], [6, 7]]


def _dt(a):
    return BF if a.dtype == bf else F32


def _core_inputs(inputs, core):
    cc = consts_common()
    b, r = core // 2, core % 2
    W = inputs["even_w_in"][0]
    hs = slice(r * 512, (r + 1) * 512)
    o = 4112
    parts = [W[:, 0:1024][:, hs], W[:, 1024:2048][:, hs], W[:, 3072:4096][:, hs],
             W[:, o:o + 1024][:, hs], W[:, o + 1024:o + 2048][:, hs], W[:, o + 3072:o + 4096][:, hs],
             W[:, 2048:3072][:, hs], W[:, o + 2048:o + 3072][:, hs], W[:, 4096 + r * 8:4096 + (r + 1) * 8]]
    perm = []
    for job in range(16):
        for rk in range(2):
            base = (rk * 8 + job) * 64 if job < 8 else 1024 + (rk * 8 + job - 8) * 64
            perm.extend(range(base, base + 64))
    perm = np.array(perm)
    cosT, sinT = rot_tables()
    maskR, qdT, kdec, cd = decay_tables(r)
    x = inputs["x"][b]
    return {"x": np.ascontiguousarray(x),
            "w": np.ascontiguousarray(np.concatenate(parts, axis=1)),
            "gn": np.ascontiguousarray(inputs["even_norm"][0].reshape(8, 128).T),
            "bfv": np.ascontiguousarray(inputs["even_b_f"][0][r * 8:(r + 1) * 8].reshape(8, 1)),
            "ident": cc["ident"], "maskD": cc["maskD"],
            "wo0": np.ascontiguousarray(inputs["even_w_out"][0][perm][:, hs]),
            "xres0": np.ascontiguousarray(x[:, hs]),
            "w1": w1_core(inputs["odd_w_in"][0], r),
            "gn1": np.ascontiguousarray(inputs["odd_norm"][0].reshape(8, 128).T),
            "cosT": cosT, "sinT": sinT, "maskR": maskR, "qdT": qdT, "kdec": kdec,
            "cdv": np.array(cd, np.float32).reshape(1, 2),
            "wo1": np.ascontiguousarray(inputs["odd_w_out"][0][:, hs]),
            "gfin": np.ascontiguousarray(inputs["final_norm"][hs].reshape(1, 512))}


def _build(sample):
    nc = bass.Bass("TRN2", target_bir_lowering=False)
    d = {k: nc.dram_tensor(k, list(v.shape), _dt(v), kind="ExternalInput").ap() for k, v in sample.items()}
    out = nc.dram_tensor("out", [S, 512], F32, kind="ExternalOutput").ap()
    scr = lambda n, shp, dt: nc.dram_tensor(n, shp, dt).ap()
    featT = scr("featT", [NFEAT, S], BF)
    vtok = scr("vtok", [S, 1024], BF)
    caug = scr("caug", [8, 6, S], BF)
    yT = scr("yT", [1024, S], BF)
    yAll = scr("yAll", [2048, S], BF)
    x1c = scr("x1c", [S, 512], F32)
    ssl1 = scr("ssl1", [128, 64], F32)
    ssa1 = scr("ssa1", [256, 64], F32)
    xnblk = scr("xnblk", [16, 512, 512], BF)
    xnAllb = scr("xnAllb", [16, 1024, 512], BF)
    y2blk = scr("y2blk", [16, 1024, 512], BF)
    y2Allb = scr("y2Allb", [16, 2048, 512], BF)
    x2c = scr("x2c", [S, 512], F32)
    ssl2 = scr("ssl2", [128, 64], F32)
    ssa2 = scr("ssa2", [256, 64], F32)
    sc = Sched(nc)
    build_proj_phase(nc, sc, d["x"], d["w"], d["gn"], d["bfv"], d["ident"], featT, vtok, caug)
    build_attn_phase(nc, sc, featT, vtok, caug, d["ident"], d["maskD"], yT,
                     on_job_done=lambda job: sc.coll(yT[job * 64:(job + 1) * 64, :], yAll[job * 128:(job + 1) * 128, :], GROUPS,
                                                     reads=[("yTd", job)]))
    build_outproj_p1(nc, sc, lambda tb: yAll[:, tb * 512:(tb + 1) * 512], d["wo0"], d["xres0"], x1c, ssl1)
    sc.coll(ssl1, ssa1, GROUPS, writes=["ssall"])
    build_outproj_p2(nc, sc, x1c, ssa1, d["ident"], None, None, lambda tb: xnblk[tb], False,
                     on_block_done=lambda tb: sc.coll(xnblk[tb], xnAllb[tb], GROUPS, reads=[("xnd", tb)]))
    build_retention(nc, sc, lambda tb: xnAllb[tb], d["w1"], d["gn1"], d["ident"], d["cosT"], d["sinT"], d["maskR"], d["qdT"], d["kdec"],
                    d["cdv"], lambda tb, c: y2blk[tb][:, c * 128:(c + 1) * 128],
                    on_block_done=lambda tb: sc.coll(y2blk[tb], y2Allb[tb], GROUPS, reads=[("y2d", tb)]))
    build_outproj_p1(nc, sc, lambda tb: y2Allb[tb], d["wo1"], x1c, x2c, ssl2)
    sc.coll(ssl2, ssa2, GROUPS, writes=["ssall"])
    build_outproj_p2(nc, sc, x2c, ssa2, d["ident"], d["gfin"], out, None, True)
    return nc


def kernel(**inputs):
    inputs = {k: np.asarray(v) for k, v in inputs.items()}
    maps = [_core_inputs(inputs, c) for c in range(NCORES)]
    nc = _build(maps[0])
    res = run_bass_kernel_spmd(nc, maps, core_ids=list(range(NCORES))).results
    out = np.empty((4, S, 1024), np.float32)
    for core in range(NCORES):
        b, r = core // 2, core % 2
        out[b, :, r * 512:(r + 1) * 512] = np.asarray(res[core]["out"])
    return out
```

```python
import concourse.bass as bass
import concourse.mybir as mybir

ENGS = ("pe", "act", "dve", "pool", "sp")
DMA_POOL = 12


class Sched:
    def __init__(self, nc, same_engine_sync=True):
        self.nc = nc
        self.q = {e: [] for e in ENGS}
        self.cnt = {e: 0 for e in ENGS}
        self.sem = {e: nc.alloc_semaphore(f"sq_{e}") for e in ENGS if e != "sp"}
        self.dsem = {}
        for e in ("sp", "pool", "act"):
            self.dsem[e] = [nc.alloc_semaphore(f"sd_{e}{i}") for i in range(DMA_POOL)]
        self.dcnt = {e: [0] * DMA_POOL for e in self.dsem}
        self.drr = {e: 0 for e in self.dsem}
        self.waited = {e: {} for e in ENGS}
        self.res = {}
        self.semobj = {}
        self.same = same_engine_sync
        self.nwaits = 0
        self.clear_sems()

    def all_sems(self):
        return list(self.sem.values()) + [s for e in self.dsem for s in self.dsem[e]]

    def clear_sems(self):
        nc = self.nc
        sems = self.all_sems()
        with nc.Block() as block:
            def body(g):
                for s in sems:
                    g.sem_clear(s)
            block.gpsimd(body)

    def _deps(self, eng, reads, writes):
        deps = {}

        def add(tok):
            if tok is None:
                return
            k, v = tok
            if deps.get(k, 0) < v:
                deps[k] = v

        for r in reads:
            st = self.res.get(r)
            if st is not None:
                add(st["w"])
        for w in writes:
            st = self.res.get(w)
            if st is not None:
                add(st["w"])
                for k, v in st["r"].items():
                    add((k, v))
        return deps

    def _commit(self, tok, reads, writes):
        for r in reads:
            st = self.res.setdefault(r, {"w": None, "r": {}})
            k, v = tok
            if st["r"].get(k, 0) < v:
                st["r"][k] = v
        for w in writes:
            self.res[w] = {"w": tok, "r": {}}

    def _waits(self, eng, deps, own_key):
        ws = []
        for k, v in deps.items():
            if k == own_key and (eng == "pe" or not self.same):
                continue
            if self.waited[eng].get(k, 0) >= v:
                continue
            self.waited[eng][k] = v
            ws.append((k, v))
        self.nwaits += len(ws)
        return ws

    def op(self, eng, fn, reads=(), writes=()):
        deps = self._deps(eng, reads, writes)
        sem = self.sem[eng]
        key = "q_" + eng
        self.semobj[key] = sem
        ws = self._waits(eng, deps, key)
        self.cnt[eng] += 1
        tok = (key, self.cnt[eng])
        self.q[eng].append((fn, ws, (sem, 1, self.cnt[eng])))
        self._commit(tok, reads, writes)
        return tok

    def dma(self, eng, fn, reads=(), writes=()):
        deps = self._deps(eng, reads, writes)
        i = self.drr[eng]
        self.drr[eng] = (i + 1) % DMA_POOL
        sem = self.dsem[eng][i]
        key = f"d_{eng}{i}"
        self.semobj[key] = sem
        prev = self.dcnt[eng][i]
        if prev > 0:
            deps[key] = max(deps.get(key, 0), 16 * prev)
        ws = self._waits(eng, deps, None)
        self.dcnt[eng][i] += 1
        tok = (key, 16 * self.dcnt[eng][i])
        self.q[eng].append((fn, ws, (sem, 16)))
        self._commit(tok, reads, writes)
        return tok

    def coll(self, src_ap, dst_ap, groups, reads=(), writes=()):
        nc = self.nc
        if not hasattr(self, "cc_toks"):
            self.cc_toks = []
        n = len(self.cc_toks)
        sem = nc.alloc_semaphore(f"ccs{n}")
        key = f"cc{n}"
        self.semobj[key] = sem
        deps = self._deps("pool", reads, writes)
        if self.cc_toks:
            pk, pv = self.cc_toks[-1]
            deps[pk] = max(deps.get(pk, 0), pv)
        ws = self._waits("pool", deps, None)
        tok = (key, 1)
        fn = lambda g: g.collective_compute("AllGather", mybir.AluOpType.bypass, replica_groups=groups,
                                            ins=[src_ap.opt()], outs=[dst_ap.opt()])
        self.q["pool"].append((fn, ws, (sem, None)))
        self._commit(tok, reads, writes)
        self.cc_toks.append(tok)
        return tok

    def barrier(self):
        toks = {}
        for e in ENGS:
            if e != "sp" and self.cnt[e] > 0:
                toks["q_" + e] = self.cnt[e]
        for e in self.dsem:
            for i in range(DMA_POOL):
                if self.dcnt[e][i] > 0:
                    toks[f"d_{e}{i}"] = 16 * self.dcnt[e][i]
        for k, v in getattr(self, "cc_toks", []):
            toks[k] = v
        for e in ENGS:
            ws = []
            for k, v in toks.items():
                if self.waited[e].get(k, 0) >= v:
                    continue
                self.waited[e][k] = v
                ws.append((k, v))
            if ws:
                self.q[e].append((None, ws, None))

    def emit(self):
        nc = self.nc
        self.barrier()
        if not any(self.q[e] for e in ENGS):
            return
        handles = {"pe": "tensor", "act": "scalar", "dve": "vector", "pool": "gpsimd", "sp": "sync"}
        if not hasattr(self, "base"):
            self.base = {}
        needed = {}
        for e in ENGS:
            for fn, ws, inc in self.q[e]:
                for k, v in ws:
                    if k.startswith("q_"):
                        needed.setdefault(k, set()).add(v)
        valmap = {}
        for k, idxs in needed.items():
            b = self.base.get(k, 0)
            for r, idx in enumerate(sorted(idxs)):
                valmap[(k, idx)] = b + r + 1
            self.base[k] = b + len(idxs)
        with nc.Block() as block:
            for e in ENGS:
                items = self.q[e]
                if not items:
                    continue

                def body(engh, items=items, e=e):
                    for fn, ws, inc in items:
                        for k, v in ws:
                            if k.startswith("q_"):
                                engh.wait_ge(self.semobj[k], valmap[(k, v)])
                            else:
                                engh.wait_ge(self.semobj[k], v)
                        if fn is not None:
                            inst = fn(engh)
                            if len(inc) == 3:
                                if ("q_" + e, inc[2]) in valmap:
                                    inst.then_inc(inc[0], 1)
                            elif inc[1] is None:
                                inst.then_inc(inc[0])
                            else:
                                inst.then_inc(inc[0], inc[1])

                getattr(block, handles[e])(body)
        self.q = {e: [] for e in ENGS}
        self.res = {}
import numpy as np
import concourse.bass as bass
import concourse.mybir as mybir
from contextlib import ExitStack

F32 = mybir.dt.float32
BF = mybir.dt.bfloat16
AF = mybir.ActivationFunctionType
ALU = mybir.AluOpType
AX = mybir.AxisListType

S = 8192
_UID = [0]
NTB = S // 512
NFEAT = 3072
NV = 1024
NW = NFEAT + NV + 8
MASKNEG = -30000.0


def build_proj_phase(nc, sc, x, w, gn, bfv, ident, featT, vtok, caug, ntb=NTB):
    with ExitStack() as es:
        Wb = es.enter_context(nc.sbuf_tensor("Wb_pa", [128, 8, NW], BF))
        gn_t = es.enter_context(nc.sbuf_tensor("gn_t_pa", [128, 8], F32))
        id_t = es.enter_context(nc.sbuf_tensor("id_t_pa", [128, 128], BF))
        eps_t = es.enter_context(nc.sbuf_tensor("eps_t_pa", [128, 1], F32))
        one_t = es.enter_context(nc.sbuf_tensor("one_t_pa", [128, 1], F32))
        nb_t = es.enter_context(nc.sbuf_tensor("nb_t_pa", [8, 1], F32))
        ones8 = es.enter_context(nc.sbuf_tensor("ones8_pa", [8, 512], F32))
        xb0 = es.enter_context(nc.sbuf_tensor("xb0_pa", [128, NW], F32))
        xb1 = es.enter_context(nc.sbuf_tensor("xb1_pa", [128, NW], F32))
        junk = es.enter_context(nc.sbuf_tensor("junk_pa", [128, 1024], F32))
        ss = es.enter_context(nc.sbuf_tensor("ss_pa", [128, 8], F32))
        xn0 = es.enter_context(nc.sbuf_tensor("xn0_pa", [128, 1024], BF))
        xn1 = es.enter_context(nc.sbuf_tensor("xn1_pa", [128, 1024], BF))
        xT0 = es.enter_context(nc.sbuf_tensor("xT0_pa", [128, 8, 512], BF))
        xT1 = es.enter_context(nc.sbuf_tensor("xT1_pa", [128, 8, 512], BF))
        stF = es.enter_context(nc.sbuf_tensor("stF_pa", [128, 8, 512], BF))
        stV = es.enter_context(nc.sbuf_tensor("stV_pa", [128, 4, 512], BF))
        fw = es.enter_context(nc.sbuf_tensor("fw_pa", [8, 2, 6, 512], F32))
        cs = es.enter_context(nc.sbuf_tensor("cs_pa", [8, 2, 6, 512], BF))
        psT0 = es.enter_context(nc.psum_tensor("psT0_pa", [128, 8, 128], BF))
        psT1 = es.enter_context(nc.psum_tensor("psT1_pa", [128, 8, 128], BF))
        psF0 = es.enter_context(nc.psum_tensor("psF0_pa", [128, 512], F32))
        psF1 = es.enter_context(nc.psum_tensor("psF1_pa", [128, 512], F32))
        psF2 = es.enter_context(nc.psum_tensor("psF2_pa", [128, 512], F32))
        psV0 = es.enter_context(nc.psum_tensor("psV0_pa", [128, 512], F32))
        psV1 = es.enter_context(nc.psum_tensor("psV1_pa", [128, 512], F32))
        psf = es.enter_context(nc.psum_tensor("psf_pa", [8, 512], F32))
        xb = [xb0, xb1]
        xn = [xn0, xn1]
        xT = [xT0, xT1]
        psT = [psT0, psT1]
        psF = [psF0, psF1, psF2]
        psV = [psV0, psV1]
        sc.dma("sp", lambda e: e.dma_start(out=gn_t[:], in_=gn), writes=["gn_t"])
        sc.dma("sp", lambda e: e.dma_start(out=id_t[:], in_=ident), writes=["id_t"])
        sc.dma("sp", lambda e: e.dma_start(out=nb_t[:], in_=bfv), writes=["nb_t"])
        sc.op("dve", lambda e: e.memset(eps_t[:], 1e-6), writes=["eps_t"])
        sc.op("dve", lambda e: e.memset(one_t[:], 1.0), writes=["one_t"])
        sc.op("dve", lambda e: e.memset(ones8[:], 1.0), writes=["ones8"])
        sc.op("dve", lambda e: e.tensor_scalar(out=nb_t[:], in0=nb_t[:], scalar1=-1.0, scalar2=None, op0=ALU.mult),
              reads=["nb_t"], writes=["nb_t"])
        for kc in range(8):
            s = kc % 2
            sc.dma("sp" if s == 0 else "pool",
                   lambda e, kc=kc, s=s: e.dma_start(out=xb[s][:, :], in_=w[kc * 128:(kc + 1) * 128, :]),
                   writes=[("xb", s)])
            eng = "dve" if s == 0 else "pool"
            sc.op(eng, lambda e, kc=kc, s=s: e.tensor_scalar(
                out=Wb[:, kc, :], in0=xb[s][:, :], scalar1=gn_t[:, kc:kc + 1], scalar2=None, op0=ALU.mult),
                reads=[("xb", s), "gn_t"], writes=[("Wb", kc)])
        WbR = [("Wb", kc) for kc in range(8)]
        evac_box = [0]

        def load_x(tb):
            s = tb % 2
            xv = xb[s][:, 0:4096].rearrange("p (j f) -> p j f", j=4)
            src = x[tb * 512:(tb + 1) * 512, :].rearrange("(j p) f -> p j f", p=128)
            sc.dma("sp", lambda e, xv=xv, src=src: e.dma_start(out=xv, in_=src), writes=[("xb", s)])

        def norm_tile(tb, j):
            s = tb % 2
            xv = xb[s][:, 0:4096].rearrange("p (j f) -> p j f", j=4)
            xs = (tb * 4 + j) % 2
            col = (tb * 4 + j) % 8
            sc.op("act", lambda e: e.activation(out=junk[:], in_=xv[:, j, :], func=AF.Square),
                  reads=[("xb", s)], writes=["junk"])
            sc.op("dve", lambda e: e.reduce_sum(out=ss[:, col:col + 1], in_=junk[:], axis=AX.X),
                  reads=["junk"], writes=[("ss", col)])
            sc.op("act", lambda e: e.activation(out=ss[:, col:col + 1], in_=ss[:, col:col + 1], func=AF.Sqrt,
                                                bias=eps_t[:, 0:1], scale=1.0 / 1024.0),
                  reads=[("ss", col), "eps_t"], writes=[("ss", col)])
            sc.op("dve", lambda e: e.reciprocal(out=ss[:, col:col + 1], in_=ss[:, col:col + 1]),
                  reads=[("ss", col)], writes=[("ss", col)])
            sc.op("dve", lambda e: e.tensor_scalar(out=xn[xs][:], in0=xv[:, j, :], scalar1=ss[:, col:col + 1], scalar2=None, op0=ALU.mult),
                  reads=[("xb", s), ("ss", col)], writes=[("xn", xs)])

        def transp_tile(tb, j):
            s = tb % 2
            xs = (tb * 4 + j) % 2
            for kc in range(8):
                sc.op("pe", lambda e, kc=kc: e.transpose(out=psT[xs][:, kc, :], in_=xn[xs][:, kc * 128:(kc + 1) * 128], identity=id_t[:]),
                      reads=[("xn", xs), "id_t"], writes=[("psT", xs)])
            sc.op("act", lambda e: e.copy(out=xT[s][:, :, j * 128:(j + 1) * 128], in_=psT[xs][:]),
                  reads=[("psT", xs)], writes=[("xT", s, j)])

        def feat_group(tb, cg):
            s = tb % 2
            pb = cg % 3
            for kc in range(8):
                sc.op("pe", lambda e, kc=kc: e.matmul(
                    psF[pb][:], lhsT=Wb[:, kc, cg * 128:(cg + 1) * 128], rhs=xT[s][:, kc, :],
                    start=(kc == 0), stop=(kc == 7)),
                    reads=[("xT", s, jj) for jj in range(4)] + [("Wb", kc)], writes=[("psF", pb)])
            sl = evac_box[0] % 8
            evac_box[0] += 1
            kind = (cg // 4) % 3
            if kind == 0:
                sc.op("act", lambda e: e.activation(out=stF[:, sl, :], in_=psF[pb][:], func=AF.Copy, scale=0.125),
                      reads=[("psF", pb)], writes=[("stF", sl)])
            elif kind == 1:
                sc.op("dve", lambda e: e.tensor_copy(out=stF[:, sl, :], in_=psF[pb][:]),
                      reads=[("psF", pb)], writes=[("stF", sl)])
            else:
                sc.op("act", lambda e: e.activation(out=stF[:, sl, :], in_=psF[pb][:], func=AF.Silu),
                      reads=[("psF", pb)], writes=[("stF", sl)])
            sc.dma("sp" if sl % 2 == 0 else "act", lambda e: e.dma_start(out=featT[cg * 128:(cg + 1) * 128, tb * 512:(tb + 1) * 512], in_=stF[:, sl, :]),
                   reads=[("stF", sl)])

        def v_group(tb, j, half):
            s = tb % 2
            pv = (j * 2 + half) % 2
            for kc in range(8):
                sc.op("pe", lambda e, kc=kc: e.matmul(
                    psV[pv][:], lhsT=xT[s][:, kc, j * 128:(j + 1) * 128],
                    rhs=Wb[:, kc, NFEAT + half * 512:NFEAT + (half + 1) * 512],
                    start=(kc == 0), stop=(kc == 7)),
                    reads=[("xT", s, j), ("Wb", kc)], writes=[("psV", pv)])
            sl = (j * 2 + half) % 4
            sc.op("dve", lambda e: e.tensor_copy(out=stV[:, sl, :], in_=psV[pv][:]),
                  reads=[("psV", pv)], writes=[("stV", sl)])
            r0 = tb * 512 + j * 128
            sc.dma("pool", lambda e: e.dma_start(out=vtok[r0:r0 + 128, half * 512:(half + 1) * 512], in_=stV[:, sl, :]),
                   reads=[("stV", sl)])

        load_x(0)
        if ntb > 1:
            load_x(1)
        for j in range(4):
            norm_tile(0, j)
            transp_tile(0, j)
        for tb in range(ntb):
            s = tb % 2
            nxt = tb + 1 < ntb
            if tb + 2 < ntb:
                load_x(tb + 2)
            groups = [(lambda cg=cg: feat_group(tb, cg)) for cg in range(NFEAT // 128)]
            groups += [(lambda j=j, half=half: v_group(tb, j, half)) for j in range(4) for half in range(2)]
            for gi, g in enumerate(groups):
                g()
                if nxt and gi % 8 == 1:
                    norm_tile(tb + 1, gi // 8)
                if nxt and gi % 8 == 5:
                    transp_tile(tb + 1, gi // 8)
            for kc in range(8):
                sc.op("pe", lambda e, kc=kc, s=s: e.matmul(
                    psf[:], lhsT=Wb[:, kc, NFEAT + NV:NFEAT + NV + 8], rhs=xT[s][:, kc, :],
                    start=(kc == 0), stop=(kc == 7)),
                    reads=[("xT", s, jj) for jj in range(4)] + [("Wb", kc)], writes=["psf"])
            fs = tb % 2
            fwv = fw[:, fs]
            csv = cs[:, fs]
            R_fw = ("fw", fs)
            sc.op("act", lambda e, fwv=fwv: e.activation(out=fwv[:, 0, :], in_=psf[:], func=AF.Exp, bias=nb_t[:, 0:1], scale=-1.0),
                  reads=["psf", "nb_t"], writes=[R_fw])
            sc.op("act", lambda e, fwv=fwv: e.activation(out=fwv[:, 1, :], in_=fwv[:, 0, :], func=AF.Ln, bias=one_t[0:8, 0:1], scale=1.0),
                  reads=[R_fw, "one_t"], writes=[R_fw])
            if tb == 0:
                init = 0.0
                rd = [R_fw, "ones8"]
            else:
                init = fw[:, 1 - fs, 2, 511:512]
                rd = [R_fw, "ones8", ("fw", 1 - fs)]
            sc.op("dve", lambda e, fwv=fwv, init=init: e.tensor_tensor_scan(
                out=fwv[:, 2, :], data0=ones8[:], data1=fwv[:, 1, :], initial=init, op0=ALU.mult, op1=ALU.subtract),
                reads=rd, writes=[R_fw])
            R_cs = ("cs", fs)
            sc.op("dve", lambda e, fwv=fwv, csv=csv: e.tensor_copy(out=csv[:, 0, :], in_=fwv[:, 2, :]), reads=[R_fw], writes=[R_cs])
            sc.op("dve", lambda e, fwv=fwv, csv=csv: e.tensor_tensor(out=fwv[:, 3, :], in0=fwv[:, 2, :], in1=csv[:, 0, :], op=ALU.subtract),
                  reads=[R_fw, R_cs], writes=[R_fw])
            sc.op("dve", lambda e, fwv=fwv, csv=csv: e.tensor_copy(out=csv[:, 1, :], in_=fwv[:, 3, :]), reads=[R_fw], writes=[R_cs])
            sc.op("dve", lambda e, fwv=fwv, csv=csv: e.tensor_tensor(out=fwv[:, 4, :], in0=fwv[:, 3, :], in1=csv[:, 1, :], op=ALU.subtract),
                  reads=[R_fw, R_cs], writes=[R_fw])
            sc.op("dve", lambda e, fwv=fwv, csv=csv: e.tensor_copy(out=csv[:, 2, :], in_=fwv[:, 4, :]), reads=[R_fw], writes=[R_cs])
            sc.op("dve", lambda e, csv=csv: e.tensor_scalar(out=csv[:, 3:6, :], in0=csv[:, 0:3, :], scalar1=-1.0, scalar2=None, op0=ALU.mult),
                  reads=[R_cs], writes=[R_cs])
            sc.dma("sp", lambda e, csv=csv, tb=tb: e.dma_start(out=caug[:, :, tb * 512:(tb + 1) * 512], in_=csv),
                   reads=[R_cs])
        sc.emit()


def dram_ap(ap, offset, dims):
    return bass.AP(ap.tensor, offset, [list(d) for d in dims])


def build_attn_phase(nc, sc, featT, vtok, caug, ident, maskD, yT, fox_heads=range(8), dil_heads=range(8), nqc=16, on_job_done=None):
    with ExitStack() as es:
        _UID[0] += 1
        _u = _UID[0]
        sb = lambda n, s, d: es.enter_context(nc.sbuf_tensor(f"{n}_u{_u}", s, d))
        ps = lambda n, s, d: es.enter_context(nc.psum_tensor(f"{n}_u{_u}", s, d))
        Qa = [sb(f"Qa{i}", [70, S], BF) for i in range(2)]
        Ka = [sb(f"Ka{i}", [70, S], BF) for i in range(2)]
        Vt = [sb(f"Vt{i}", [128, 3, 64, 65], BF) for i in range(2)]
        acc = sb("acc", [65, S], F32)
        pT2 = [sb(f"pT{i}", [128, 1024], BF) for i in range(2)]
        pT = [pT2[0][:, 0:512], pT2[0][:, 512:1024], pT2[1][:, 0:512], pT2[1][:, 512:1024]]
        Gc = [sb(f"Gc{i}", [64, 512], BF) for i in range(3)]
        rec = [sb(f"rec{i}", [65, 512], F32) for i in range(3)]
        bcs = [sb(f"bcs{i}", [64, 512], F32) for i in range(3)]
        obt = [sb(f"obt{i}", [64, 512], F32) for i in range(3)]
        ysb = [sb(f"ysb{i}", [64, 512], BF) for i in range(3)]
        ones_b = sb("ones_b", [65, 64], BF)
        rhi = [sb(f"rhi{i}", [65, 512], BF) for i in range(3)]
        rlo = [sb(f"rlo{i}", [65, 512], BF) for i in range(3)]
        id_t = sb("id_t2", [128, 128], BF)
        mk_t = sb("mk_t", [128, 256], BF)
        mk_m = sb("mk_m", [128, 256], BF)
        psS2 = [ps(f"psS{i}", [128, 1024], F32) for i in range(2)]
        psS = [psS2[0][:, 0:512], psS2[0][:, 512:1024], psS2[1][:, 0:512], psS2[1][:, 512:1024]]
        psO = [ps(f"psO{i}", [128, 512], F32) for i in range(2)]
        psB = ps("psB", [64, 512], F32)

        sc.dma("sp", lambda e: e.dma_start(out=id_t[:], in_=ident), writes=["id_t"])
        sc.dma("sp", lambda e: e.dma_start(out=mk_t[:], in_=maskD), writes=["mk_t"])
        sc.op("dve", lambda e: e.memset(ones_b[:], 1.0), writes=["ones_b"])
        sc.op("dve", lambda e: e.tensor_scalar(out=mk_m[:], in0=mk_t[:], scalar1=0.0, scalar2=None, op0=ALU.is_equal),
              reads=["mk_t"], writes=["mk_m"])
        for i in range(2):
            sc.op("dve", lambda e, i=i: e.memset(Qa[i][64:70, :], 1.0), writes=[("Qa", i)])
            sc.op("dve", lambda e, i=i: e.memset(Ka[i][64:70, :], 1.0), writes=[("Ka", i)])
            sc.op("pool", lambda e, i=i: e.memset(Vt[i][:, :, :, 64:65], 1.0), writes=[("Vt", i)])

        jobs = [("fox", h) for h in fox_heads] + [("dil", h) for h in dil_heads]

        def load(ji):
            kind, h = jobs[ji]
            s = ji % 2
            if kind == "fox":
                qrow, krow, vcol = h * 64, 512 + h * 64, h * 64
            else:
                qrow, krow, vcol = 1536 + h * 64, 2048 + h * 64, 512 + h * 64
            sc.dma("sp", lambda e: e.dma_start(out=Qa[s][0:64, :], in_=featT[qrow:qrow + 64, :]), writes=[("Qa", s)])
            sc.dma("sp", lambda e: e.dma_start(out=Ka[s][0:64, :], in_=featT[krow:krow + 64, :]), writes=[("Ka", s)])
            if kind == "fox":
                sc.dma("sp", lambda e: e.dma_start(out=Qa[s][64:67, :], in_=caug[h, 0:3, :]), writes=[("Qa", s)])
                sc.dma("sp", lambda e: e.dma_start(out=Ka[s][67:70, :], in_=caug[h, 3:6, :]), writes=[("Ka", s)])
                pats = [(0, 1)]
            else:
                pats = [(0, 1), (1, 4), (2, 16)]
            for pi, r in pats:
                nblk = S // r // 128
                for s_ in range(r):
                    step = 16 if nblk >= 16 else nblk
                    for j0 in range(0, nblk, step):
                        src = dram_ap(vtok, (s_ + r * 128 * j0) * 1024 + vcol,
                                      [[r * 1024, 128], [r * 128 * 1024, step], [1, 64]])
                        t0 = s_ * nblk + j0
                        sc.dma("sp", lambda e, src=src, pi=pi, t0=t0, step=step: e.dma_start(
                            out=Vt[s][:, pi, t0:t0 + step, 0:64], in_=src), writes=[("Vt", s)])

        NS = 3
        pending = []
        slot_ctr = [0]
        tick = [0]

        def normalize(kind, h, qc, src_ps, ob, job=0):
            pending.append({"state": 0, "kind": kind, "h": h, "qc": qc, "src_ps": src_ps, "ob": ob, "job": job})

        def _srcs(t):
            cols = slice(t["qc"] * 512, (t["qc"] + 1) * 512)
            if t["src_ps"]:
                return psO[t["ob"]][64:65, :], psO[t["ob"]][0:64, :], [("psO", t["ob"])], cols
            return acc[64:65, cols], acc[0:64, cols], [("acc", t["qc"])], cols

        def stage1(t):
            rs = slot_ctr[0] % NS
            slot_ctr[0] += 1
            t["rs"] = rs
            den, num, rd, cols = _srcs(t)
            grow = (1024 if t["kind"] == "fox" else 2560) + t["h"] * 64
            sc.dma("sp", lambda e: e.dma_start(out=Gc[rs][:], in_=featT[grow:grow + 64, cols]), writes=[("Gc", rs)])
            sc.op("dve", lambda e: e.reciprocal(out=rec[rs][64:65, :], in_=den), reads=rd, writes=[("rec", rs)])
            sc.op("dve", lambda e: e.tensor_copy(out=rhi[rs][64:65, :], in_=rec[rs][64:65, :]), reads=[("rec", rs)], writes=[("rhi", rs)])
            sc.op("dve", lambda e: e.tensor_tensor(out=rlo[rs][64:65, :], in0=rec[rs][64:65, :], in1=rhi[rs][64:65, :], op=ALU.subtract),
                  reads=[("rec", rs), ("rhi", rs)], writes=[("rlo", rs)])
            t["state"] = 1
            t["t_issue"] = tick[0]

        def stage2(t):
            rs = t["rs"]
            den, num, rd, cols = _srcs(t)
            yrow = (0 if t["kind"] == "fox" else 512) + t["h"] * 64
            job = t["job"]
            sc.op("pe", lambda e: e.matmul(psB[:], lhsT=ones_b[64:65, 0:64], rhs=rhi[rs][64:65, :], start=True, stop=False),
                  reads=[("rhi", rs), "ones_b"], writes=["psB"])
            sc.op("pe", lambda e: e.matmul(psB[:], lhsT=ones_b[64:65, 0:64], rhs=rlo[rs][64:65, :], start=False, stop=True),
                  reads=[("rlo", rs), "ones_b"], writes=["psB"])
            if t["src_ps"]:
                sc.op("dve", lambda e: e.tensor_copy(out=bcs[rs][:], in_=psB[:]), reads=["psB"], writes=[("bcs", rs)])
                sc.op("dve", lambda e: e.tensor_tensor(out=obt[rs][:], in0=num, in1=bcs[rs][:], op=ALU.mult),
                      reads=rd + [("bcs", rs)], writes=[("obt", rs)])
            else:
                sc.op("dve", lambda e: e.tensor_tensor(out=obt[rs][:], in0=num, in1=psB[:], op=ALU.mult),
                      reads=rd + ["psB"], writes=[("obt", rs)])
            sc.op("pool", lambda e: e.tensor_tensor(out=ysb[rs][:], in0=obt[rs][:], in1=Gc[rs][:], op=ALU.mult),
                  reads=[("obt", rs), ("Gc", rs)], writes=[("ysb", rs)])
            sc.dma("pool", lambda e: e.dma_start(out=yT[yrow:yrow + 64, cols], in_=ysb[rs][:]), reads=[("ysb", rs)], writes=[("yTd", job)])
            pending.remove(t)

        def pump():
            tick[0] += 1
            for t in list(pending):
                if t["state"] == 1:
                    if tick[0] - t["t_issue"] >= 6:
                        stage2(t)
                    break
            if sum(1 for t in pending if t["state"] == 1) < NS:
                for t in pending:
                    if t["state"] == 0:
                        stage1(t)
                        break

        def flush(pred=lambda t: True):
            for t in list(pending):
                if pred(t):
                    for u in list(pending):
                        if u is t:
                            break
                        if u["state"] == 1:
                            stage2(u)
                    if t["state"] == 0:
                        stage1(t)
                    stage2(t)

        tile_ctr = [0]

        def run_tiles(tiles):
            LA = 3
            base = tile_ctr[0]
            n = len(tiles)
            for i in range(min(LA, n)):
                tiles[i]["s_fn"]((base + i) % 4)
            for i in range(n):
                if i + LA < n:
                    tiles[i + LA]["s_fn"]((base + i + LA) % 4)
                b = (base + i) % 4
                c0, c1 = tiles[i]["cr"]
                sc.op("act", lambda e, b=b, c0=c0, c1=c1: e.activation(out=pT[b][:, c0:c1], in_=psS[b][:, c0:c1], func=AF.Exp),
                      reads=[("psS", b)], writes=[("pT", b)])
                if "mask" in tiles[i]:
                    m0 = tiles[i]["mask"]
                    sc.op("dve", lambda e, b=b, c0=c0, c1=c1, m0=m0: e.tensor_tensor(
                        out=pT[b][:, c0:c1], in0=pT[b][:, c0:c1], in1=mk_m[:, m0:m0 + (c1 - c0)], op=ALU.mult),
                        reads=[("pT", b), "mk_m"], writes=[("pT", b)])
                tiles[i]["pv_fn"](b)
                if tiles[i].get("post"):
                    tiles[i]["post"]()
                pump()
            tile_ctr[0] = base + n

        ob_box = [0]

        def do_job(ji, kind, h):
            s = ji % 2
            ob_ctr = ob_box[0]
            RQ, RK, RV = ("Qa", s), ("Ka", s), ("Vt", s)
            tiles = []
            if kind == "fox":
                for qc in range(nqc):
                    ob = ob_ctr % 2
                    ob_ctr += 1
                    nk = 4 * qc + 4
                    for kt in range(nk):
                        j = kt - 4 * qc
                        c0 = 128 * max(j, 0)

                        def s_fn(b, kt=kt, qc=qc, j=j, c0=c0):
                            kap = Ka[s][0:70, kt * 128:(kt + 1) * 128]
                            if j < 0:
                                sc.op("pe", lambda e: e.matmul(psS[b][:, 0:512], lhsT=kap, rhs=Qa[s][0:70, qc * 512:(qc + 1) * 512],
                                                               start=True, stop=True),
                                      reads=[RQ, RK], writes=[("psS", b)])
                            else:
                                q0 = qc * 512 + c0
                                sc.op("pe", lambda e: e.matmul(psS[b][:, c0:c0 + 128], lhsT=id_t[:], rhs=mk_t[:, 0:128],
                                                               start=True, stop=False),
                                      reads=["id_t", "mk_t"], writes=[("psS", b)])
                                sc.op("pe", lambda e: e.matmul(psS[b][:, c0:c0 + 128], lhsT=kap, rhs=Qa[s][0:70, q0:q0 + 128],
                                                               start=False, stop=True),
                                      reads=[RQ, RK], writes=[("psS", b)])
                                if c0 + 128 < 512:
                                    sc.op("pe", lambda e: e.matmul(psS[b][:, c0 + 128:512], lhsT=kap,
                                                                   rhs=Qa[s][0:70, q0 + 128:(qc + 1) * 512], start=True, stop=True),
                                          reads=[RQ, RK], writes=[("psS", b)])

                        def pv_fn(b, kt=kt, c0=c0, ob=ob, nk=nk):
                            if kt == 0:
                                flush(lambda t: t["src_ps"] and t["ob"] == ob)
                            sc.op("pe", lambda e: e.matmul(psO[ob][0:65, c0:512], lhsT=Vt[s][:, 0, kt, 0:65], rhs=pT[b][:, c0:512],
                                                           start=(kt == 0), stop=(kt == nk - 1), skip_group_check=True),
                                  reads=[("pT", b), RV], writes=[("psO", ob)])

                        t = {"s_fn": s_fn, "cr": (c0, 512), "pv_fn": pv_fn}
                        if kt == nk - 1:
                            t["post"] = (lambda qc=qc, ob=ob: normalize("fox", h, qc, True, ob, ji))
                        tiles.append(t)
                run_tiles_pairs(tiles)
            else:
                for pi, r in [(0, 1), (1, 4), (2, 16)]:
                    nblk = S // r // 128
                    for s_ in range(r):
                        for c in range(nblk // 4):
                            n0 = 4 * c
                            ob = ob_ctr % 2
                            ob_ctr += 1
                            qbase = s_ + r * 128 * n0
                            js = ([n0 - 1] if n0 > 0 else []) + list(range(n0, n0 + 4))
                            for j in js:
                                b_lo, b_hi = max(j, n0), min(j + 1, n0 + 3)
                                c0, c1 = (b_lo - n0) * 128, (b_hi - n0 + 1) * 128
                                m0 = 0 if b_lo == j else 128

                                def s_fn(b, j=j, c0=c0, c1=c1, m0=m0, r=r, s_=s_, qbase=qbase):
                                    kb = s_ + r * 128 * j
                                    kap = Ka[s][0:64, kb:kb + r * 127 + 1:r]
                                    qap = Qa[s][0:64, qbase + r * c0:qbase + r * (c1 - 1) + 1:r]
                                    sc.op("pe", lambda e: e.matmul(psS[b][:, c0:c1], lhsT=id_t[:], rhs=mk_t[:, m0:m0 + (c1 - c0)],
                                                                   start=True, stop=False),
                                          reads=["id_t", "mk_t"], writes=[("psS", b)])
                                    sc.op("pe", lambda e: e.matmul(psS[b][:, c0:c1], lhsT=kap, rhs=qap, start=False, stop=True),
                                          reads=[RQ, RK], writes=[("psS", b)])

                                def pv_fn(b, j=j, b_lo=b_lo, b_hi=b_hi, n0=n0, ob=ob, pi=pi, s_=s_, nblk=nblk):
                                    flush(lambda t: t["src_ps"] and t["ob"] == ob)
                                    for bb in range(b_lo, b_hi + 1):
                                        cb = (bb - n0) * 128
                                        st = (j == bb - 1) or (bb == 0 and j == 0)
                                        sc.op("pe", lambda e, cb=cb, st=st, bb=bb: e.matmul(
                                            psO[ob][0:65, cb:cb + 128], lhsT=Vt[s][:, pi, s_ * nblk + j, 0:65], rhs=pT[b][:, cb:cb + 128],
                                            start=st, stop=(j == bb), skip_group_check=True),
                                            reads=[("pT", b), RV], writes=[("psO", ob)])

                                t = {"s_fn": s_fn, "cr": (c0, c1), "pv_fn": pv_fn}
                                if j == js[-1]:
                                    def post(ob=ob, pi=pi, r=r, qbase=qbase, c=c):
                                        av = acc[0:65, qbase:qbase + r * 511 + 1:r]
                                        ares = [("acc", k) for k in range(r * c, r * c + r)]
                                        flush(lambda t: (not t["src_ps"]) and t["qc"] in range(r * c, r * c + r))
                                        if pi == 0:
                                            sc.op("dve", lambda e: e.tensor_copy(out=av, in_=psO[ob][0:65, :]),
                                                  reads=[("psO", ob)], writes=ares)
                                        else:
                                            sc.op("dve", lambda e: e.tensor_tensor(out=av, in0=av, in1=psO[ob][0:65, :], op=ALU.add),
                                                  reads=[("psO", ob)] + ares, writes=ares)
                                    t["post"] = post
                                tiles.append(t)
                run_tiles(tiles)
                for qc in range(nqc):
                    normalize("dil", h, qc, False, 0, ji)
            ob_box[0] = ob_ctr

        def run_tiles_pairs(tiles):
            if tile_ctr[0] % 2:
                tile_ctr[0] += 1
            base = tile_ctr[0]
            n = len(tiles)
            pairs = [list(range(i, min(i + 2, n))) for i in range(0, n, 2)]

            def issue_s(p):
                for i in pairs[p]:
                    tiles[i]["s_fn"]((base + i) % 4)
            issue_s(0)
            for p in range(len(pairs)):
                if p + 1 < len(pairs):
                    issue_s(p + 1)
                for _ in pairs[p]:
                    pump()
                idx = pairs[p]
                b2 = ((base + idx[0]) % 4) // 2
                lo = tiles[idx[0]]["cr"][0]
                hi = 512 * (len(idx) - 1) + tiles[idx[-1]]["cr"][1]
                bs = [(base + i) % 4 for i in idx]
                if len(idx) == 2 and tiles[idx[1]]["cr"][0] > 0:
                    for i, b in zip(idx, bs):
                        c0, c1 = tiles[i]["cr"]
                        sc.op("act", lambda e, b=b, c0=c0, c1=c1: e.activation(out=pT[b][:, c0:c1], in_=psS[b][:, c0:c1], func=AF.Exp),
                              reads=[("psS", b)], writes=[("pT", b)])
                else:
                    sc.op("act", lambda e, b2=b2, lo=lo, hi=hi: e.activation(out=pT2[b2][:, lo:hi], in_=psS2[b2][:, lo:hi], func=AF.Exp),
                          reads=[("psS", b) for b in bs], writes=[("pT", b) for b in bs])
                for i in idx:
                    tiles[i]["pv_fn"]((base + i) % 4)
                    if tiles[i].get("post"):
                        tiles[i]["post"]()
            tile_ctr[0] = base + n

        done_box = [0]

        def notify_done():
            while done_box[0] < len(jobs) and done_box[0] < cur_job[0] + 0 and not any(t["job"] == done_box[0] for t in pending):
                if on_job_done is not None:
                    on_job_done(done_box[0])
                done_box[0] += 1

        cur_job = [0]
        _pump0 = pump

        def pump():
            _pump0()
            notify_done()

        load(0)
        for ji, (kind, h) in enumerate(jobs):
            cur_job[0] = ji
            if ji + 1 < len(jobs):
                load(ji + 1)
            do_job(ji, kind, h)
        cur_job[0] = len(jobs)
        flush()
        notify_done()
        sc.emit()
import numpy as np
import concourse.bass as bass
import concourse.mybir as mybir
from contextlib import ExitStack

F32 = mybir.dt.float32
BF = mybir.dt.bfloat16
AF = mybir.ActivationFunctionType
ALU = mybir.AluOpType
AX = mybir.AxisListType
S = 8192
_UID = [0]


def build_outproj(nc, sc, yT, wout, xres, ident, gvec, out_main, out_xT, final, ntok=4096):
    with ExitStack() as es:
        _UID[0] += 1
        _u = _UID[0]
        sb = lambda n, s, d: es.enter_context(nc.sbuf_tensor(f"{n}_u{_u}", s, d))
        ps = lambda n, s, d: es.enter_context(nc.psum_tensor(f"{n}_u{_u}", s, d))
        Wo = sb("Wo", [128, 16, 1024], BF)
        wst = [sb(f"wst{i}", [128, 1024], F32) for i in range(2)]
        yt = [sb(f"yt{i}", [128, 16, 512], BF) for i in range(2)]
        xt = [sb(f"xt{i}", [128, 1024], F32) for i in range(2)]
        x1 = [sb(f"x1{i}", [128, 1024], F32) for i in range(2)]
        junk = sb("junkb", [128, 1024], F32)
        ss = sb("ssb", [128, 8], F32)
        eps_t = sb("epsb", [128, 1], F32)
        id_t = sb("idb", [128, 128], BF)
        xn = [sb(f"xnb{i}", [128, 1024], BF) for i in range(2)]
        xTs = [sb(f"xTs{i}", [128, 8, 512], BF) for i in range(2)]
        gft = sb("gft", [128, 1024], F32)
        psA = [ps(f"psA{i}", [128, 512], F32) for i in range(4)]
        psT = [ps(f"psTb{i}", [128, 8, 128], BF) for i in range(2)]
        sc.dma("sp", lambda e: e.dma_start(out=id_t[:], in_=ident), writes=["id_t"])
        sc.op("dve", lambda e: e.memset(eps_t[:], 1e-6), writes=["eps_t"])
        if final:
            gsrc = bass.AP(gvec.tensor, 0, [[0, 128], [1, 1024]])
            sc.dma("sp", lambda e: e.dma_start(out=gft[:], in_=gsrc), writes=["gft"])
        for kc in range(16):
            s = kc % 2
            sc.dma("sp" if s == 0 else "pool", lambda e, kc=kc, s=s: e.dma_start(out=wst[s][:], in_=wout[kc * 128:(kc + 1) * 128, :]),
                   writes=[("wst", s)])
            sc.op("dve" if s == 0 else "pool", lambda e, kc=kc, s=s: e.tensor_copy(out=Wo[:, kc, :], in_=wst[s][:]),
                  reads=[("wst", s)], writes=[("Wo", kc)])
        ti = 0
        for tb in range(ntok // 512):
            s = tb % 2
            ysrc = yT[:, tb * 512:(tb + 1) * 512].rearrange("(kc p) t -> p kc t", p=128)
            sc.dma("sp", lambda e, s=s, ysrc=ysrc: e.dma_start(out=yt[s][:], in_=ysrc), writes=[("yt", s)])
            for j in range(4):
                xs = ti % 2
                col = ti % 8
                ti += 1
                r0 = tb * 512 + j * 128
                sc.dma("pool", lambda e, xs=xs, r0=r0: e.dma_start(out=xt[xs][:], in_=xres[r0:r0 + 128, :]), writes=[("xt", xs)])
                for half in range(2):
                    pb = (xs * 2 + half)
                    for kc in range(16):
                        sc.op("pe", lambda e, pb=pb, kc=kc, s=s, j=j, half=half: e.matmul(
                            psA[pb][:], lhsT=yt[s][:, kc, j * 128:(j + 1) * 128], rhs=Wo[:, kc, half * 512:(half + 1) * 512],
                            start=(kc == 0), stop=(kc == 15)),
                            reads=[("yt", s), ("Wo", kc)], writes=[("psA", pb)])
                    sc.op("dve", lambda e, pb=pb, xs=xs, half=half: e.tensor_tensor(
                        out=x1[xs][:, half * 512:(half + 1) * 512], in0=xt[xs][:, half * 512:(half + 1) * 512], in1=psA[pb][:], op=ALU.add),
                        reads=[("psA", pb), ("xt", xs)], writes=[("x1", xs)])
                if not final:
                    sc.dma("sp", lambda e, xs=xs, r0=r0: e.dma_start(out=out_main[r0:r0 + 128, :], in_=x1[xs][:]), reads=[("x1", xs)])
                sc.op("act", lambda e, xs=xs: e.activation(out=junk[:], in_=x1[xs][:], func=AF.Square), reads=[("x1", xs)], writes=["junk"])
                sc.op("dve", lambda e, col=col: e.reduce_sum(out=ss[:, col:col + 1], in_=junk[:], axis=AX.X), reads=["junk"], writes=[("ss", col)])
                sc.op("act", lambda e, col=col: e.activation(out=ss[:, col:col + 1], in_=ss[:, col:col + 1], func=AF.Sqrt,
                                                             bias=eps_t[:, 0:1], scale=1.0 / 1024.0),
                      reads=[("ss", col), "eps_t"], writes=[("ss", col)])
                sc.op("dve", lambda e, col=col: e.reciprocal(out=ss[:, col:col + 1], in_=ss[:, col:col + 1]), reads=[("ss", col)], writes=[("ss", col)])
                if final:
                    sc.op("dve", lambda e, xs=xs, col=col: e.scalar_tensor_tensor(
                        out=xt[xs][:], in0=x1[xs][:], scalar=ss[:, col:col + 1], in1=gft[:], op0=ALU.mult, op1=ALU.mult),
                        reads=[("x1", xs), ("ss", col), "gft"], writes=[("xt", xs)])
                    sc.dma("sp", lambda e, xs=xs, r0=r0: e.dma_start(out=out_main[r0:r0 + 128, :], in_=xt[xs][:]), reads=[("xt", xs)])
                else:
                    sc.op("dve", lambda e, xs=xs, col=col: e.tensor_scalar(out=xn[xs][:], in0=x1[xs][:], scalar1=ss[:, col:col + 1],
                                                                         scalar2=None, op0=ALU.mult),
                          reads=[("x1", xs), ("ss", col)], writes=[("xn", xs)])
                    for kc in range(8):
                        sc.op("pe", lambda e, xs=xs, kc=kc: e.transpose(out=psT[xs][:, kc, :], in_=xn[xs][:, kc * 128:(kc + 1) * 128], identity=id_t[:]),
                              reads=[("xn", xs), "id_t"], writes=[("psT", xs)])
                    sc.op("act", lambda e, xs=xs, s=s, j=j: e.copy(out=xTs[s][:, :, j * 128:(j + 1) * 128], in_=psT[xs][:]),
                          reads=[("psT", xs)], writes=[("xTs", s)])
            if not final:
                dst = out_xT[:, tb * 512:(tb + 1) * 512].rearrange("(kc p) t -> p kc t", p=128)
                sc.dma("sp", lambda e, s=s, dst=dst: e.dma_start(out=dst, in_=xTs[s][:]), reads=[("xTs", s)])
        sc.emit()


NW1 = 3072


def build_retention(nc, sc, xsrc_fn, w1, gn, ident, cosT, sinT, maskR, qdT, kdec, cd, ydst_fn, ntb=16, on_block_done=None):
    with ExitStack() as es:
        _UID[0] += 1
        _u = _UID[0]
        sb = lambda n, s, d: es.enter_context(nc.sbuf_tensor(f"{n}_u{_u}", s, d))
        ps = lambda n, s, d: es.enter_context(nc.psum_tensor(f"{n}_u{_u}", s, d))
        Wb = sb("W1b", [128, 8, NW1], BF)
        wst = [sb(f"w1st{i}", [128, NW1], F32) for i in range(2)]
        gn_t = sb("gn1", [128, 8], F32)
        gn16 = sb("gn16", [128, 8], F32)
        id_t = sb("idc", [128, 128], BF)
        eps_t = sb("epsc", [128, 1], F32)
        mk = sb("mkR", [128, 2, 128], F32)
        qd = sb("qd_sb", [128, 2, 128], F32)
        kd = sb("kd_sb", [128, 2], F32)
        xT = [sb(f"xTc{i}", [128, 8, 512], BF) for i in range(2)]
        cs_t = [sb(f"cos{i}", [128, 512], F32) for i in range(2)]
        sn_t = [sb(f"sin{i}", [128, 512], F32) for i in range(2)]
        tm = [sb(f"tm{i}", [128, 4, 512], F32) for i in range(2)]
        QT = [sb(f"QT{i}", [128, 2, 2, 512], BF) for i in range(2)]
        QdT = [sb(f"QdT{i}", [128, 2, 2, 512], BF) for i in range(2)]
        KT = [sb(f"KT{i}", [128, 2, 2, 512], BF) for i in range(2)]
        Ktok = [sb(f"Ktok{i}", [128, 4, 2, 256], BF) for i in range(2)]
        Vt = [sb(f"Vc{i}", [128, 4, 2, 512], BF) for i in range(2)]
        Gt = [sb(f"Gc{i}", [128, 4, 2, 512], BF) for i in range(2)]
        St = sb("St", [128, 2, 2, 512], F32)
        Stb = [sb(f"Stb{i}", [128, 2, 2, 512], BF) for i in range(2)]
        Sm = [sb(f"Sm{i}", [128, 128], BF) for i in range(2)]
        stats = sb("stats", [128, 2, 2, 6], F32)
        mv = sb("mv", [128, 2, 2, 2], F32)
        yn = [sb(f"yn{i}", [128, 512], F32) for i in range(2)]
        y2 = [sb(f"y2{i}", [128, 1024], BF) for i in range(2)]
        y2s = [sb(f"y2s{i}", [128, 8, 128], BF) for i in range(2)]
        psF = [ps(f"pcF{i}", [128, 512], F32) for i in range(2)]
        psS = ps("pcS", [128, 2, 128], F32)
        psO = [ps(f"pcO{i}", [128, 512], F32) for i in range(2)]
        psU = [ps(f"pcU{i}", [128, 512], F32) for i in range(2)]
        psT = ps("pcT", [128, 8, 128], BF)

        for (t, src, nm) in ((gn_t, gn, "gn_t"), (id_t, ident, "id_t"), (mk, maskR, "mk"), (qd, qdT, "qd"), (kd, kdec, "kd")):
            sc.dma("sp", lambda e, t=t, src=src: e.dma_start(out=t[:], in_=src), writes=[nm])
        cdt = sb("cdt", [128, 2], F32)
        cdsrc = bass.AP(cd.tensor, 0, [[0, 128], [1, 2]])
        sc.dma("sp", lambda e: e.dma_start(out=cdt[:], in_=cdsrc), writes=["cdt"])
        sc.op("dve", lambda e: e.memset(eps_t[:], 1e-6), writes=["eps_t"])
        sc.op("dve", lambda e: e.memset(St[:], 0.0), writes=[("St", a, b) for a in range(2) for b in range(2)])
        sc.op("dve", lambda e: e.memset(Stb[0][:], 0.0), writes=[("Stb", 0, a, b) for a in range(2) for b in range(2)])
        sc.op("dve", lambda e: e.tensor_scalar(out=gn16[:], in0=gn_t[:], scalar1=1.0 / 16.0, scalar2=None, op0=ALU.mult),
              reads=["gn_t"], writes=["gn16"])
        for kc in range(8):
            s = kc % 2
            sc.dma("sp" if s == 0 else "pool", lambda e, kc=kc, s=s: e.dma_start(out=wst[s][:], in_=w1[kc * 128:(kc + 1) * 128, :]),
                   writes=[("wst", s)])
            eng = "dve" if s == 0 else "pool"
            for (c0, c1, gt, gname) in ((0, 512, gn_t, "gn_t"), (512, 1024, gn16, "gn16"), (1024, NW1, gn_t, "gn_t")):
                sc.op(eng, lambda e, kc=kc, s=s, c0=c0, c1=c1, gt=gt: e.tensor_scalar(
                    out=Wb[:, kc, c0:c1], in0=wst[s][:, c0:c1], scalar1=gt[:, kc:kc + 1], scalar2=None, op0=ALU.mult),
                    reads=[("wst", s), gname], writes=[("Wb", kc, c0)])
        WR = lambda kc: [("Wb", kc, 0), ("Wb", kc, 512), ("Wb", kc, 1024)]
        chunk_i = 0
        deferred = [None]
        kdefer = []
        def load_blk(tb):
            s = tb % 2
            t0 = tb * 512
            src = xsrc_fn(tb).rearrange("(kc p) t -> p kc t", p=128)
            sc.dma("sp", lambda e: e.dma_start(out=xT[s][:, 0:4, :], in_=src[:, 0:4, :]), writes=[("xT", s)])
            sc.dma("act", lambda e: e.dma_start(out=xT[s][:, 4:8, :], in_=src[:, 4:8, :]), writes=[("xT", s)])
            sc.dma("pool", lambda e: e.dma_start(out=cs_t[s][:], in_=cosT[:, t0:t0 + 512]), writes=[("cos", s)])
            sc.dma("pool", lambda e: e.dma_start(out=sn_t[s][:], in_=sinT[:, t0:t0 + 512]), writes=[("sin", s)])

        load_blk(0)
        for tb in range(ntb):
            s = tb % 2
            t0 = tb * 512
            if tb + 1 < ntb:
                load_blk(tb + 1)
            for gi in range(4):
                isk, h = gi // 2, gi % 2
                fb = [psF[0], psF[1]] if gi % 2 == 0 else [psO[0], psO[1]]
                fr = [("psF", 0), ("psF", 1)] if gi % 2 == 0 else [("psO", 0), ("psO", 1)]
                for eo in range(2):
                    cg = gi * 2 + eo
                    for kc in range(8):
                        sc.op("pe", lambda e, eo=eo, kc=kc, cg=cg, s=s, fb=fb: e.matmul(
                            fb[eo][:], lhsT=Wb[:, kc, cg * 128:(cg + 1) * 128], rhs=xT[s][:, kc, :], start=(kc == 0), stop=(kc == 7)),
                            reads=[("xT", s)] + WR(kc), writes=[fr[eo]])
                ts = gi % 2
                RT = ("tm", ts)
                sc.op("dve", lambda e, ts=ts, s=s, fb=fb: e.tensor_tensor(out=tm[ts][:, 0, :], in0=fb[0][:], in1=cs_t[s][:], op=ALU.mult),
                      reads=[fr[0], ("cos", s)], writes=[RT])
                sc.op("dve", lambda e, ts=ts, s=s, fb=fb: e.tensor_tensor(out=tm[ts][:, 1, :], in0=fb[1][:], in1=sn_t[s][:], op=ALU.mult),
                      reads=[fr[1], ("sin", s)], writes=[RT])
                sc.op("dve", lambda e, ts=ts, s=s, fb=fb: e.tensor_tensor(out=tm[ts][:, 2, :], in0=fb[0][:], in1=sn_t[s][:], op=ALU.mult),
                      reads=[fr[0], ("sin", s)], writes=[RT])
                sc.op("dve", lambda e, ts=ts, s=s, fb=fb: e.tensor_tensor(out=tm[ts][:, 3, :], in0=fb[1][:], in1=cs_t[s][:], op=ALU.mult),
                      reads=[fr[1], ("cos", s)], writes=[RT])
                dst = (KT if isk else QT)[s]
                RD = ("KT" if isk else "QT", s)
                sc.op("pool", lambda e, ts=ts, dst=dst, h=h: e.tensor_tensor(out=dst[:, h, 0, :], in0=tm[ts][:, 0, :], in1=tm[ts][:, 1, :], op=ALU.subtract),
                      reads=[RT], writes=[RD])
                sc.op("pool", lambda e, ts=ts, dst=dst, h=h: e.tensor_tensor(out=dst[:, h, 1, :], in0=tm[ts][:, 2, :], in1=tm[ts][:, 3, :], op=ALU.add),
                      reads=[RT], writes=[RD])
                if not isk:
                    for eo in range(2):
                        qv = QT[s][:, h, eo, :].rearrange("p (c t) -> p c t", c=4)
                        ov = QdT[s][:, h, eo, :].rearrange("p (c t) -> p c t", c=4)
                        base = qd[:, h, :]
                        dv_ = bass.AP(base.tensor, base.offset, [list(base.ap[0]), [0, 4], [1, 128]])
                        sc.op("dve", lambda e, qv=qv, ov=ov, dv_=dv_: e.tensor_tensor(out=ov, in0=qv, in1=dv_, op=ALU.mult),
                              reads=[RD, "qd"], writes=[("QdT", s)])
                else:
                    def ktrans(s=s, h=h, RD=RD):
                        for c in range(4):
                            for eo in range(2):
                                sc.op("pe", lambda e, c=c, eo=eo: e.transpose(
                                    out=psT[:, c * 2 + eo, :], in_=KT[s][:, h, eo, c * 128:(c + 1) * 128], identity=id_t[:]),
                                    reads=[RD, "id_t"], writes=["psT"])
                        kv = Ktok[s][:, :, h, :].rearrange("p c (eo i) -> p c eo i", eo=2)
                        pv = psT[:].rearrange("p (c eo) i -> p c eo i", eo=2)
                        sc.op("act", lambda e: e.activation(out=kv, in_=pv, func=AF.Copy, scale=kd[:, h:h + 1]),
                              reads=["psT", "kd"], writes=[("Ktok", s)])
                    kdefer.append(ktrans)
            for j in range(4):
                for grp in range(4):
                    pb = grp % 2
                    for kc in range(8):
                        sc.op("pe", lambda e, pb=pb, kc=kc, j=j, grp=grp, s=s: e.matmul(
                            psU[pb][:], lhsT=xT[s][:, kc, j * 128:(j + 1) * 128], rhs=Wb[:, kc, 1024 + grp * 512:1024 + (grp + 1) * 512],
                            start=(kc == 0), stop=(kc == 7)),
                            reads=[("xT", s)] + WR(kc), writes=[("psU", pb)])
                    if grp < 2:
                        sc.op("act", lambda e, pb=pb, j=j, grp=grp, s=s: e.copy(out=Vt[s][:, j, grp, :], in_=psU[pb][:]),
                              reads=[("psU", pb)], writes=[("Vt", s)])
                    else:
                        sc.op("act", lambda e, pb=pb, j=j, grp=grp, s=s: e.activation(out=Gt[s][:, j, grp - 2, :], in_=psU[pb][:], func=AF.Silu),
                              reads=[("psU", pb)], writes=[("Gt", s)])
            for f in kdefer:
                f()
            kdefer.clear()
            for c in range(4):
                cur = chunk_i % 2
                nxt = 1 - cur
                ys = chunk_i % 2
                cc = slice(c * 128, (c + 1) * 128)
                oset = chunk_i % 2
                ob = [psO[0], psO[1]] if oset == 0 else [psF[0], psF[1]]
                orr = [("psO", 0), ("psO", 1)] if oset == 0 else [("psF", 0), ("psF", 1)]
                for h in range(2):
                    for eo in range(2):
                        sc.op("pe", lambda e, h=h, eo=eo, s=s, cc=cc: e.matmul(
                            psS[:, h, :], lhsT=KT[s][:, h, eo, cc], rhs=QT[s][:, h, eo, cc], start=(eo == 0), stop=(eo == 1)),
                            reads=[("KT", s), ("QT", s)], writes=[("psS", h)])
                    sc.op("dve", lambda e, h=h: e.tensor_tensor(out=Sm[h][:], in0=psS[:, h, :], in1=mk[:, h, :], op=ALU.mult),
                          reads=[("psS", h), "mk"], writes=[("Sm", h)])
                    sc.op("pe", lambda e, h=h, s=s, c=c, ob=ob: e.matmul(ob[h][:], lhsT=Sm[h][:], rhs=Vt[s][:, c, h, :], start=True, stop=False),
                          reads=[("Sm", h), ("Vt", s)], writes=[orr[h]])
                    for eo in range(2):
                        sc.op("pe", lambda e, h=h, eo=eo, s=s, cc=cc, cur=cur, ob=ob: e.matmul(
                            ob[h][:], lhsT=QdT[s][:, h, eo, cc], rhs=Stb[cur][:, eo, h, :], start=False, stop=(eo == 1)),
                            reads=[("QdT", s), ("Stb", cur, eo, h)], writes=[orr[h]])
                    for half in range(2):
                        sc.op("pe", lambda e, h=h, half=half, s=s, c=c: e.matmul(
                            psU[half][:], lhsT=Ktok[s][:, c, h, half * 128:(half + 1) * 128], rhs=Vt[s][:, c, h, :], start=True, stop=True),
                            reads=[("Ktok", s), ("Vt", s)], writes=[("psU", half)])
                        sc.op("dve", lambda e, h=h, half=half: e.scalar_tensor_tensor(
                            out=St[:, half, h, :], in0=St[:, half, h, :], scalar=cdt[:, h:h + 1], in1=psU[half][:], op0=ALU.mult, op1=ALU.add),
                            reads=[("St", half, h), ("psU", half), "cdt"], writes=[("St", half, h)])
                        sc.op("act", lambda e, h=h, half=half, nxt=nxt: e.copy(out=Stb[nxt][:, half, h, :], in_=St[:, half, h, :]),
                              reads=[("St", half, h)], writes=[("Stb", nxt, half, h)])
                    sr = ("stats", ys, h)
                    sc.op("dve", lambda e, h=h, ob=ob, ys=ys: e.bn_stats(out=stats[:, ys, h, :], in_=ob[h][:]), reads=[orr[h]], writes=[sr])
                    sc.op("dve", lambda e, h=h, ys=ys: e.bn_aggr(out=mv[:, ys, h, :], in_=stats[:, ys, h, :]), reads=[sr], writes=[("mv", ys, h)])
                    sc.op("act", lambda e, h=h, ys=ys: e.activation(out=mv[:, ys, h, 1:2], in_=mv[:, ys, h, 1:2], func=AF.Sqrt, bias=eps_t[:, 0:1], scale=1.0),
                          reads=[("mv", ys, h), "eps_t"], writes=[("mv", ys, h)])
                    sc.op("dve", lambda e, h=h, ys=ys: e.reciprocal(out=mv[:, ys, h, 1:2], in_=mv[:, ys, h, 1:2]), reads=[("mv", ys, h)], writes=[("mv", ys, h)])
                    sc.op("dve", lambda e, h=h, ob=ob, ys=ys: e.tensor_scalar(out=yn[h][:], in0=ob[h][:], scalar1=mv[:, ys, h, 0:1], scalar2=mv[:, ys, h, 1:2],
                                                                        op0=ALU.subtract, op1=ALU.mult),
                          reads=[orr[h], ("mv", ys, h)], writes=[("yn", h)])
                    sc.op("pool", lambda e, h=h, ys=ys, s=s, c=c: e.tensor_tensor(out=y2[ys][:, h * 512:(h + 1) * 512], in0=yn[h][:], in1=Gt[s][:, c, h, :], op=ALU.mult),
                          reads=[("yn", h), ("Gt", s)], writes=[("y2", ys, h)])

                def epilogue(ys=ys, tb=tb, c=c):
                    for kc in range(8):
                        sc.op("pe", lambda e, kc=kc: e.transpose(out=psT[:, kc, :], in_=y2[ys][:, kc * 128:(kc + 1) * 128], identity=id_t[:]),
                              reads=[("y2", ys, kc // 4), "id_t"], writes=["psT"])
                    sc.op("act", lambda e: e.copy(out=y2s[ys][:], in_=psT[:]), reads=["psT"], writes=[("y2s", ys)])
                    dst = ydst_fn(tb, c).rearrange("(kc p) t -> p kc t", p=128)
                    sc.dma("sp", lambda e: e.dma_start(out=dst, in_=y2s[ys][:]), reads=[("y2s", ys)], writes=[("y2d", tb)])
                    if c == 3 and on_block_done is not None:
                        on_block_done(tb)

                if deferred[0] is not None:
                    deferred[0]()
                deferred[0] = epilogue
                chunk_i += 1

        if deferred[0] is not None:
            deferred[0]()
        sc.emit()
import numpy as np
import concourse.bass as bass
import concourse.mybir as mybir
from contextlib import ExitStack

F32 = mybir.dt.float32
BF = mybir.dt.bfloat16
AF = mybir.ActivationFunctionType
ALU = mybir.AluOpType
AX = mybir.AxisListType
S = 8192
_UID = [0]


def build_outproj_p1(nc, sc, ysrc_fn, wout, xres, xout, ssloc):
    with ExitStack() as es:
        _UID[0] += 1
        _u = _UID[0]
        sb = lambda n, s, d: es.enter_context(nc.sbuf_tensor(f"{n}_u{_u}", s, d))
        ps = lambda n, s, d: es.enter_context(nc.psum_tensor(f"{n}_u{_u}", s, d))
        Wo = sb("Wo", [128, 16, 512], BF)
        wst = [sb(f"wst{i}", [128, 512], F32) for i in range(2)]
        yt = [sb(f"yt{i}", [128, 16, 512], BF) for i in range(2)]
        xt = [sb(f"xt{i}", [128, 512], F32) for i in range(2)]
        x1 = [sb(f"x1{i}", [128, 512], F32) for i in range(2)]
        junk = sb("junkb", [128, 512], F32)
        ssp = sb("ssp", [128, 64], F32)
        psA = [ps(f"psA{i}", [128, 512], F32) for i in range(4)]
        for kc in range(16):
            s = kc % 2
            sc.dma("sp" if s == 0 else "pool", lambda e, kc=kc, s=s: e.dma_start(out=wst[s][:], in_=wout[kc * 128:(kc + 1) * 128, :]),
                   writes=[("wst", s)])
            sc.op("dve" if s == 0 else "pool", lambda e, kc=kc, s=s: e.tensor_copy(out=Wo[:, kc, :], in_=wst[s][:]),
                  reads=[("wst", s)], writes=[("Wo", kc)])
        ti = 0

        def load_y(tb):
            s = tb % 2
            ysrc = ysrc_fn(tb).rearrange("(kc p) t -> p kc t", p=128)
            for q4, qn in enumerate(("sp", "act", "sp", "act")):
                sc.dma(qn, lambda e, q4=q4: e.dma_start(out=yt[s][:, q4 * 4:(q4 + 1) * 4, :], in_=ysrc[:, q4 * 4:(q4 + 1) * 4, :]),
                       writes=[("yt", s, q4)])

        load_y(0)
        for tb in range(S // 512):
            s = tb % 2
            if tb + 1 < S // 512:
                load_y(tb + 1)
            for j in range(4):
                xs = ti % 2
                pb = ti % 4
                tile = ti
                ti += 1
                r0 = tb * 512 + j * 128
                sc.dma("pool", lambda e, xs=xs, r0=r0: e.dma_start(out=xt[xs][:], in_=xres[r0:r0 + 128, :]), writes=[("xt", xs)])
                for kc in range(16):
                    sc.op("pe", lambda e, pb=pb, kc=kc, s=s, j=j: e.matmul(
                        psA[pb][:], lhsT=yt[s][:, kc, j * 128:(j + 1) * 128], rhs=Wo[:, kc, :], start=(kc == 0), stop=(kc == 15)),
                        reads=[("yt", s, kc // 4), ("Wo", kc)], writes=[("psA", pb)])
                sc.op("dve", lambda e, pb=pb, xs=xs: e.tensor_tensor(out=x1[xs][:], in0=xt[xs][:], in1=psA[pb][:], op=ALU.add),
                      reads=[("psA", pb), ("xt", xs)], writes=[("x1", xs)])
                sc.dma("sp", lambda e, xs=xs, r0=r0: e.dma_start(out=xout[r0:r0 + 128, :], in_=x1[xs][:]), reads=[("x1", xs)])
                sc.op("act", lambda e, xs=xs: e.activation(out=junk[:], in_=x1[xs][:], func=AF.Square), reads=[("x1", xs)], writes=["junk"])
                sc.op("dve", lambda e, tile=tile: e.reduce_sum(out=ssp[:, tile:tile + 1], in_=junk[:], axis=AX.X),
                      reads=["junk"], writes=["ssp"])
        sc.dma("sp", lambda e: e.dma_start(out=ssloc, in_=ssp[:]), reads=["ssp"])
        sc.emit()


def build_outproj_p2(nc, sc, xin, ssall, ident, gvec, out_main, xdst_fn, final, on_block_done=None):
    with ExitStack() as es:
        _UID[0] += 1
        _u = _UID[0]
        sb = lambda n, s, d: es.enter_context(nc.sbuf_tensor(f"{n}_u{_u}", s, d))
        ps = lambda n, s, d: es.enter_context(nc.psum_tensor(f"{n}_u{_u}", s, d))
        ssa = sb("ssa", [128, 2, 64], F32)
        rstd = sb("rstd", [128, 64], F32)
        eps_t = sb("epsb", [128, 1], F32)
        id_t = sb("idb", [128, 128], BF)
        gft = sb("gft", [128, 512], F32)
        xt = [sb(f"xq{i}", [128, 512], F32) for i in range(4)]
        xo = [sb(f"xo{i}", [128, 512], F32) for i in range(4)]
        xn = [sb(f"xnb{i}", [128, 512], BF) for i in range(4)]
        xTs = [sb(f"xTs{i}", [128, 4, 512], BF) for i in range(2)]
        psT = [ps(f"psTb{i}", [128, 4, 128], BF) for i in range(4)]
        sc.dma("sp", lambda e: e.dma_start(out=id_t[:], in_=ident), writes=["id_t"])
        sc.op("dve", lambda e: e.memset(eps_t[:], 1e-6), writes=["eps_t"])
        sc.dma("sp", lambda e: e.dma_start(out=ssa[:], in_=ssall.rearrange("(r p) t -> p r t", p=128)), reads=["ssall"], writes=["ssa"])
        if final:
            gsrc = bass.AP(gvec.tensor, 0, [[0, 128], [1, 512]])
            sc.dma("sp", lambda e: e.dma_start(out=gft[:], in_=gsrc), writes=["gft"])
        sc.op("dve", lambda e: e.tensor_tensor(out=rstd[:], in0=ssa[:, 0, :], in1=ssa[:, 1, :], op=ALU.add), reads=["ssa"], writes=["rstd"])
        sc.op("act", lambda e: e.activation(out=rstd[:], in_=rstd[:], func=AF.Sqrt, bias=eps_t[:, 0:1], scale=1.0 / 1024.0),
              reads=["rstd", "eps_t"], writes=["rstd"])
        sc.op("dve", lambda e: e.reciprocal(out=rstd[:], in_=rstd[:]), reads=["rstd"], writes=["rstd"])
        ti = 0
        for tb in range(S // 512):
            s = tb % 2
            for j in range(4):
                xs = ti % 4
                tile = ti
                ti += 1
                r0 = tb * 512 + j * 128
                sc.dma("sp" if xs % 2 == 0 else "pool", lambda e, xs=xs, r0=r0: e.dma_start(out=xt[xs][:], in_=xin[r0:r0 + 128, :]), writes=[("xt", xs)])
                if final:
                    sc.op("dve", lambda e, xs=xs, tile=tile: e.scalar_tensor_tensor(
                        out=xo[xs][:], in0=xt[xs][:], scalar=rstd[:, tile:tile + 1], in1=gft[:], op0=ALU.mult, op1=ALU.mult),
                        reads=[("xt", xs), "rstd", "gft"], writes=[("xo", xs)])
                    sc.dma("sp", lambda e, xs=xs, r0=r0: e.dma_start(out=out_main[r0:r0 + 128, :], in_=xo[xs][:]), reads=[("xo", xs)])
                else:
                    sc.op("dve", lambda e, xs=xs, tile=tile: e.tensor_scalar(out=xn[xs][:], in0=xt[xs][:], scalar1=rstd[:, tile:tile + 1],
                                                                           scalar2=None, op0=ALU.mult),
                          reads=[("xt", xs), "rstd"], writes=[("xn", xs)])
                    for kc in range(4):
                        sc.op("pe", lambda e, xs=xs, kc=kc: e.transpose(out=psT[xs][:, kc, :], in_=xn[xs][:, kc * 128:(kc + 1) * 128], identity=id_t[:]),
                              reads=[("xn", xs), "id_t"], writes=[("psT", xs)])
                    sc.op("act", lambda e, xs=xs, s=s, j=j: e.copy(out=xTs[s][:, :, j * 128:(j + 1) * 128], in_=psT[xs][:]),
                          reads=[("psT", xs)], writes=[("xTs", s)])
            if not final:
                dst = xdst_fn(tb).rearrange("(kc p) t -> p kc t", p=128)
                sc.dma("sp", lambda e, s=s, dst=dst: e.dma_start(out=dst, in_=xTs[s][:]), reads=[("xTs", s)], writes=[("xnd", tb)])
                if on_block_done is not None:
                    on_block_done(tb)
        sc.emit()
import ml_dtypes
import numpy as np, ml_dtypes
bf = ml_dtypes.bfloat16
S = 8192
def consts_common():
    kk = np.arange(128)[:, None]; qq = np.arange(128)[None, :]
    maskD = np.concatenate([np.where(qq >= kk, 0.0, -30000.0), np.where(kk >= qq, 0.0, -30000.0)], axis=1).astype(bf)
    return {"ident": np.eye(128, dtype=bf), "maskD": maskD}
def rot_tables():
    inv = (1.0 / (np.float32(10000.0) ** np.linspace(0.0, 1.0, 128, dtype=np.float32))).astype(np.float32)
    ang = (np.arange(S, dtype=np.float32)[:, None] * inv[None, :]).astype(np.float32)
    return np.ascontiguousarray(np.cos(ang.astype(np.float64)).T.astype(np.float32)), np.ascontiguousarray(np.sin(ang.astype(np.float64)).T.astype(np.float32))
def decay_tables(hp):
    Hs = [2 * hp, 2 * hp + 1]
    lg = [float(np.log1p(-np.float32(2.0) ** np.float32(-5.0 - H))) for H in Hs]
    pos = np.arange(128, dtype=np.float64)
    maskR = np.zeros((128, 2, 128), np.float32); qdT = np.zeros((128, 2, 128), np.float32); kdec = np.zeros((128, 2), np.float32); cd = []
    for i, l in enumerate(lg):
        rel = pos[None, :] - pos[:, None]
        maskR[:, i, :] = np.where(rel >= 0, np.exp(l * np.maximum(rel, 0)), 0.0)
        qdT[:, i, :] = np.exp(l * (pos + 1.0))[None, :]
        kdec[:, i] = np.exp(l * (127.0 - pos))
        cd.append(float(np.exp(l * 128.0)))
    return maskR, qdT, kdec, cd
def w1_core(Wodd, hp):
    Hs = [2 * hp, 2 * hp + 1]
    cols = []
    for base in (0, 1024):
        for H in Hs:
            cols.append(Wodd[:, base + H * 256: base + (H + 1) * 256][:, 0::2])
            cols.append(Wodd[:, base + H * 256: base + (H + 1) * 256][:, 1::2])
    for base in (2048, 4096):
        for H in Hs:
            cols.append(Wodd[:, base + H * 512: base + (H + 1) * 512])
    return np.ascontiguousarray(np.concatenate(cols, axis=1))


from concourse.bass_utils import run_bass_kernel_spmd

NCORES = 8
GROUPS = [[0, 1], [2, 3], [4, 5], [6, 7]]


def _dt(a):
    return BF if a.dtype == bf else F32


def _core_inputs(inputs, core):
    cc = consts_common()
    b, r = core // 2, core % 2
    W = inputs["even_w_in"][0]
    hs = slice(r * 512, (r + 1) * 512)
    o = 4112
    parts = [W[:, 0:1024][:, hs], W[:, 1024:2048][:, hs], W[:, 3072:4096][:, hs],
             W[:, o:o + 1024][:, hs], W[:, o + 1024:o + 2048][:, hs], W[:, o + 3072:o + 4096][:, hs],
             W[:, 2048:3072][:, hs], W[:, o + 2048:o + 3072][:, hs], W[:, 4096 + r * 8:4096 + (r + 1) * 8]]
    perm = []
    for job in range(16):
        for rk in range(2):
            base = (rk * 8 + job) * 64 if job < 8 else 1024 + (rk * 8 + job - 8) * 64
            perm.extend(range(base, base + 64))
    perm = np.array(perm)
    cosT, sinT = rot_tables()
    maskR, qdT, kdec, cd = decay_tables(r)
    x = inputs["x"][b]
    return {"x": np.ascontiguousarray(x),
            "w": np.ascontiguousarray(np.concatenate(parts, axis=1)),
            "gn": np.ascontiguousarray(inputs["even_norm"][0].reshape(8, 128).T),
            "bfv": np.ascontiguousarray(inputs["even_b_f"][0][r * 8:(r + 1) * 8].reshape(8, 1)),
            "ident": cc["ident"], "maskD": cc["maskD"],
            "wo0": np.ascontiguousarray(inputs["even_w_out"][0][perm][:, hs]),
            "xres0": np.ascontiguousarray(x[:, hs]),
            "w1": w1_core(inputs["odd_w_in"][0], r),
            "gn1": np.ascontiguousarray(inputs["odd_norm"][0].reshape(8, 128).T),
            "cosT": cosT, "sinT": sinT, "maskR": maskR, "qdT": qdT, "kdec": kdec,
            "cdv": np.array(cd, np.float32).reshape(1, 2),
            "wo1": np.ascontiguousarray(inputs["odd_w_out"][0][:, hs]),
            "gfin": np.ascontiguousarray(inputs["final_norm"][hs].reshape(1, 512))}


def _build(sample):
    nc = bass.Bass("TRN2", target_bir_lowering=False)
    d = {k: nc.dram_tensor(k, list(v.shape), _dt(v), kind="ExternalInput").ap() for k, v in sample.items()}
    out = nc.dram_tensor("out", [S, 512], F32, kind="ExternalOutput").ap()
    scr = lambda n, shp, dt: nc.dram_tensor(n, shp, dt).ap()
    featT = scr("featT", [NFEAT, S], BF)
    vtok = scr("vtok", [S, 1024], BF)
    caug = scr("caug", [8, 6, S], BF)
    yT = scr("yT", [1024, S], BF)
    yAll = scr("yAll", [2048, S], BF)
    x1c = scr("x1c", [S, 512], F32)
    ssl1 = scr("ssl1", [128, 64], F32)
    ssa1 = scr("ssa1", [256, 64], F32)
    xnblk = scr("xnblk", [16, 512, 512], BF)
    xnAllb = scr("xnAllb", [16, 1024, 512], BF)
    y2blk = scr("y2blk", [16, 1024, 512], BF)
    y2Allb = scr("y2Allb", [16, 2048, 512], BF)
    x2c = scr("x2c", [S, 512], F32)
    ssl2 = scr("ssl2", [128, 64], F32)
    ssa2 = scr("ssa2", [256, 64], F32)
    sc = Sched(nc)
    build_proj_phase(nc, sc, d["x"], d["w"], d["gn"], d["bfv"], d["ident"], featT, vtok, caug)
    build_attn_phase(nc, sc, featT, vtok, caug, d["ident"], d["maskD"], yT,
                     on_job_done=lambda job: sc.coll(yT[job * 64:(job + 1) * 64, :], yAll[job * 128:(job + 1) * 128, :], GROUPS,
                                                     reads=[("yTd", job)]))
    build_outproj_p1(nc, sc, lambda tb: yAll[:, tb * 512:(tb + 1) * 512], d["wo0"], d["xres0"], x1c, ssl1)
    sc.coll(ssl1, ssa1, GROUPS, writes=["ssall"])
    build_outproj_p2(nc, sc, x1c, ssa1, d["ident"], None, None, lambda tb: xnblk[tb], False,
                     on_block_done=lambda tb: sc.coll(xnblk[tb], xnAllb[tb], GROUPS, reads=[("xnd", tb)]))
    build_retention(nc, sc, lambda tb: xnAllb[tb], d["w1"], d["gn1"], d["ident"], d["cosT"], d["sinT"], d["maskR"], d["qdT"], d["kdec"],
                    d["cdv"], lambda tb, c: y2blk[tb][:, c * 128:(c + 1) * 128],
                    on_block_done=lambda tb: sc.coll(y2blk[tb], y2Allb[tb], GROUPS, reads=[("y2d", tb)]))
    build_outproj_p1(nc, sc, lambda tb: y2Allb[tb], d["wo1"], x1c, x2c, ssl2)
    sc.coll(ssl2, ssa2, GROUPS, writes=["ssall"])
    build_outproj_p2(nc, sc, x2c, ssa2, d["ident"], d["gfin"], out, None, True)
    return nc


def kernel(**inputs):
    inputs = {k: np.asarray(v) for k, v in inputs.items()}
    maps = [_core_inputs(inputs, c) for c in range(NCORES)]
    nc = _build(maps[0])
    res = run_bass_kernel_spmd(nc, maps, core_ids=list(range(NCORES))).results
    out = np.empty((4, S, 1024), np.float32)
    for core in range(NCORES):
        b, r = core // 2, core % 2
        out[b, :, r * 512:(r + 1) * 512] = np.asarray(res[core]["out"])
    return out
```

```python
import concourse.bass as bass
import concourse.mybir as mybir

ENGS = ("pe", "act", "dve", "pool", "sp")
DMA_POOL = 12


class Sched:
    def __init__(self, nc, same_engine_sync=True):
        self.nc = nc
        self.q = {e: [] for e in ENGS}
        self.cnt = {e: 0 for e in ENGS}
        self.sem = {e: nc.alloc_semaphore(f"sq_{e}") for e in ENGS if e != "sp"}
        self.dsem = {}
        for e in ("sp", "pool", "act"):
            self.dsem[e] = [nc.alloc_semaphore(f"sd_{e}{i}") for i in range(DMA_POOL)]
        self.dcnt = {e: [0] * DMA_POOL for e in self.dsem}
        self.drr = {e: 0 for e in self.dsem}
        self.waited = {e: {} for e in ENGS}
        self.res = {}
        self.semobj = {}
        self.same = same_engine_sync
        self.nwaits = 0
        self.clear_sems()

    def all_sems(self):
        return list(self.sem.values()) + [s for e in self.dsem for s in self.dsem[e]]

    def clear_sems(self):
        nc = self.nc
        sems = self.all_sems()
        with nc.Block() as block:
            def body(g):
                for s in sems:
                    g.sem_clear(s)
            block.gpsimd(body)

    def _deps(self, eng, reads, writes):
        deps = {}

        def add(tok):
            if tok is None:
                return
            k, v = tok
            if deps.get(k, 0) < v:
                deps[k] = v

        for r in reads:
            st = self.res.get(r)
            if st is not None:
                add(st["w"])
        for w in writes:
            st = self.res.get(w)
            if st is not None:
                add(st["w"])
                for k, v in st["r"].items():
                    add((k, v))
        return deps

    def _commit(self, tok, reads, writes):
        for r in reads:
            st = self.res.setdefault(r, {"w": None, "r": {}})
            k, v = tok
            if st["r"].get(k, 0) < v:
                st["r"][k] = v
        for w in writes:
            self.res[w] = {"w": tok, "r": {}}

    def _waits(self, eng, deps, own_key):
        ws = []
        for k, v in deps.items():
            if k == own_key and (eng == "pe" or not self.same):
                continue
            if self.waited[eng].get(k, 0) >= v:
                continue
            self.waited[eng][k] = v
            ws.append((k, v))
        self.nwaits += len(ws)
        return ws

    def op(self, eng, fn, reads=(), writes=()):
        deps = self._deps(eng, reads, writes)
        sem = self.sem[eng]
        key = "q_" + eng
        self.semobj[key] = sem
        ws = self._waits(eng, deps, key)
        self.cnt[eng] += 1
        tok = (key, self.cnt[eng])
        self.q[eng].append((fn, ws, (sem, 1, self.cnt[eng])))
        self._commit(tok, reads, writes)
        return tok

    def dma(self, eng, fn, reads=(), writes=()):
        deps = self._deps(eng, reads, writes)
        i = self.drr[eng]
        self.drr[eng] = (i + 1) % DMA_POOL
        sem = self.dsem[eng][i]
        key = f"d_{eng}{i}"
        self.semobj[key] = sem
        prev = self.dcnt[eng][i]
        if prev > 0:
            deps[key] = max(deps.get(key, 0), 16 * prev)
        ws = self._waits(eng, deps, None)
        self.dcnt[eng][i] += 1
        tok = (key, 16 * self.dcnt[eng][i])
        self.q[eng].append((fn, ws, (sem, 16)))
        self._commit(tok, reads, writes)
        return tok

    def coll(self, src_ap, dst_ap, groups, reads=(), writes=()):
        nc = self.nc
        if not hasattr(self, "cc_toks"):
            self.cc_toks = []
        n = len(self.cc_toks)
        sem = nc.alloc_semaphore(f"ccs{n}")
        key = f"cc{n}"
        self.semobj[key] = sem
        deps = self._deps("pool", reads, writes)
        if self.cc_toks:
            pk, pv = self.cc_toks[-1]
            deps[pk] = max(deps.get(pk, 0), pv)
        ws = self._waits("pool", deps, None)
        tok = (key, 1)
        fn = lambda g: g.collective_compute("AllGather", mybir.AluOpType.bypass, replica_groups=groups,
                                            ins=[src_ap.opt()], outs=[dst_ap.opt()])
        self.q["pool"].append((fn, ws, (sem, None)))
        self._commit(tok, reads, writes)
        self.cc_toks.append(tok)
        return tok

    def barrier(self):
        toks = {}
        for e in ENGS:
            if e != "sp" and self.cnt[e] > 0:
                toks["q_" + e] = self.cnt[e]
        for e in self.dsem:
            for i in range(DMA_POOL):
                if self.dcnt[e][i] > 0:
                    toks[f"d_{e}{i}"] = 16 * self.dcnt[e][i]
        for k, v in getattr(self, "cc_toks", []):
            toks[k] = v
        for e in ENGS:
            ws = []
            for k, v in toks.items():
                if self.waited[e].get(k, 0) >= v:
                    continue
                self.waited[e][k] = v
                ws.append((k, v))
            if ws:
                self.q[e].append((None, ws, None))

    def emit(self):
        nc = self.nc
        self.barrier()
        if not any(self.q[e] for e in ENGS):
            return
        handles = {"pe": "tensor", "act": "scalar", "dve": "vector", "pool": "gpsimd", "sp": "sync"}
        if not hasattr(self, "base"):
            self.base = {}
        needed = {}
        for e in ENGS:
            for fn, ws, inc in self.q[e]:
                for k, v in ws:
                    if k.startswith("q_"):
                        needed.setdefault(k, set()).add(v)
        valmap = {}
        for k, idxs in needed.items():
            b = self.base.get(k, 0)
            for r, idx in enumerate(sorted(idxs)):
                valmap[(k, idx)] = b + r + 1
            self.base[k] = b + len(idxs)
        with nc.Block() as block:
            for e in ENGS:
                items = self.q[e]
                if not items:
                    continue

                def body(engh, items=items, e=e):
                    for fn, ws, inc in items:
                        for k, v in ws:
                            if k.startswith("q_"):
                                engh.wait_ge(self.semobj[k], valmap[(k, v)])
                            else:
                                engh.wait_ge(self.semobj[k], v)
                        if fn is not None:
                            inst = fn(engh)
                            if len(inc) == 3:
                                if ("q_" + e, inc[2]) in valmap:
                                    inst.then_inc(inc[0], 1)
                            elif inc[1] is None:
                                inst.then_inc(inc[0])
                            else:
                                inst.then_inc(inc[0], inc[1])

                getattr(block, handles[e])(body)
        self.q = {e: [] for e in ENGS}
        self.res = {}
import numpy as np
import concourse.bass as bass
import concourse.mybir as mybir
from contextlib import ExitStack

F32 = mybir.dt.float32
BF = mybir.dt.bfloat16
AF = mybir.ActivationFunctionType
ALU = mybir.AluOpType
AX = mybir.AxisListType

S = 8192
_UID = [0]
NTB = S // 512
NFEAT = 3072
NV = 1024
NW = NFEAT + NV + 8
MASKNEG = -30000.0


def build_proj_phase(nc, sc, x, w, gn, bfv, ident, featT, vtok, caug, ntb=NTB):
    with ExitStack() as es:
        Wb = es.enter_context(nc.sbuf_tensor("Wb_pa", [128, 8, NW], BF))
        gn_t = es.enter_context(nc.sbuf_tensor("gn_t_pa", [128, 8], F32))
        id_t = es.enter_context(nc.sbuf_tensor("id_t_pa", [128, 128], BF))
        eps_t = es.enter_context(nc.sbuf_tensor("eps_t_pa", [128, 1], F32))
        one_t = es.enter_context(nc.sbuf_tensor("one_t_pa", [128, 1], F32))
        nb_t = es.enter_context(nc.sbuf_tensor("nb_t_pa", [8, 1], F32))
        ones8 = es.enter_context(nc.sbuf_tensor("ones8_pa", [8, 512], F32))
        xb0 = es.enter_context(nc.sbuf_tensor("xb0_pa", [128, NW], F32))
        xb1 = es.enter_context(nc.sbuf_tensor("xb1_pa", [128, NW], F32))
        junk = es.enter_context(nc.sbuf_tensor("junk_pa", [128, 1024], F32))
        ss = es.enter_context(nc.sbuf_tensor("ss_pa", [128, 8], F32))
        xn0 = es.enter_context(nc.sbuf_tensor("xn0_pa", [128, 1024], BF))
        xn1 = es.enter_context(nc.sbuf_tensor("xn1_pa", [128, 1024], BF))
        xT0 = es.enter_context(nc.sbuf_tensor("xT0_pa", [128, 8, 512], BF))
        xT1 = es.enter_context(nc.sbuf_tensor("xT1_pa", [128, 8, 512], BF))
        stF = es.enter_context(nc.sbuf_tensor("stF_pa", [128, 8, 512], BF))
        stV = es.enter_context(nc.sbuf_tensor("stV_pa", [128, 4, 512], BF))
        fw = es.enter_context(nc.sbuf_tensor("fw_pa", [8, 2, 6, 512], F32))
        cs = es.enter_context(nc.sbuf_tensor("cs_pa", [8, 2, 6, 512], BF))
        psT0 = es.enter_context(nc.psum_tensor("psT0_pa", [128, 8, 128], BF))
        psT1 = es.enter_context(nc.psum_tensor("psT1_pa", [128, 8, 128], BF))
        psF0 = es.enter_context(nc.psum_tensor("psF0_pa", [128, 512], F32))
        psF1 = es.enter_context(nc.psum_tensor("psF1_pa", [128, 512], F32))
        psF2 = es.enter_context(nc.psum_tensor("psF2_pa", [128, 512], F32))
        psV0 = es.enter_context(nc.psum_tensor("psV0_pa", [128, 512], F32))
        psV1 = es.enter_context(nc.psum_tensor("psV1_pa", [128, 512], F32))
        psf = es.enter_context(nc.psum_tensor("psf_pa", [8, 512], F32))
        xb = [xb0, xb1]
        xn = [xn0, xn1]
        xT = [xT0, xT1]
        psT = [psT0, psT1]
        psF = [psF0, psF1, psF2]
        psV = [psV0, psV1]
        sc.dma("sp", lambda e: e.dma_start(out=gn_t[:], in_=gn), writes=["gn_t"])
        sc.dma("sp", lambda e: e.dma_start(out=id_t[:], in_=ident), writes=["id_t"])
        sc.dma("sp", lambda e: e.dma_start(out=nb_t[:], in_=bfv), writes=["nb_t"])
        sc.op("dve", lambda e: e.memset(eps_t[:], 1e-6), writes=["eps_t"])
        sc.op("dve", lambda e: e.memset(one_t[:], 1.0), writes=["one_t"])
        sc.op("dve", lambda e: e.memset(ones8[:], 1.0), writes=["ones8"])
        sc.op("dve", lambda e: e.tensor_scalar(out=nb_t[:], in0=nb_t[:], scalar1=-1.0, scalar2=None, op0=ALU.mult),
              reads=["nb_t"], writes=["nb_t"])
        for kc in range(8):
            s = kc % 2
            sc.dma("sp" if s == 0 else "pool",
                   lambda e, kc=kc, s=s: e.dma_start(out=xb[s][:, :], in_=w[kc * 128:(kc + 1) * 128, :]),
                   writes=[("xb", s)])
            eng = "dve" if s == 0 else "pool"
            sc.op(eng, lambda e, kc=kc, s=s: e.tensor_scalar(
                out=Wb[:, kc, :], in0=xb[s][:, :], scalar1=gn_t[:, kc:kc + 1], scalar2=None, op0=ALU.mult),
                reads=[("xb", s), "gn_t"], writes=[("Wb", kc)])
        WbR = [("Wb", kc) for kc in range(8)]
        evac_box = [0]

        def load_x(tb):
            s = tb % 2
            xv = xb[s][:, 0:4096].rearrange("p (j f) -> p j f", j=4)
            src = x[tb * 512:(tb + 1) * 512, :].rearrange("(j p) f -> p j f", p=128)
            sc.dma("sp", lambda e, xv=xv, src=src: e.dma_start(out=xv, in_=src), writes=[("xb", s)])

        def norm_tile(tb, j):
            s = tb % 2
            xv = xb[s][:, 0:4096].rearrange("p (j f) -> p j f", j=4)
            xs = (tb * 4 + j) % 2
            col = (tb * 4 + j) % 8
            sc.op("act", lambda e: e.activation(out=junk[:], in_=xv[:, j, :], func=AF.Square),
                  reads=[("xb", s)], writes=["junk"])
            sc.op("dve", lambda e: e.reduce_sum(out=ss[:, col:col + 1], in_=junk[:], axis=AX.X),
                  reads=["junk"], writes=[("ss", col)])
            sc.op("act", lambda e: e.activation(out=ss[:, col:col + 1], in_=ss[:, col:col + 1], func=AF.Sqrt,
                                                bias=eps_t[:, 0:1], scale=1.0 / 1024.0),
                  reads=[("ss", col), "eps_t"], writes=[("ss", col)])
            sc.op("dve", lambda e: e.reciprocal(out=ss[:, col:col + 1], in_=ss[:, col:col + 1]),
                  reads=[("ss", col)], writes=[("ss", col)])
            sc.op("dve", lambda e: e.tensor_scalar(out=xn[xs][:], in0=xv[:, j, :], scalar1=ss[:, col:col + 1], scalar2=None, op0=ALU.mult),
                  reads=[("xb", s), ("ss", col)], writes=[("xn", xs)])

        def transp_tile(tb, j):
            s = tb % 2
            xs = (tb * 4 + j) % 2
            for kc in range(8):
                sc.op("pe", lambda e, kc=kc: e.transpose(out=psT[xs][:, kc, :], in_=xn[xs][:, kc * 128:(kc + 1) * 128], identity=id_t[:]),
                      reads=[("xn", xs), "id_t"], writes=[("psT", xs)])
            sc.op("act", lambda e: e.copy(out=xT[s][:, :, j * 128:(j + 1) * 128], in_=psT[xs][:]),
                  reads=[("psT", xs)], writes=[("xT", s, j)])

        def feat_group(tb, cg):
            s = tb % 2
            pb = cg % 3
            for kc in range(8):
                sc.op("pe", lambda e, kc=kc: e.matmul(
                    psF[pb][:], lhsT=Wb[:, kc, cg * 128:(cg + 1) * 128], rhs=xT[s][:, kc, :],
                    start=(kc == 0), stop=(kc == 7)),
                    reads=[("xT", s, jj) for jj in range(4)] + [("Wb", kc)], writes=[("psF", pb)])
            sl = evac_box[0] % 8
            evac_box[0] += 1
            kind = (cg // 4) % 3
            if kind == 0:
                sc.op("act", lambda e: e.activation(out=stF[:, sl, :], in_=psF[pb][:], func=AF.Copy, scale=0.125),
                      reads=[("psF", pb)], writes=[("stF", sl)])
            elif kind == 1:
                sc.op("dve", lambda e: e.tensor_copy(out=stF[:, sl, :], in_=psF[pb][:]),
                      reads=[("psF", pb)], writes=[("stF", sl)])
            else:
                sc.op("act", lambda e: e.activation(out=stF[:, sl, :], in_=psF[pb][:], func=AF.Silu),
                      reads=[("psF", pb)], writes=[("stF", sl)])
            sc.dma("sp" if sl % 2 == 0 else "act", lambda e: e.dma_start(out=featT[cg * 128:(cg + 1) * 128, tb * 512:(tb + 1) * 512], in_=stF[:, sl, :]),
                   reads=[("stF", sl)])

        def v_group(tb, j, half):
            s = tb % 2
            pv = (j * 2 + half) % 2
            for kc in range(8):
                sc.op("pe", lambda e, kc=kc: e.matmul(
                    psV[pv][:], lhsT=xT[s][:, kc, j * 128:(j + 1) * 128],
                    rhs=Wb[:, kc, NFEAT + half * 512:NFEAT + (half + 1) * 512],
                    start=(kc == 0), stop=(kc == 7)),
                    reads=[("xT", s, j), ("Wb", kc)], writes=[("psV", pv)])
            sl = (j * 2 + half) % 4
            sc.op("dve", lambda e: e.tensor_copy(out=stV[:, sl, :], in_=psV[pv][:]),
                  reads=[("psV", pv)], writes=[("stV", sl)])
            r0 = tb * 512 + j * 128
            sc.dma("pool", lambda e: e.dma_start(out=vtok[r0:r0 + 128, half * 512:(half + 1) * 512], in_=stV[:, sl, :]),
                   reads=[("stV", sl)])

        load_x(0)
        if ntb > 1:
            load_x(1)
        for j in range(4):
            norm_tile(0, j)
            transp_tile(0, j)
        for tb in range(ntb):
            s = tb % 2
            nxt = tb + 1 < ntb
            if tb + 2 < ntb:
                load_x(tb + 2)
            groups = [(lambda cg=cg: feat_group(tb, cg)) for cg in range(NFEAT // 128)]
            groups += [(lambda j=j, half=half: v_group(tb, j, half)) for j in range(4) for half in range(2)]
            for gi, g in enumerate(groups):
                g()
                if nxt and gi % 8 == 1:
                    norm_tile(tb + 1, gi // 8)
                if nxt and gi % 8 == 5:
                    transp_tile(tb + 1, gi // 8)
            for kc in range(8):
                sc.op("pe", lambda e, kc=kc, s=s: e.matmul(
                    psf[:], lhsT=Wb[:, kc, NFEAT + NV:NFEAT + NV + 8], rhs=xT[s][:, kc, :],
                    start=(kc == 0), stop=(kc == 7)),
                    reads=[("xT", s, jj) for jj in range(4)] + [("Wb", kc)], writes=["psf"])
            fs = tb % 2
            fwv = fw[:, fs]
            csv = cs[:, fs]
            R_fw = ("fw", fs)
            sc.op("act", lambda e, fwv=fwv: e.activation(out=fwv[:, 0, :], in_=psf[:], func=AF.Exp, bias=nb_t[:, 0:1], scale=-1.0),
                  reads=["psf", "nb_t"], writes=[R_fw])
            sc.op("act", lambda e, fwv=fwv: e.activation(out=fwv[:, 1, :], in_=fwv[:, 0, :], func=AF.Ln, bias=one_t[0:8, 0:1], scale=1.0),
                  reads=[R_fw, "one_t"], writes=[R_fw])
            if tb == 0:
                init = 0.0
                rd = [R_fw, "ones8"]
            else:
                init = fw[:, 1 - fs, 2, 511:512]
                rd = [R_fw, "ones8", ("fw", 1 - fs)]
            sc.op("dve", lambda e, fwv=fwv, init=init: e.tensor_tensor_scan(
                out=fwv[:, 2, :], data0=ones8[:], data1=fwv[:, 1, :], initial=init, op0=ALU.mult, op1=ALU.subtract),
                reads=rd, writes=[R_fw])
            R_cs = ("cs", fs)
            sc.op("dve", lambda e, fwv=fwv, csv=csv: e.tensor_copy(out=csv[:, 0, :], in_=fwv[:, 2, :]), reads=[R_fw], writes=[R_cs])
            sc.op("dve", lambda e, fwv=fwv, csv=csv: e.tensor_tensor(out=fwv[:, 3, :], in0=fwv[:, 2, :], in1=csv[:, 0, :], op=ALU.subtract),
                  reads=[R_fw, R_cs], writes=[R_fw])
            sc.op("dve", lambda e, fwv=fwv, csv=csv: e.tensor_copy(out=csv[:, 1, :], in_=fwv[:, 3, :]), reads=[R_fw], writes=[R_cs])
            sc.op("dve", lambda e, fwv=fwv, csv=csv: e.tensor_tensor(out=fwv[:, 4, :], in0=fwv[:, 3, :], in1=csv[:, 1, :], op=ALU.subtract),
                  reads=[R_fw, R_cs], writes=[R_fw])
            sc.op("dve", lambda e, fwv=fwv, csv=csv: e.tensor_copy(out=csv[:, 2, :], in_=fwv[:, 4, :]), reads=[R_fw], writes=[R_cs])
            sc.op("dve", lambda e, csv=csv: e.tensor_scalar(out=csv[:, 3:6, :], in0=csv[:, 0:3, :], scalar1=-1.0, scalar2=None, op0=ALU.mult),
                  reads=[R_cs], writes=[R_cs])
            sc.dma("sp", lambda e, csv=csv, tb=tb: e.dma_start(out=caug[:, :, tb * 512:(tb + 1) * 512], in_=csv),
                   reads=[R_cs])
        sc.emit()


def dram_ap(ap, offset, dims):
    return bass.AP(ap.tensor, offset, [list(d) for d in dims])


def build_attn_phase(nc, sc, featT, vtok, caug, ident, maskD, yT, fox_heads=range(8), dil_heads=range(8), nqc=16, on_job_done=None):
    with ExitStack() as es:
        _UID[0] += 1
        _u = _UID[0]
        sb = lambda n, s, d: es.enter_context(nc.sbuf_tensor(f"{n}_u{_u}", s, d))
        ps = lambda n, s, d: es.enter_context(nc.psum_tensor(f"{n}_u{_u}", s, d))
        Qa = [sb(f"Qa{i}", [70, S], BF) for i in range(2)]
        Ka = [sb(f"Ka{i}", [70, S], BF) for i in range(2)]
        Vt = [sb(f"Vt{i}", [128, 3, 64, 65], BF) for i in range(2)]
        acc = sb("acc", [65, S], F32)
        pT2 = [sb(f"pT{i}", [128, 1024], BF) for i in range(2)]
        pT = [pT2[0][:, 0:512], pT2[0][:, 512:1024], pT2[1][:, 0:512], pT2[1][:, 512:1024]]
        Gc = [sb(f"Gc{i}", [64, 512], BF) for i in range(3)]
        rec = [sb(f"rec{i}", [65, 512], F32) for i in range(3)]
        bcs = [sb(f"bcs{i}", [64, 512], F32) for i in range(3)]
        obt = [sb(f"obt{i}", [64, 512], F32) for i in range(3)]
        ysb = [sb(f"ysb{i}", [64, 512], BF) for i in range(3)]
        ones_b = sb("ones_b", [65, 64], BF)
        rhi = [sb(f"rhi{i}", [65, 512], BF) for i in range(3)]
        rlo = [sb(f"rlo{i}", [65, 512], BF) for i in range(3)]
        id_t = sb("id_t2", [128, 128], BF)
        mk_t = sb("mk_t", [128, 256], BF)
        mk_m = sb("mk_m", [128, 256], BF)
        psS2 = [ps(f"psS{i}", [128, 1024], F32) for i in range(2)]
        psS = [psS2[0][:, 0:512], psS2[0][:, 512:1024], psS2[1][:, 0:512], psS2[1][:, 512:1024]]
        psO = [ps(f"psO{i}", [128, 512], F32) for i in range(2)]
        psB = ps("psB", [64, 512], F32)

        sc.dma("sp", lambda e: e.dma_start(out=id_t[:], in_=ident), writes=["id_t"])
        sc.dma("sp", lambda e: e.dma_start(out=mk_t[:], in_=maskD), writes=["mk_t"])
        sc.op("dve", lambda e: e.memset(ones_b[:], 1.0), writes=["ones_b"])
        sc.op("dve", lambda e: e.tensor_scalar(out=mk_m[:], in0=mk_t[:], scalar1=0.0, scalar2=None, op0=ALU.is_equal),
              reads=["mk_t"], writes=["mk_m"])
        for i in range(2):
            sc.op("dve", lambda e, i=i: e.memset(Qa[i][64:70, :], 1.0), writes=[("Qa", i)])
            sc.op("dve", lambda e, i=i: e.memset(Ka[i][64:70, :], 1.0), writes=[("Ka", i)])
            sc.op("pool", lambda e, i=i: e.memset(Vt[i][:, :, :, 64:65], 1.0), writes=[("Vt", i)])

        jobs = [("fox", h) for h in fox_heads] + [("dil", h) for h in dil_heads]

        def load(ji):
            kind, h = jobs[ji]
            s = ji % 2
            if kind == "fox":
                qrow, krow, vcol = h * 64, 512 + h * 64, h * 64
            else:
                qrow, krow, vcol = 1536 + h * 64, 2048 + h * 64, 512 + h * 64
            sc.dma("sp", lambda e: e.dma_start(out=Qa[s][0:64, :], in_=featT[qrow:qrow + 64, :]), writes=[("Qa", s)])
            sc.dma("sp", lambda e: e.dma_start(out=Ka[s][0:64, :], in_=featT[krow:krow + 64, :]), writes=[("Ka", s)])
            if kind == "fox":
                sc.dma("sp", lambda e: e.dma_start(out=Qa[s][64:67, :], in_=caug[h, 0:3, :]), writes=[("Qa", s)])
                sc.dma("sp", lambda e: e.dma_start(out=Ka[s][67:70, :], in_=caug[h, 3:6, :]), writes=[("Ka", s)])
                pats = [(0, 1)]
            else:
                pats = [(0, 1), (1, 4), (2, 16)]
            for pi, r in pats:
                nblk = S // r // 128
                for s_ in range(r):
                    step = 16 if nblk >= 16 else nblk
                    for j0 in range(0, nblk, step):
                        src = dram_ap(vtok, (s_ + r * 128 * j0) * 1024 + vcol,
                                      [[r * 1024, 128], [r * 128 * 1024, step], [1, 64]])
                        t0 = s_ * nblk + j0
                        sc.dma("sp", lambda e, src=src, pi=pi, t0=t0, step=step: e.dma_start(
                            out=Vt[s][:, pi, t0:t0 + step, 0:64], in_=src), writes=[("Vt", s)])

        NS = 3
        pending = []
        slot_ctr = [0]
        tick = [0]

        def normalize(kind, h, qc, src_ps, ob, job=0):
            pending.append({"state": 0, "kind": kind, "h": h, "qc": qc, "src_ps": src_ps, "ob": ob, "job": job})

        def _srcs(t):
            cols = slice(t["qc"] * 512, (t["qc"] + 1) * 512)
            if t["src_ps"]:
                return psO[t["ob"]][64:65, :], psO[t["ob"]][0:64, :], [("psO", t["ob"])], cols
            return acc[64:65, cols], acc[0:64, cols], [("acc", t["qc"])], cols

        def stage1(t):
            rs = slot_ctr[0] % NS
            slot_ctr[0] += 1
            t["rs"] = rs
            den, num, rd, cols = _srcs(t)
            grow = (1024 if t["kind"] == "fox" else 2560) + t["h"] * 64
            sc.dma("sp", lambda e: e.dma_start(out=Gc[rs][:], in_=featT[grow:grow + 64, cols]), writes=[("Gc", rs)])
            sc.op("dve", lambda e: e.reciprocal(out=rec[rs][64:65, :], in_=den), reads=rd, writes=[("rec", rs)])
            sc.op("dve", lambda e: e.tensor_copy(out=rhi[rs][64:65, :], in_=rec[rs][64:65, :]), reads=[("rec", rs)], writes=[("rhi", rs)])
            sc.op("dve", lambda e: e.tensor_tensor(out=rlo[rs][64:65, :], in0=rec[rs][64:65, :], in1=rhi[rs][64:65, :], op=ALU.subtract),
                  reads=[("rec", rs), ("rhi", rs)], writes=[("rlo", rs)])
            t["state"] = 1
            t["t_issue"] = tick[0]

        def stage2(t):
            rs = t["rs"]
            den, num, rd, cols = _srcs(t)
            yrow = (0 if t["kind"] == "fox" else 512) + t["h"] * 64
            job = t["job"]
            sc.op("pe", lambda e: e.matmul(psB[:], lhsT=ones_b[64:65, 0:64], rhs=rhi[rs][64:65, :], start=True, stop=False),
                  reads=[("rhi", rs), "ones_b"], writes=["psB"])
            sc.op("pe", lambda e: e.matmul(psB[:], lhsT=ones_b[64:65, 0:64], rhs=rlo[rs][64:65, :], start=False, stop=True),
                  reads=[("rlo", rs), "ones_b"], writes=["psB"])
            if t["src_ps"]:
                sc.op("dve", lambda e: e.tensor_copy(out=bcs[rs][:], in_=psB[:]), reads=["psB"], writes=[("bcs", rs)])
                sc.op("dve", lambda e: e.tensor_tensor(out=obt[rs][:], in0=num, in1=bcs[rs][:], op=ALU.mult),
                      reads=rd + [("bcs", rs)], writes=[("obt", rs)])
            else:
                sc.op("dve", lambda e: e.tensor_tensor(out=obt[rs][:], in0=num, in1=psB[:], op=ALU.mult),
                      reads=rd + ["psB"], writes=[("obt", rs)])
            sc.op("pool", lambda e: e.tensor_tensor(out=ysb[rs][:], in0=obt[rs][:], in1=Gc[rs][:], op=ALU.mult),
                  reads=[("obt", rs), ("Gc", rs)], writes=[("ysb", rs)])
            sc.dma("pool", lambda e: e.dma_start(out=yT[yrow:yrow + 64, cols], in_=ysb[rs][:]), reads=[("ysb", rs)], writes=[("yTd", job)])
            pending.remove(t)

        def pump():
            tick[0] += 1
            for t in list(pending):
                if t["state"] == 1:
                    if tick[0] - t["t_issue"] >= 6:
                        stage2(t)
                    break
            if sum(1 for t in pending if t["state"] == 1) < NS:
                for t in pending:
                    if t["state"] == 0:
                        stage1(t)
                        break

        def flush(pred=lambda t: True):
            for t in list(pending):
                if pred(t):
                    for u in list(pending):
                        if u is t:
                            break
                        if u["state"] == 1:
                            stage2(u)
                    if t["state"] == 0:
                        stage1(t)
                    stage2(t)

        tile_ctr = [0]

        def run_tiles(tiles):
            LA = 3
            base = tile_ctr[0]
            n = len(tiles)
            for i in range(min(LA, n)):
                tiles[i]["s_fn"]((base + i) % 4)
            for i in range(n):
                if i + LA < n:
                    tiles[i + LA]["s_fn"]((base + i + LA) % 4)
                b = (base + i) % 4
                c0, c1 = tiles[i]["cr"]
                sc.op("act", lambda e, b=b, c0=c0, c1=c1: e.activation(out=pT[b][:, c0:c1], in_=psS[b][:, c0:c1], func=AF.Exp),
                      reads=[("psS", b)], writes=[("pT", b)])
                if "mask" in tiles[i]:
                    m0 = tiles[i]["mask"]
                    sc.op("dve", lambda e, b=b, c0=c0, c1=c1, m0=m0: e.tensor_tensor(
                        out=pT[b][:, c0:c1], in0=pT[b][:, c0:c1], in1=mk_m[:, m0:m0 + (c1 - c0)], op=ALU.mult),
                        reads=[("pT", b), "mk_m"], writes=[("pT", b)])
                tiles[i]["pv_fn"](b)
                if tiles[i].get("post"):
                    tiles[i]["post"]()
                pump()
            tile_ctr[0] = base + n

        ob_box = [0]

        def do_job(ji, kind, h):
            s = ji % 2
            ob_ctr = ob_box[0]
            RQ, RK, RV = ("Qa", s), ("Ka", s), ("Vt", s)
            tiles = []
            if kind == "fox":
                for qc in range(nqc):
                    ob = ob_ctr % 2
                    ob_ctr += 1
                    nk = 4 * qc + 4
                    for kt in range(nk):
                        j = kt - 4 * qc
                        c0 = 128 * max(j, 0)

                        def s_fn(b, kt=kt, qc=qc, j=j, c0=c0):
                            kap = Ka[s][0:70, kt * 128:(kt + 1) * 128]
                            if j < 0:
                                sc.op("pe", lambda e: e.matmul(psS[b][:, 0:512], lhsT=kap, rhs=Qa[s][0:70, qc * 512:(qc + 1) * 512],
                                                               start=True, stop=True),
                                      reads=[RQ, RK], writes=[("psS", b)])
                            else:
                                q0 = qc * 512 + c0
                                sc.op("pe", lambda e: e.matmul(psS[b][:, c0:c0 + 128], lhsT=id_t[:], rhs=mk_t[:, 0:128],
                                                               start=True, stop=False),
                                      reads=["id_t", "mk_t"], writes=[("psS", b)])
                                sc.op("pe", lambda e: e.matmul(psS[b][:, c0:c0 + 128], lhsT=kap, rhs=Qa[s][0:70, q0:q0 + 128],
                                                               start=False, stop=True),
                                      reads=[RQ, RK], writes=[("psS", b)])
                                if c0 + 128 < 512:
                                    sc.op("pe", lambda e: e.matmul(psS[b][:, c0 + 128:512], lhsT=kap,
                                                                   rhs=Qa[s][0:70, q0 + 128:(qc + 1) * 512], start=True, stop=True),
                                          reads=[RQ, RK], writes=[("psS", b)])

                        def pv_fn(b, kt=kt, c0=c0, ob=ob, nk=nk):
                            if kt == 0:
                                flush(lambda t: t["src_ps"] and t["ob"] == ob)
                            sc.op("pe", lambda e: e.matmul(psO[ob][0:65, c0:512], lhsT=Vt[s][:, 0, kt, 0:65], rhs=pT[b][:, c0:512],
                                                           start=(kt == 0), stop=(kt == nk - 1), skip_group_check=True),
                                  reads=[("pT", b), RV], writes=[("psO", ob)])

                        t = {"s_fn": s_fn, "cr": (c0, 512), "pv_fn": pv_fn}
                        if kt == nk - 1:
                            t["post"] = (lambda qc=qc, ob=ob: normalize("fox", h, qc, True, ob, ji))
                        tiles.append(t)
                run_tiles_pairs(tiles)
            else:
                for pi, r in [(0, 1), (1, 4), (2, 16)]:
                    nblk = S // r // 128
                    for s_ in range(r):
                        for c in range(nblk // 4):
                            n0 = 4 * c
                            ob = ob_ctr % 2
                            ob_ctr += 1
                            qbase = s_ + r * 128 * n0
                            js = ([n0 - 1] if n0 > 0 else []) + list(range(n0, n0 + 4))
                            for j in js:
                                b_lo, b_hi = max(j, n0), min(j + 1, n0 + 3)
                                c0, c1 = (b_lo - n0) * 128, (b_hi - n0 + 1) * 128
                                m0 = 0 if b_lo == j else 128

                                def s_fn(b, j=j, c0=c0, c1=c1, m0=m0, r=r, s_=s_, qbase=qbase):
                                    kb = s_ + r * 128 * j
                                    kap = Ka[s][0:64, kb:kb + r * 127 + 1:r]
                                    qap = Qa[s][0:64, qbase + r * c0:qbase + r * (c1 - 1) + 1:r]
                                    sc.op("pe", lambda e: e.matmul(psS[b][:, c0:c1], lhsT=id_t[:], rhs=mk_t[:, m0:m0 + (c1 - c0)],
                                                                   start=True, stop=False),
                                          reads=["id_t", "mk_t"], writes=[("psS", b)])
                                    sc.op("pe", lambda e: e.matmul(psS[b][:, c0:c1], lhsT=kap, rhs=qap, start=False, stop=True),
                                          reads=[RQ, RK], writes=[("psS", b)])

                                def pv_fn(b, j=j, b_lo=b_lo, b_hi=b_hi, n0=n0, ob=ob, pi=pi, s_=s_, nblk=nblk):
                                    flush(lambda t: t["src_ps"] and t["ob"] == ob)
                                    for bb in range(b_lo, b_hi + 1):
                                        cb = (bb - n0) * 128
                                        st = (j == bb - 1) or (bb == 0 and j == 0)
                                        sc.op("pe", lambda e, cb=cb, st=st, bb=bb: e.matmul(
                                            psO[ob][0:65, cb:cb + 128], lhsT=Vt[s][:, pi, s_ * nblk + j, 0:65], rhs=pT[b][:, cb:cb + 128],
                                            start=st, stop=(j == bb), skip_group_check=True),
                                            reads=[("pT", b), RV], writes=[("psO", ob)])

                                t = {"s_fn": s_fn, "cr": (c0, c1), "pv_fn": pv_fn}
                                if j == js[-1]:
                                    def post(ob=ob, pi=pi, r=r, qbase=qbase, c=c):
                                        av = acc[0:65, qbase:qbase + r * 511 + 1:r]
                                        ares = [("acc", k) for k in range(r * c, r * c + r)]
                                        flush(lambda t: (not t["src_ps"]) and t["qc"] in range(r * c, r * c + r))
                                        if pi == 0:
                                            sc.op("dve", lambda e: e.tensor_copy(out=av, in_=psO[ob][0:65, :]),
                                                  reads=[("psO", ob)], writes=ares)
                                        else:
                                            sc.op("dve", lambda e: e.tensor_tensor(out=av, in0=av, in1=psO[ob][0:65, :], op=ALU.add),
                                                  reads=[("psO", ob)] + ares, writes=ares)
                                    t["post"] = post
                                tiles.append(t)
                run_tiles(tiles)
                for qc in range(nqc):
                    normalize("dil", h, qc, False, 0, ji)
            ob_box[0] = ob_ctr

        def run_tiles_pairs(tiles):
            if tile_ctr[0] % 2:
                tile_ctr[0] += 1
            base = tile_ctr[0]
            n = len(tiles)
            pairs = [list(range(i, min(i + 2, n))) for i in range(0, n, 2)]

            def issue_s(p):
                for i in pairs[p]:
                    tiles[i]["s_fn"]((base + i) % 4)
            issue_s(0)
            for p in range(len(pairs)):
                if p + 1 < len(pairs):
                    issue_s(p + 1)
                for _ in pairs[p]:
                    pump()
                idx = pairs[p]
                b2 = ((base + idx[0]) % 4) // 2
                lo = tiles[idx[0]]["cr"][0]
                hi = 512 * (len(idx) - 1) + tiles[idx[-1]]["cr"][1]
                bs = [(base + i) % 4 for i in idx]
                if len(idx) == 2 and tiles[idx[1]]["cr"][0] > 0:
                    for i, b in zip(idx, bs):
                        c0, c1 = tiles[i]["cr"]
                        sc.op("act", lambda e, b=b, c0=c0, c1=c1: e.activation(out=pT[b][:, c0:c1], in_=psS[b][:, c0:c1], func=AF.Exp),
                              reads=[("psS", b)], writes=[("pT", b)])
                else:
                    sc.op("act", lambda e, b2=b2, lo=lo, hi=hi: e.activation(out=pT2[b2][:, lo:hi], in_=psS2[b2][:, lo:hi], func=AF.Exp),
                          reads=[("psS", b) for b in bs], writes=[("pT", b) for b in bs])
                for i in idx:
                    tiles[i]["pv_fn"]((base + i) % 4)
                    if tiles[i].get("post"):
                        tiles[i]["post"]()
            tile_ctr[0] = base + n

        done_box = [0]

        def notify_done():
            while done_box[0] < len(jobs) and done_box[0] < cur_job[0] + 0 and not any(t["job"] == done_box[0] for t in pending):
                if on_job_done is not None:
                    on_job_done(done_box[0])
                done_box[0] += 1

        cur_job = [0]
        _pump0 = pump

        def pump():
            _pump0()
            notify_done()

        load(0)
        for ji, (kind, h) in enumerate(jobs):
            cur_job[0] = ji
            if ji + 1 < len(jobs):
                load(ji + 1)
            do_job(ji, kind, h)
        cur_job[0] = len(jobs)
        flush()
        notify_done()
        sc.emit()
import numpy as np
import concourse.bass as bass
import concourse.mybir as mybir
from contextlib import ExitStack

F32 = mybir.dt.float32
BF = mybir.dt.bfloat16
AF = mybir.ActivationFunctionType
ALU = mybir.AluOpType
AX = mybir.AxisListType
S = 8192
_UID = [0]


def build_outproj(nc, sc, yT, wout, xres, ident, gvec, out_main, out_xT, final, ntok=4096):
    with ExitStack() as es:
        _UID[0] += 1
        _u = _UID[0]
        sb = lambda n, s, d: es.enter_context(nc.sbuf_tensor(f"{n}_u{_u}", s, d))
        ps = lambda n, s, d: es.enter_context(nc.psum_tensor(f"{n}_u{_u}", s, d))
        Wo = sb("Wo", [128, 16, 1024], BF)
        wst = [sb(f"wst{i}", [128, 1024], F32) for i in range(2)]
        yt = [sb(f"yt{i}", [128, 16, 512], BF) for i in range(2)]
        xt = [sb(f"xt{i}", [128, 1024], F32) for i in range(2)]
        x1 = [sb(f"x1{i}", [128, 1024], F32) for i in range(2)]
        junk = sb("junkb", [128, 1024], F32)
        ss = sb("ssb", [128, 8], F32)
        eps_t = sb("epsb", [128, 1], F32)
        id_t = sb("idb", [128, 128], BF)
        xn = [sb(f"xnb{i}", [128, 1024], BF) for i in range(2)]
        xTs = [sb(f"xTs{i}", [128, 8, 512], BF) for i in range(2)]
        gft = sb("gft", [128, 1024], F32)
        psA = [ps(f"psA{i}", [128, 512], F32) for i in range(4)]
        psT = [ps(f"psTb{i}", [128, 8, 128], BF) for i in range(2)]
        sc.dma("sp", lambda e: e.dma_start(out=id_t[:], in_=ident), writes=["id_t"])
        sc.op("dve", lambda e: e.memset(eps_t[:], 1e-6), writes=["eps_t"])
        if final:
            gsrc = bass.AP(gvec.tensor, 0, [[0, 128], [1, 1024]])
            sc.dma("sp", lambda e: e.dma_start(out=gft[:], in_=gsrc), writes=["gft"])
        for kc in range(16):
            s = kc % 2
            sc.dma("sp" if s == 0 else "pool", lambda e, kc=kc, s=s: e.dma_start(out=wst[s][:], in_=wout[kc * 128:(kc + 1) * 128, :]),
                   writes=[("wst", s)])
            sc.op("dve" if s == 0 else "pool", lambda e, kc=kc, s=s: e.tensor_copy(out=Wo[:, kc, :], in_=wst[s][:]),
                  reads=[("wst", s)], writes=[("Wo", kc)])
        ti = 0
        for tb in range(ntok // 512):
            s = tb % 2
            ysrc = yT[:, tb * 512:(tb + 1) * 512].rearrange("(kc p) t -> p kc t", p=128)
            sc.dma("sp", lambda e, s=s, ysrc=ysrc: e.dma_start(out=yt[s][:], in_=ysrc), writes=[("yt", s)])
            for j in range(4):
                xs = ti % 2
                col = ti % 8
                ti += 1
                r0 = tb * 512 + j * 128
                sc.dma("pool", lambda e, xs=xs, r0=r0: e.dma_start(out=xt[xs][:], in_=xres[r0:r0 + 128, :]), writes=[("xt", xs)])
                for half in range(2):
                    pb = (xs * 2 + half)
                    for kc in range(16):
                        sc.op("pe", lambda e, pb=pb, kc=kc, s=s, j=j, half=half: e.matmul(
                            psA[pb][:], lhsT=yt[s][:, kc, j * 128:(j + 1) * 128], rhs=Wo[:, kc, half * 512:(half + 1) * 512],
                            start=(kc == 0), stop=(kc == 15)),
                            reads=[("yt", s), ("Wo", kc)], writes=[("psA", pb)])
                    sc.op("dve", lambda e, pb=pb, xs=xs, half=half: e.tensor_tensor(
                        out=x1[xs][:, half * 512:(half + 1) * 512], in0=xt[xs][:, half * 512:(half + 1) * 512], in1=psA[pb][:], op=ALU.add),
                        reads=[("psA", pb), ("xt", xs)], writes=[("x1", xs)])
                if not final:
                    sc.dma("sp", lambda e, xs=xs, r0=r0: e.dma_start(out=out_main[r0:r0 + 128, :], in_=x1[xs][:]), reads=[("x1", xs)])
                sc.op("act", lambda e, xs=xs: e.activation(out=junk[:], in_=x1[xs][:], func=AF.Square), reads=[("x1", xs)], writes=["junk"])
                sc.op("dve", lambda e, col=col: e.reduce_sum(out=ss[:, col:col + 1], in_=junk[:], axis=AX.X), reads=["junk"], writes=[("ss", col)])
                sc.op("act", lambda e, col=col: e.activation(out=ss[:, col:col + 1], in_=ss[:, col:col + 1], func=AF.Sqrt,
                                                             bias=eps_t[:, 0:1], scale=1.0 / 1024.0),
                      reads=[("ss", col), "eps_t"], writes=[("ss", col)])
                sc.op("dve", lambda e, col=col: e.reciprocal(out=ss[:, col:col + 1], in_=ss[:, col:col + 1]), reads=[("ss", col)], writes=[("ss", col)])
                if final:
                    sc.op("dve", lambda e, xs=xs, col=col: e.scalar_tensor_tensor(
                        out=xt[xs][:], in0=x1[xs][:], scalar=ss[:, col:col + 1], in1=gft[:], op0=ALU.mult, op1=ALU.mult),
                        reads=[("x1", xs), ("ss", col), "gft"], writes=[("xt", xs)])
                    sc.dma("sp", lambda e, xs=xs, r0=r0: e.dma_start(out=out_main[r0:r0 + 128, :], in_=xt[xs][:]), reads=[("xt", xs)])
                else:
                    sc.op("dve", lambda e, xs=xs, col=col: e.tensor_scalar(out=xn[xs][:], in0=x1[xs][:], scalar1=ss[:, col:col + 1],
                                                                         scalar2=None, op0=ALU.mult),
                          reads=[("x1", xs), ("ss", col)], writes=[("xn", xs)])
                    for kc in range(8):
                        sc.op("pe", lambda e, xs=xs, kc=kc: e.transpose(out=psT[xs][:, kc, :], in_=xn[xs][:, kc * 128:(kc + 1) * 128], identity=id_t[:]),
                              reads=[("xn", xs), "id_t"], writes=[("psT", xs)])
                    sc.op("act", lambda e, xs=xs, s=s, j=j: e.copy(out=xTs[s][:, :, j * 128:(j + 1) * 128], in_=psT[xs][:]),
                          reads=[("psT", xs)], writes=[("xTs", s)])
            if not final:
                dst = out_xT[:, tb * 512:(tb + 1) * 512].rearrange("(kc p) t -> p kc t", p=128)
                sc.dma("sp", lambda e, s=s, dst=dst: e.dma_start(out=dst, in_=xTs[s][:]), reads=[("xTs", s)])
        sc.emit()


NW1 = 3072


def build_retention(nc, sc, xsrc_fn, w1, gn, ident, cosT, sinT, maskR, qdT, kdec, cd, ydst_fn, ntb=16, on_block_done=None):
    with ExitStack() as es:
        _UID[0] += 1
        _u = _UID[0]
        sb = lambda n, s, d: es.enter_context(nc.sbuf_tensor(f"{n}_u{_u}", s, d))
        ps = lambda n, s, d: es.enter_context(nc.psum_tensor(f"{n}_u{_u}", s, d))
        Wb = sb("W1b", [128, 8, NW1], BF)
        wst = [sb(f"w1st{i}", [128, NW1], F32) for i in range(2)]
        gn_t = sb("gn1", [128, 8], F32)
        gn16 = sb("gn16", [128, 8], F32)
        id_t = sb("idc", [128, 128], BF)
        eps_t = sb("epsc", [128, 1], F32)
        mk = sb("mkR", [128, 2, 128], F32)
        qd = sb("qd_sb", [128, 2, 128], F32)
        kd = sb("kd_sb", [128, 2], F32)
        xT = [sb(f"xTc{i}", [128, 8, 512], BF) for i in range(2)]
        cs_t = [sb(f"cos{i}", [128, 512], F32) for i in range(2)]
        sn_t = [sb(f"sin{i}", [128, 512], F32) for i in range(2)]
        tm = [sb(f"tm{i}", [128, 4, 512], F32) for i in range(2)]
        QT = [sb(f"QT{i}", [128, 2, 2, 512], BF) for i in range(2)]
        QdT = [sb(f"QdT{i}", [128, 2, 2, 512], BF) for i in range(2)]
        KT = [sb(f"KT{i}", [128, 2, 2, 512], BF) for i in range(2)]
        Ktok = [sb(f"Ktok{i}", [128, 4, 2, 256], BF) for i in range(2)]
        Vt = [sb(f"Vc{i}", [128, 4, 2, 512], BF) for i in range(2)]
        Gt = [sb(f"Gc{i}", [128, 4, 2, 512], BF) for i in range(2)]
        St = sb("St", [128, 2, 2, 512], F32)
        Stb = [sb(f"Stb{i}", [128, 2, 2, 512], BF) for i in range(2)]
        Sm = [sb(f"Sm{i}", [128, 128], BF) for i in range(2)]
        stats = sb("stats", [128, 2, 2, 6], F32)
        mv = sb("mv", [128, 2, 2, 2], F32)
        yn = [sb(f"yn{i}", [128, 512], F32) for i in range(2)]
        y2 = [sb(f"y2{i}", [128, 1024], BF) for i in range(2)]
        y2s = [sb(f"y2s{i}", [128, 8, 128], BF) for i in range(2)]
        psF = [ps(f"pcF{i}", [128, 512], F32) for i in range(2)]
        psS = ps("pcS", [128, 2, 128], F32)
        psO = [ps(f"pcO{i}", [128, 512], F32) for i in range(2)]
        psU = [ps(f"pcU{i}", [128, 512], F32) for i in range(2)]
        psT = ps("pcT", [128, 8, 128], BF)

        for (t, src, nm) in ((gn_t, gn, "gn_t"), (id_t, ident, "id_t"), (mk, maskR, "mk"), (qd, qdT, "qd"), (kd, kdec, "kd")):
            sc.dma("sp", lambda e, t=t, src=src: e.dma_start(out=t[:], in_=src), writes=[nm])
        cdt = sb("cdt", [128, 2], F32)
        cdsrc = bass.AP(cd.tensor, 0, [[0, 128], [1, 2]])
        sc.dma("sp", lambda e: e.dma_start(out=cdt[:], in_=cdsrc), writes=["cdt"])
        sc.op("dve", lambda e: e.memset(eps_t[:], 1e-6), writes=["eps_t"])
        sc.op("dve", lambda e: e.memset(St[:], 0.0), writes=[("St", a, b) for a in range(2) for b in range(2)])
        sc.op("dve", lambda e: e.memset(Stb[0][:], 0.0), writes=[("Stb", 0, a, b) for a in range(2) for b in range(2)])
        sc.op("dve", lambda e: e.tensor_scalar(out=gn16[:], in0=gn_t[:], scalar1=1.0 / 16.0, scalar2=None, op0=ALU.mult),
              reads=["gn_t"], writes=["gn16"])
        for kc in range(8):
            s = kc % 2
            sc.dma("sp" if s == 0 else "pool", lambda e, kc=kc, s=s: e.dma_start(out=wst[s][:], in_=w1[kc * 128:(kc + 1) * 128, :]),
                   writes=[("wst", s)])
            eng = "dve" if s == 0 else "pool"
            for (c0, c1, gt, gname) in ((0, 512, gn_t, "gn_t"), (512, 1024, gn16, "gn16"), (1024, NW1, gn_t, "gn_t")):
                sc.op(eng, lambda e, kc=kc, s=s, c0=c0, c1=c1, gt=gt: e.tensor_scalar(
                    out=Wb[:, kc, c0:c1], in0=wst[s][:, c0:c1], scalar1=gt[:, kc:kc + 1], scalar2=None, op0=ALU.mult),
                    reads=[("wst", s), gname], writes=[("Wb", kc, c0)])
        WR = lambda kc: [("Wb", kc, 0), ("Wb", kc, 512), ("Wb", kc, 1024)]
        chunk_i = 0
        deferred = [None]
        gn_def = [None]
        kdefer = []
        def load_blk(tb):
            s = tb % 2
            t0 = tb * 512
            src = xsrc_fn(tb).rearrange("(kc p) t -> p kc t", p=128)
            sc.dma("sp", lambda e: e.dma_start(out=xT[s][:, 0:4, :], in_=src[:, 0:4, :]), writes=[("xT", s)])
            sc.dma("act", lambda e: e.dma_start(out=xT[s][:, 4:8, :], in_=src[:, 4:8, :]), writes=[("xT", s)])
            sc.dma("pool", lambda e: e.dma_start(out=cs_t[s][:], in_=cosT[:, t0:t0 + 512]), writes=[("cos", s)])
            sc.dma("pool", lambda e: e.dma_start(out=sn_t[s][:], in_=sinT[:, t0:t0 + 512]), writes=[("sin", s)])

        load_blk(0)
        for tb in range(ntb):
            s = tb % 2
            t0 = tb * 512
            if tb + 1 < ntb:
                load_blk(tb + 1)
            for gi in range(4):
                isk, h = gi // 2, gi % 2
                fb = [psF[0], psF[1]] if gi % 2 == 0 else [psO[0], psO[1]]
                fr = [("psF", 0), ("psF", 1)] if gi % 2 == 0 else [("psO", 0), ("psO", 1)]
                for eo in range(2):
                    cg = gi * 2 + eo
                    for kc in range(8):
                        sc.op("pe", lambda e, eo=eo, kc=kc, cg=cg, s=s, fb=fb: e.matmul(
                            fb[eo][:], lhsT=Wb[:, kc, cg * 128:(cg + 1) * 128], rhs=xT[s][:, kc, :], start=(kc == 0), stop=(kc == 7)),
                            reads=[("xT", s)] + WR(kc), writes=[fr[eo]])
                ts = gi % 2
                RT = ("tm", ts)
                sc.op("dve", lambda e, ts=ts, s=s, fb=fb: e.tensor_tensor(out=tm[ts][:, 0, :], in0=fb[0][:], in1=cs_t[s][:], op=ALU.mult),
                      reads=[fr[0], ("cos", s)], writes=[RT])
                sc.op("dve", lambda e, ts=ts, s=s, fb=fb: e.tensor_tensor(out=tm[ts][:, 1, :], in0=fb[1][:], in1=sn_t[s][:], op=ALU.mult),
                      reads=[fr[1], ("sin", s)], writes=[RT])
                sc.op("dve", lambda e, ts=ts, s=s, fb=fb: e.tensor_tensor(out=tm[ts][:, 2, :], in0=fb[0][:], in1=sn_t[s][:], op=ALU.mult),
                      reads=[fr[0], ("sin", s)], writes=[RT])
                sc.op("dve", lambda e, ts=ts, s=s, fb=fb: e.tensor_tensor(out=tm[ts][:, 3, :], in0=fb[1][:], in1=cs_t[s][:], op=ALU.mult),
                      reads=[fr[1], ("cos", s)], writes=[RT])
                dst = (KT if isk else QT)[s]
                RD = ("KT" if isk else "QT", s)
                sc.op("pool", lambda e, ts=ts, dst=dst, h=h: e.tensor_tensor(out=dst[:, h, 0, :], in0=tm[ts][:, 0, :], in1=tm[ts][:, 1, :], op=ALU.subtract),
                      reads=[RT], writes=[RD])
                sc.op("pool", lambda e, ts=ts, dst=dst, h=h: e.tensor_tensor(out=dst[:, h, 1, :], in0=tm[ts][:, 2, :], in1=tm[ts][:, 3, :], op=ALU.add),
                      reads=[RT], writes=[RD])
                if not isk:
                    for eo in range(2):
                        qv = QT[s][:, h, eo, :].rearrange("p (c t) -> p c t", c=4)
                        ov = QdT[s][:, h, eo, :].rearrange("p (c t) -> p c t", c=4)
                        base = qd[:, h, :]
                        dv_ = bass.AP(base.tensor, base.offset, [list(base.ap[0]), [0, 4], [1, 128]])
                        sc.op("dve", lambda e, qv=qv, ov=ov, dv_=dv_: e.tensor_tensor(out=ov, in0=qv, in1=dv_, op=ALU.mult),
                              reads=[RD, "qd"], writes=[("QdT", s)])
                else:
                    def ktrans(s=s, h=h, RD=RD):
                        for c in range(4):
                            for eo in range(2):
                                sc.op("pe", lambda e, c=c, eo=eo: e.transpose(
                                    out=psT[:, c * 2 + eo, :], in_=KT[s][:, h, eo, c * 128:(c + 1) * 128], identity=id_t[:]),
                                    reads=[RD, "id_t"], writes=["psT"])
                        kv = Ktok[s][:, :, h, :].rearrange("p c (eo i) -> p c eo i", eo=2)
                        pv = psT[:].rearrange("p (c eo) i -> p c eo i", eo=2)
                        sc.op("act", lambda e: e.activation(out=kv, in_=pv, func=AF.Copy, scale=kd[:, h:h + 1]),
                              reads=["psT", "kd"], writes=[("Ktok", s)])
                    kdefer.append(ktrans)
            for j in range(4):
                for grp in range(4):
                    pb = grp % 2
                    for kc in range(8):
                        sc.op("pe", lambda e, pb=pb, kc=kc, j=j, grp=grp, s=s: e.matmul(
                            psU[pb][:], lhsT=xT[s][:, kc, j * 128:(j + 1) * 128], rhs=Wb[:, kc, 1024 + grp * 512:1024 + (grp + 1) * 512],
                            start=(kc == 0), stop=(kc == 7)),
                            reads=[("xT", s)] + WR(kc), writes=[("psU", pb)])
                    if grp < 2:
                        sc.op("act", lambda e, pb=pb, j=j, grp=grp, s=s: e.copy(out=Vt[s][:, j, grp, :], in_=psU[pb][:]),
                              reads=[("psU", pb)], writes=[("Vt", s)])
                    else:
                        sc.op("act", lambda e, pb=pb, j=j, grp=grp, s=s: e.activation(out=Gt[s][:, j, grp - 2, :], in_=psU[pb][:], func=AF.Silu),
                              reads=[("psU", pb)], writes=[("Gt", s)])
            for f in kdefer:
                f()
            kdefer.clear()
            for c in range(4):
                cur = chunk_i % 2
                nxt = 1 - cur
                ys = chunk_i % 2
                cc = slice(c * 128, (c + 1) * 128)
                oset = chunk_i % 2
                ob = [psO[0], psO[1]] if oset == 0 else [psF[0], psF[1]]
                orr = [("psO", 0), ("psO", 1)] if oset == 0 else [("psF", 0), ("psF", 1)]
                for h in range(2):
                    for eo in range(2):
                        sc.op("pe", lambda e, h=h, eo=eo, s=s, cc=cc: e.matmul(
                            psS[:, h, :], lhsT=KT[s][:, h, eo, cc], rhs=QT[s][:, h, eo, cc], start=(eo == 0), stop=(eo == 1)),
                            reads=[("KT", s), ("QT", s)], writes=["psSbank"])
                    sc.op("dve", lambda e, h=h: e.tensor_tensor(out=Sm[h][:], in0=psS[:, h, :], in1=mk[:, h, :], op=ALU.mult),
                          reads=["psSbank", "mk"], writes=[("Sm", h)])
                if gn_def[0] is not None:
                    gn_def[0]()
                    gn_def[0] = None
                for h in range(2):
                    sc.op("pe", lambda e, h=h, s=s, c=c, ob=ob: e.matmul(ob[h][:], lhsT=Sm[h][:], rhs=Vt[s][:, c, h, :], start=True, stop=False),
                          reads=[("Sm", h), ("Vt", s)], writes=[orr[h]])
                    for eo in range(2):
                        sc.op("pe", lambda e, h=h, eo=eo, s=s, cc=cc, cur=cur, ob=ob: e.matmul(
                            ob[h][:], lhsT=QdT[s][:, h, eo, cc], rhs=Stb[cur][:, eo, h, :], start=False, stop=(eo == 1)),
                            reads=[("QdT", s), ("Stb", cur, eo, h)], writes=[orr[h]])
                    for half in range(2):
                        sc.op("pe", lambda e, h=h, half=half, s=s, c=c: e.matmul(
                            psU[half][:], lhsT=Ktok[s][:, c, h, half * 128:(half + 1) * 128], rhs=Vt[s][:, c, h, :], start=True, stop=True),
                            reads=[("Ktok", s), ("Vt", s)], writes=[("psU", half)])
                        sc.op("dve", lambda e, h=h, half=half: e.scalar_tensor_tensor(
                            out=St[:, half, h, :], in0=St[:, half, h, :], scalar=cdt[:, h:h + 1], in1=psU[half][:], op0=ALU.mult, op1=ALU.add),
                            reads=[("St", half, h), ("psU", half), "cdt"], writes=[("St", half, h)])
                        sc.op("act", lambda e, h=h, half=half, nxt=nxt: e.copy(out=Stb[nxt][:, half, h, :], in_=St[:, half, h, :]),
                              reads=[("St", half, h)], writes=[("Stb", nxt, half, h)])

                def groupnorm(ys=ys, ob=ob, orr=orr, s=s, c=c):
                    for h in range(2):
                        sr = ("stats", ys, h)
                        sc.op("dve", lambda e, h=h: e.bn_stats(out=stats[:, ys, h, :], in_=ob[h][:]), reads=[orr[h]], writes=[sr])
                        sc.op("dve", lambda e, h=h: e.bn_aggr(out=mv[:, ys, h, :], in_=stats[:, ys, h, :]), reads=[sr], writes=[("mv", ys, h)])
                        sc.op("act", lambda e, h=h: e.activation(out=mv[:, ys, h, 1:2], in_=mv[:, ys, h, 1:2], func=AF.Sqrt, bias=eps_t[:, 0:1], scale=1.0),
                              reads=[("mv", ys, h), "eps_t"], writes=[("mv", ys, h)])
                    for h in range(2):
                        sc.op("dve", lambda e, h=h: e.reciprocal(out=mv[:, ys, h, 1:2], in_=mv[:, ys, h, 1:2]), reads=[("mv", ys, h)], writes=[("mv", ys, h)])
                        sc.op("dve", lambda e, h=h: e.tensor_scalar(out=yn[h][:], in0=ob[h][:], scalar1=mv[:, ys, h, 0:1], scalar2=mv[:, ys, h, 1:2],
                                                                    op0=ALU.subtract, op1=ALU.mult),
                              reads=[orr[h], ("mv", ys, h)], writes=[("yn", h)])
                        sc.op("pool", lambda e, h=h: e.tensor_tensor(out=y2[ys][:, h * 512:(h + 1) * 512], in0=yn[h][:], in1=Gt[s][:, c, h, :], op=ALU.mult),
                              reads=[("yn", h), ("Gt", s)], writes=[("y2", ys, h)])
                gn_def[0] = groupnorm

                def epilogue(ys=ys, tb=tb, c=c):
                    for kc in range(8):
                        sc.op("pe", lambda e, kc=kc: e.transpose(out=psT[:, kc, :], in_=y2[ys][:, kc * 128:(kc + 1) * 128], identity=id_t[:]),
                              reads=[("y2", ys, kc // 4), "id_t"], writes=["psT"])
                    sc.op("act", lambda e: e.copy(out=y2s[ys][:], in_=psT[:]), reads=["psT"], writes=[("y2s", ys)])
                    dst = ydst_fn(tb, c).rearrange("(kc p) t -> p kc t", p=128)
                    sc.dma("sp", lambda e: e.dma_start(out=dst, in_=y2s[ys][:]), reads=[("y2s", ys)], writes=[("y2d", tb)])
                    if c == 3 and on_block_done is not None:
                        on_block_done(tb)

                if deferred[0] is not None:
                    deferred[0]()
                deferred[0] = epilogue
                chunk_i += 1
            if gn_def[0] is not None:
                gn_def[0]()
                gn_def[0] = None

        if gn_def[0] is not None:
            gn_def[0]()
            gn_def[0] = None
        if deferred[0] is not None:
            deferred[0]()
        sc.emit()
import numpy as np
import concourse.bass as bass
import concourse.mybir as mybir
from contextlib import ExitStack

F32 = mybir.dt.float32
BF = mybir.dt.bfloat16
AF = mybir.ActivationFunctionType
ALU = mybir.AluOpType
AX = mybir.AxisListType
S = 8192
_UID = [0]


def build_outproj_p1(nc, sc, ysrc_fn, wout, xres, xout, ssloc):
    with ExitStack() as es:
        _UID[0] += 1
        _u = _UID[0]
        sb = lambda n, s, d: es.enter_context(nc.sbuf_tensor(f"{n}_u{_u}", s, d))
        ps = lambda n, s, d: es.enter_context(nc.psum_tensor(f"{n}_u{_u}", s, d))
        Wo = sb("Wo", [128, 16, 512], BF)
        wst = [sb(f"wst{i}", [128, 512], F32) for i in range(2)]
        yt = [sb(f"yt{i}", [128, 16, 512], BF) for i in range(2)]
        xt = [sb(f"xt{i}", [128, 512], F32) for i in range(2)]
        x1 = [sb(f"x1{i}", [128, 512], F32) for i in range(2)]
        junk = sb("junkb", [128, 512], F32)
        ssp = sb("ssp", [128, 64], F32)
        psA = [ps(f"psA{i}", [128, 512], F32) for i in range(4)]
        for kc in range(16):
            s = kc % 2
            sc.dma("sp" if s == 0 else "pool", lambda e, kc=kc, s=s: e.dma_start(out=wst[s][:], in_=wout[kc * 128:(kc + 1) * 128, :]),
                   writes=[("wst", s)])
            sc.op("dve" if s == 0 else "pool", lambda e, kc=kc, s=s: e.tensor_copy(out=Wo[:, kc, :], in_=wst[s][:]),
                  reads=[("wst", s)], writes=[("Wo", kc)])
        ti = 0

        def load_y(tb):
            s = tb % 2
            ysrc = ysrc_fn(tb).rearrange("(kc p) t -> p kc t", p=128)
            for q4, qn in enumerate(("sp", "act", "sp", "act")):
                sc.dma(qn, lambda e, q4=q4: e.dma_start(out=yt[s][:, q4 * 4:(q4 + 1) * 4, :], in_=ysrc[:, q4 * 4:(q4 + 1) * 4, :]),
                       writes=[("yt", s, q4)])

        load_y(0)
        for tb in range(S // 512):
            s = tb % 2
            if tb + 1 < S // 512:
                load_y(tb + 1)
            for j in range(4):
                xs = ti % 2
                pb = ti % 4
                tile = ti
                ti += 1
                r0 = tb * 512 + j * 128
                sc.dma("pool", lambda e, xs=xs, r0=r0: e.dma_start(out=xt[xs][:], in_=xres[r0:r0 + 128, :]), writes=[("xt", xs)])
                for kc in range(16):
                    sc.op("pe", lambda e, pb=pb, kc=kc, s=s, j=j: e.matmul(
                        psA[pb][:], lhsT=yt[s][:, kc, j * 128:(j + 1) * 128], rhs=Wo[:, kc, :], start=(kc == 0), stop=(kc == 15)),
                        reads=[("yt", s, kc // 4), ("Wo", kc)], writes=[("psA", pb)])
                sc.op("dve", lambda e, pb=pb, xs=xs: e.tensor_tensor(out=x1[xs][:], in0=xt[xs][:], in1=psA[pb][:], op=ALU.add),
                      reads=[("psA", pb), ("xt", xs)], writes=[("x1", xs)])
                sc.dma("sp", lambda e, xs=xs, r0=r0: e.dma_start(out=xout[r0:r0 + 128, :], in_=x1[xs][:]), reads=[("x1", xs)])
                sc.op("act", lambda e, xs=xs: e.activation(out=junk[:], in_=x1[xs][:], func=AF.Square), reads=[("x1", xs)], writes=["junk"])
                sc.op("dve", lambda e, tile=tile: e.reduce_sum(out=ssp[:, tile:tile + 1], in_=junk[:], axis=AX.X),
                      reads=["junk"], writes=["ssp"])
        sc.dma("sp", lambda e: e.dma_start(out=ssloc, in_=ssp[:]), reads=["ssp"])
        sc.emit()


def build_outproj_p2(nc, sc, xin, ssall, ident, gvec, out_main, xdst_fn, final, on_block_done=None):
    with ExitStack() as es:
        _UID[0] += 1
        _u = _UID[0]
        sb = lambda n, s, d: es.enter_context(nc.sbuf_tensor(f"{n}_u{_u}", s, d))
        ps = lambda n, s, d: es.enter_context(nc.psum_tensor(f"{n}_u{_u}", s, d))
        ssa = sb("ssa", [128, 2, 64], F32)
        rstd = sb("rstd", [128, 64], F32)
        eps_t = sb("epsb", [128, 1], F32)
        id_t = sb("idb", [128, 128], BF)
        gft = sb("gft", [128, 512], F32)
        xt = [sb(f"xq{i}", [128, 512], F32) for i in range(4)]
        xo = [sb(f"xo{i}", [128, 512], F32) for i in range(4)]
        xn = [sb(f"xnb{i}", [128, 512], BF) for i in range(4)]
        xTs = [sb(f"xTs{i}", [128, 4, 512], BF) for i in range(2)]
        psT = [ps(f"psTb{i}", [128, 4, 128], BF) for i in range(4)]
        sc.dma("sp", lambda e: e.dma_start(out=id_t[:], in_=ident), writes=["id_t"])
        sc.op("dve", lambda e: e.memset(eps_t[:], 1e-6), writes=["eps_t"])
        sc.dma("sp", lambda e: e.dma_start(out=ssa[:], in_=ssall.rearrange("(r p) t -> p r t", p=128)), reads=["ssall"], writes=["ssa"])
        if final:
            gsrc = bass.AP(gvec.tensor, 0, [[0, 128], [1, 512]])
            sc.dma("sp", lambda e: e.dma_start(out=gft[:], in_=gsrc), writes=["gft"])
        sc.op("dve", lambda e: e.tensor_tensor(out=rstd[:], in0=ssa[:, 0, :], in1=ssa[:, 1, :], op=ALU.add), reads=["ssa"], writes=["rstd"])
        sc.op("act", lambda e: e.activation(out=rstd[:], in_=rstd[:], func=AF.Sqrt, bias=eps_t[:, 0:1], scale=1.0 / 1024.0),
              reads=["rstd", "eps_t"], writes=["rstd"])
        sc.op("dve", lambda e: e.reciprocal(out=rstd[:], in_=rstd[:]), reads=["rstd"], writes=["rstd"])
        ti = 0
        for tb in range(S // 512):
            s = tb % 2
            for j in range(4):
                xs = ti % 4
                tile = ti
                ti += 1
                r0 = tb * 512 + j * 128
                sc.dma("sp" if xs % 2 == 0 else "pool", lambda e, xs=xs, r0=r0: e.dma_start(out=xt[xs][:], in_=xin[r0:r0 + 128, :]), writes=[("xt", xs)])
                if final:
                    sc.op("dve", lambda e, xs=xs, tile=tile: e.scalar_tensor_tensor(
                        out=xo[xs][:], in0=xt[xs][:], scalar=rstd[:, tile:tile + 1], in1=gft[:], op0=ALU.mult, op1=ALU.mult),
                        reads=[("xt", xs), "rstd", "gft"], writes=[("xo", xs)])
                    sc.dma("sp", lambda e, xs=xs, r0=r0: e.dma_start(out=out_main[r0:r0 + 128, :], in_=xo[xs][:]), reads=[("xo", xs)])
                else:
                    sc.op("dve", lambda e, xs=xs, tile=tile: e.tensor_scalar(out=xn[xs][:], in0=xt[xs][:], scalar1=rstd[:, tile:tile + 1],
                                                                           scalar2=None, op0=ALU.mult),
                          reads=[("xt", xs), "rstd"], writes=[("xn", xs)])
                    for kc in range(4):
                        sc.op("pe", lambda e, xs=xs, kc=kc: e.transpose(out=psT[xs][:, kc, :], in_=xn[xs][:, kc * 128:(kc + 1) * 128], identity=id_t[:]),
                              reads=[("xn", xs), "id_t"], writes=[("psT", xs)])
                    sc.op("act", lambda e, xs=xs, s=s, j=j: e.copy(out=xTs[s][:, :, j * 128:(j + 1) * 128], in_=psT[xs][:]),
                          reads=[("psT", xs)], writes=[("xTs", s)])
            if not final:
                dst = xdst_fn(tb).rearrange("(kc p) t -> p kc t", p=128)
                sc.dma("sp", lambda e, s=s, dst=dst: e.dma_start(out=dst, in_=xTs[s][:]), reads=[("xTs", s)], writes=[("xnd", tb)])
                if on_block_done is not None:
                    on_block_done(tb)
        sc.emit()
import ml_dtypes
import numpy as np, ml_dtypes
bf = ml_dtypes.bfloat16
S = 8192
def consts_common():
    kk = np.arange(128)[:, None]; qq = np.arange(128)[None, :]
    maskD = np.concatenate([np.where(qq >= kk, 0.0, -30000.0), np.where(kk >= qq, 0.0, -30000.0)], axis=1).astype(bf)
    return {"ident": np.eye(128, dtype=bf), "maskD": maskD}
def rot_tables():
    inv = (1.0 / (np.float32(10000.0) ** np.linspace(0.0, 1.0, 128, dtype=np.float32))).astype(np.float32)
    ang = (np.arange(S, dtype=np.float32)[:, None] * inv[None, :]).astype(np.float32)
    return np.ascontiguousarray(np.cos(ang.astype(np.float64)).T.astype(np.float32)), np.ascontiguousarray(np.sin(ang.astype(np.float64)).T.astype(np.float32))
def decay_tables(hp):
    Hs = [2 * hp, 2 * hp + 1]
    lg = [float(np.log1p(-np.float32(2.0) ** np.float32(-5.0 - H))) for H in Hs]
    pos = np.arange(128, dtype=np.float64)
    maskR = np.zeros((128, 2, 128), np.float32); qdT = np.zeros((128, 2, 128), np.float32); kdec = np.zeros((128, 2), np.float32); cd = []
    for i, l in enumerate(lg):
        rel = pos[None, :] - pos[:, None]
        maskR[:, i, :] = np.where(rel >= 0, np.exp(l * np.maximum(rel, 0)), 0.0)
        qdT[:, i, :] = np.exp(l * (pos + 1.0))[None, :]
        kdec[:, i] = np.exp(l * (127.0 - pos))
        cd.append(float(np.exp(l * 128.0)))
    return maskR, qdT, kdec, cd
def w1_core(Wodd, hp):
    Hs = [2 * hp, 2 * hp + 1]
    cols = []
    for base in (0, 1024):
        for H in Hs:
            cols.append(Wodd[:, base + H * 256: base + (H + 1) * 256][:, 0::2])
            cols.append(Wodd[:, base + H * 256: base + (H + 1) * 256][:, 1::2])
    for base in (2048, 4096):
        for H in Hs:
            cols.append(Wodd[:, base + H * 512: base + (H + 1) * 512])
    return np.ascontiguousarray(np.concatenate(cols, axis=1))


from concourse.bass_utils import run_bass_kernel_spmd

NCORES = 8
GROUPS = [[0, 1], [2, 3], [4, 5], [6, 7]]


def _dt(a):
    return BF if a.dtype == bf else F32


def _core_inputs(inputs, core):
    cc = consts_common()
    b, r = core // 2, core % 2
    W = inputs["even_w_in"][0]
    hs = slice(r * 512, (r + 1) * 512)
    o = 4112
    parts = [W[:, 0:1024][:, hs], W[:, 1024:2048][:, hs], W[:, 3072:4096][:, hs],
             W[:, o:o + 1024][:, hs], W[:, o + 1024:o + 2048][:, hs], W[:, o + 3072:o + 4096][:, hs],
             W[:, 2048:3072][:, hs], W[:, o + 2048:o + 3072][:, hs], W[:, 4096 + r * 8:4096 + (r + 1) * 8]]
    perm = []
    for job in range(16):
        for rk in range(2):
            base = (rk * 8 + job) * 64 if job < 8 else 1024 + (rk * 8 + job - 8) * 64
            perm.extend(range(base, base + 64))
    perm = np.array(perm)
    cosT, sinT = rot_tables()
    maskR, qdT, kdec, cd = decay_tables(r)
    x = inputs["x"][b]
    return {"x": np.ascontiguousarray(x),
            "w": np.ascontiguousarray(np.concatenate(parts, axis=1)),
            "gn": np.ascontiguousarray(inputs["even_norm"][0].reshape(8, 128).T),
            "bfv": np.ascontiguousarray(inputs["even_b_f"][0][r * 8:(r + 1) * 8].reshape(8, 1)),
            "ident": cc["ident"], "maskD": cc["maskD"],
            "wo0": np.ascontiguousarray(inputs["even_w_out"][0][perm][:, hs]),
            "xres0": np.ascontiguousarray(x[:, hs]),
            "w1": w1_core(inputs["odd_w_in"][0], r),
            "gn1": np.ascontiguousarray(inputs["odd_norm"][0].reshape(8, 128).T),
            "cosT": cosT, "sinT": sinT, "maskR": maskR, "qdT": qdT, "kdec": kdec,
            "cdv": np.array(cd, np.float32).reshape(1, 2),
            "wo1": np.ascontiguousarray(inputs["odd_w_out"][0][:, hs]),
            "gfin": np.ascontiguousarray(inputs["final_norm"][hs].reshape(1, 512))}


def _build(sample):
    nc = bass.Bass("TRN2", target_bir_lowering=False)
    d = {k: nc.dram_tensor(k, list(v.shape), _dt(v), kind="ExternalInput").ap() for k, v in sample.items()}
    out = nc.dram_tensor("out", [S, 512], F32, kind="ExternalOutput").ap()
    scr = lambda n, shp, dt: nc.dram_tensor(n, shp, dt).ap()
    featT = scr("featT", [NFEAT, S], BF)
    vtok = scr("vtok", [S, 1024], BF)
    caug = scr("caug", [8, 6, S], BF)
    yT = scr("yT", [1024, S], BF)
    yAll = scr("yAll", [2048, S], BF)
    x1c = scr("x1c", [S, 512], F32)
    ssl1 = scr("ssl1", [128, 64], F32)
    ssa1 = scr("ssa1", [256, 64], F32)
    xnblk = scr("xnblk", [16, 512, 512], BF)
    xnAllb = scr("xnAllb", [16, 1024, 512], BF)
    y2blk = scr("y2blk", [16, 1024, 512], BF)
    y2Allb = scr("y2Allb", [16, 2048, 512], BF)
    x2c = scr("x2c", [S, 512], F32)
    ssl2 = scr("ssl2", [128, 64], F32)
    ssa2 = scr("ssa2", [256, 64], F32)
    sc = Sched(nc)
    build_proj_phase(nc, sc, d["x"], d["w"], d["gn"], d["bfv"], d["ident"], featT, vtok, caug)
    build_attn_phase(nc, sc, featT, vtok, caug, d["ident"], d["maskD"], yT,
                     on_job_done=lambda job: sc.coll(yT[job * 64:(job + 1) * 64, :], yAll[job * 128:(job + 1) * 128, :], GROUPS,
                                                     reads=[("yTd", job)]))
    build_outproj_p1(nc, sc, lambda tb: yAll[:, tb * 512:(tb + 1) * 512], d["wo0"], d["xres0"], x1c, ssl1)
    sc.coll(ssl1, ssa1, GROUPS, writes=["ssall"])
    build_outproj_p2(nc, sc, x1c, ssa1, d["ident"], None, None, lambda tb: xnblk[tb], False,
                     on_block_done=lambda tb: sc.coll(xnblk[tb], xnAllb[tb], GROUPS, reads=[("xnd", tb)]))
    build_retention(nc, sc, lambda tb: xnAllb[tb], d["w1"], d["gn1"], d["ident"], d["cosT"], d["sinT"], d["maskR"], d["qdT"], d["kdec"],
                    d["cdv"], lambda tb, c: y2blk[tb][:, c * 128:(c + 1) * 128],
                    on_block_done=lambda tb: sc.coll(y2blk[tb], y2Allb[tb], GROUPS, reads=[("y2d", tb)]))
    build_outproj_p1(nc, sc, lambda tb: y2Allb[tb], d["wo1"], x1c, x2c, ssl2)
    sc.coll(ssl2, ssa2, GROUPS, writes=["ssall"])
    build_outproj_p2(nc, sc, x2c, ssa2, d["ident"], d["gfin"], out, None, True)
    return nc


def kernel(**inputs):
    inputs = {k: np.asarray(v) for k, v in inputs.items()}
    maps = [_core_inputs(inputs, c) for c in range(NCORES)]
    nc = _build(maps[0])
    res = run_bass_kernel_spmd(nc, maps, core_ids=list(range(NCORES))).results
    out = np.empty((4, S, 1024), np.float32)
    for core in range(NCORES):
        b, r = core // 2, core % 2
        out[b, :, r * 512:(r + 1) * 512] = np.asarray(res[core]["out"])
    return out
```

```python
import concourse.bass as bass
import concourse.mybir as mybir

ENGS = ("pe", "act", "dve", "pool", "sp")
DMA_POOL = 12


class Sched:
    def __init__(self, nc, same_engine_sync=True):
        self.nc = nc
        self.q = {e: [] for e in ENGS}
        self.cnt = {e: 0 for e in ENGS}
        self.sem = {e: nc.alloc_semaphore(f"sq_{e}") for e in ENGS if e != "sp"}
        self.dsem = {}
        for e in ("sp", "pool", "act"):
            self.dsem[e] = [nc.alloc_semaphore(f"sd_{e}{i}") for i in range(DMA_POOL)]
        self.dcnt = {e: [0] * DMA_POOL for e in self.dsem}
        self.drr = {e: 0 for e in self.dsem}
        self.waited = {e: {} for e in ENGS}
        self.res = {}
        self.semobj = {}
        self.same = same_engine_sync
        self.nwaits = 0
        self.clear_sems()

    def all_sems(self):
        return list(self.sem.values()) + [s for e in self.dsem for s in self.dsem[e]]

    def clear_sems(self):
        nc = self.nc
        sems = self.all_sems()
        with nc.Block() as block:
            def body(g):
                for s in sems:
                    g.sem_clear(s)
            block.gpsimd(body)

    def _deps(self, eng, reads, writes):
        deps = {}

        def add(tok):
            if tok is None:
                return
            k, v = tok
            if deps.get(k, 0) < v:
                deps[k] = v

        for r in reads:
            st = self.res.get(r)
            if st is not None:
                add(st["w"])
        for w in writes:
            st = self.res.get(w)
            if st is not None:
                add(st["w"])
                for k, v in st["r"].items():
                    add((k, v))
        return deps

    def _commit(self, tok, reads, writes):
        for r in reads:
            st = self.res.setdefault(r, {"w": None, "r": {}})
            k, v = tok
            if st["r"].get(k, 0) < v:
                st["r"][k] = v
        for w in writes:
            self.res[w] = {"w": tok, "r": {}}

    def _waits(self, eng, deps, own_key):
        ws = []
        for k, v in deps.items():
            if k == own_key and (eng == "pe" or not self.same):
                continue
            if self.waited[eng].get(k, 0) >= v:
                continue
            self.waited[eng][k] = v
            ws.append((k, v))
        self.nwaits += len(ws)
        return ws

    def op(self, eng, fn, reads=(), writes=()):
        deps = self._deps(eng, reads, writes)
        sem = self.sem[eng]
        key = "q_" + eng
        self.semobj[key] = sem
        ws = self._waits(eng, deps, key)
        self.cnt[eng] += 1
        tok = (key, self.cnt[eng])
        self.q[eng].append((fn, ws, (sem, 1, self.cnt[eng])))
        self._commit(tok, reads, writes)
        return tok

    def dma(self, eng, fn, reads=(), writes=()):
        deps = self._deps(eng, reads, writes)
        i = self.drr[eng]
        self.drr[eng] = (i + 1) % DMA_POOL
        sem = self.dsem[eng][i]
        key = f"d_{eng}{i}"
        self.semobj[key] = sem
        prev = self.dcnt[eng][i]
        if prev > 0:
            deps[key] = max(deps.get(key, 0), 16 * prev)
        ws = self._waits(eng, deps, None)
        self.dcnt[eng][i] += 1
        tok = (key, 16 * self.dcnt[eng][i])
        self.q[eng].append((fn, ws, (sem, 16)))
        self._commit(tok, reads, writes)
        return tok

    def coll(self, src_ap, dst_ap, groups, reads=(), writes=()):
        nc = self.nc
        if not hasattr(self, "cc_toks"):
            self.cc_toks = []
        n = len(self.cc_toks)
        sem = nc.alloc_semaphore(f"ccs{n}")
        key = f"cc{n}"
        self.semobj[key] = sem
        deps = self._deps("pool", reads, writes)
        if self.cc_toks:
            pk, pv = self.cc_toks[-1]
            deps[pk] = max(deps.get(pk, 0), pv)
        ws = self._waits("pool", deps, None)
        tok = (key, 1)
        fn = lambda g: g.collective_compute("AllGather", mybir.AluOpType.bypass, replica_groups=groups,
                                            ins=[src_ap.opt()], outs=[dst_ap.opt()])
        self.q["pool"].append((fn, ws, (sem, None)))
        self._commit(tok, reads, writes)
        self.cc_toks.append(tok)
        return tok

    def barrier(self):
        toks = {}
        for e in ENGS:
            if e != "sp" and self.cnt[e] > 0:
                toks["q_" + e] = self.cnt[e]
        for e in self.dsem:
            for i in range(DMA_POOL):
                if self.dcnt[e][i] > 0:
                    toks[f"d_{e}{i}"] = 16 * self.dcnt[e][i]
        for k, v in getattr(self, "cc_toks", []):
            toks[k] = v
        for e in ENGS:
            ws = []
            for k, v in toks.items():
                if self.waited[e].get(k, 0) >= v:
                    continue
                self.waited[e][k] = v
                ws.append((k, v))
            if ws:
                self.q[e].append((None, ws, None))

    def emit(self):
        nc = self.nc
        self.barrier()
        if not any(self.q[e] for e in ENGS):
            return
        handles = {"pe": "tensor", "act": "scalar", "dve": "vector", "pool": "gpsimd", "sp": "sync"}
        if not hasattr(self, "base"):
            self.base = {}
        needed = {}
        for e in ENGS:
            for fn, ws, inc in self.q[e]:
                for k, v in ws:
                    if k.startswith("q_"):
                        needed.setdefault(k, set()).add(v)
        valmap = {}
        for k, idxs in needed.items():
            b = self.base.get(k, 0)
            for r, idx in enumerate(sorted(idxs)):
                valmap[(k, idx)] = b + r + 1
            self.base[k] = b + len(idxs)
        with nc.Block() as block:
            for e in ENGS:
                items = self.q[e]
                if not items:
                    continue

                def body(engh, items=items, e=e):
                    for fn, ws, inc in items:
                        for k, v in ws:
                            if k.startswith("q_"):
                                engh.wait_ge(self.semobj[k], valmap[(k, v)])
                            else:
                                engh.wait_ge(self.semobj[k], v)
                        if fn is not None:
                            inst = fn(engh)
                            if len(inc) == 3:
                                if ("q_" + e, inc[2]) in valmap:
                                    inst.then_inc(inc[0], 1)
                            elif inc[1] is None:
                                inst.then_inc(inc[0])
                            else:
                                inst.then_inc(inc[0], inc[1])

                getattr(block, handles[e])(body)
        self.q = {e: [] for e in ENGS}
        self.res = {}
import numpy as np
import concourse.bass as bass
import concourse.mybir as mybir
from contextlib import ExitStack

F32 = mybir.dt.float32
BF = mybir.dt.bfloat16
AF = mybir.ActivationFunctionType
ALU = mybir.AluOpType
AX = mybir.AxisListType

S = 8192
_UID = [0]
NTB = S // 512
NFEAT = 3072
NV = 1024
NW = NFEAT + NV + 8
MASKNEG = -30000.0


def build_proj_phase(nc, sc, x, w, gn, bfv, ident, featT, vtok, caug, ntb=NTB):
    with ExitStack() as es:
        Wb = es.enter_context(nc.sbuf_tensor("Wb_pa", [128, 8, NW], BF))
        gn_t = es.enter_context(nc.sbuf_tensor("gn_t_pa", [128, 8], F32))
        id_t = es.enter_context(nc.sbuf_tensor("id_t_pa", [128, 128], BF))
        eps_t = es.enter_context(nc.sbuf_tensor("eps_t_pa", [128, 1], F32))
        one_t = es.enter_context(nc.sbuf_tensor("one_t_pa", [128, 1], F32))
        nb_t = es.enter_context(nc.sbuf_tensor("nb_t_pa", [8, 1], F32))
        ones8 = es.enter_context(nc.sbuf_tensor("ones8_pa", [8, 512], F32))
        xb0 = es.enter_context(nc.sbuf_tensor("xb0_pa", [128, NW], F32))
        xb1 = es.enter_context(nc.sbuf_tensor("xb1_pa", [128, NW], F32))
        junk = es.enter_context(nc.sbuf_tensor("junk_pa", [128, 1024], F32))
        ss = es.enter_context(nc.sbuf_tensor("ss_pa", [128, 8], F32))
        xn0 = es.enter_context(nc.sbuf_tensor("xn0_pa", [128, 1024], BF))
        xn1 = es.enter_context(nc.sbuf_tensor("xn1_pa", [128, 1024], BF))
        xT0 = es.enter_context(nc.sbuf_tensor("xT0_pa", [128, 8, 512], BF))
        xT1 = es.enter_context(nc.sbuf_tensor("xT1_pa", [128, 8, 512], BF))
        stF = es.enter_context(nc.sbuf_tensor("stF_pa", [128, 8, 512], BF))
        stV = es.enter_context(nc.sbuf_tensor("stV_pa", [128, 4, 512], BF))
        fw = es.enter_context(nc.sbuf_tensor("fw_pa", [8, 2, 6, 512], F32))
        cs = es.enter_context(nc.sbuf_tensor("cs_pa", [8, 2, 6, 512], BF))
        psT0 = es.enter_context(nc.psum_tensor("psT0_pa", [128, 8, 128], BF))
        psT1 = es.enter_context(nc.psum_tensor("psT1_pa", [128, 8, 128], BF))
        psF0 = es.enter_context(nc.psum_tensor("psF0_pa", [128, 512], F32))
        psF1 = es.enter_context(nc.psum_tensor("psF1_pa", [128, 512], F32))
        psF2 = es.enter_context(nc.psum_tensor("psF2_pa", [128, 512], F32))
        psV0 = es.enter_context(nc.psum_tensor("psV0_pa", [128, 512], F32))
        psV1 = es.enter_context(nc.psum_tensor("psV1_pa", [128, 512], F32))
        psf = es.enter_context(nc.psum_tensor("psf_pa", [8, 512], F32))
        xb = [xb0, xb1]
        xn = [xn0, xn1]
        xT = [xT0, xT1]
        psT = [psT0, psT1]
        psF = [psF0, psF1, psF2]
        psV = [psV0, psV1]
        sc.dma("sp", lambda e: e.dma_start(out=gn_t[:], in_=gn), writes=["gn_t"])
        sc.dma("sp", lambda e: e.dma_start(out=id_t[:], in_=ident), writes=["id_t"])
        sc.dma("sp", lambda e: e.dma_start(out=nb_t[:], in_=bfv), writes=["nb_t"])
        sc.op("dve", lambda e: e.memset(eps_t[:], 1e-6), writes=["eps_t"])
        sc.op("dve", lambda e: e.memset(one_t[:], 1.0), writes=["one_t"])
        sc.op("dve", lambda e: e.memset(ones8[:], 1.0), writes=["ones8"])
        sc.op("dve", lambda e: e.tensor_scalar(out=nb_t[:], in0=nb_t[:], scalar1=-1.0, scalar2=None, op0=ALU.mult),
              reads=["nb_t"], writes=["nb_t"])
        for kc in range(8):
            s = kc % 2
            sc.dma("sp" if s == 0 else "pool",
                   lambda e, kc=kc, s=s: e.dma_start(out=xb[s][:, :], in_=w[kc * 128:(kc + 1) * 128, :]),
                   writes=[("xb", s)])
            eng = "dve" if s == 0 else "pool"
            sc.op(eng, lambda e, kc=kc, s=s: e.tensor_scalar(
                out=Wb[:, kc, :], in0=xb[s][:, :], scalar1=gn_t[:, kc:kc + 1], scalar2=None, op0=ALU.mult),
                reads=[("xb", s), "gn_t"], writes=[("Wb", kc)])
        WbR = [("Wb", kc) for kc in range(8)]
        evac_box = [0]

        def load_x(tb):
            s = tb % 2
            xv = xb[s][:, 0:4096].rearrange("p (j f) -> p j f", j=4)
            src = x[tb * 512:(tb + 1) * 512, :].rearrange("(j p) f -> p j f", p=128)
            sc.dma("sp", lambda e, xv=xv, src=src: e.dma_start(out=xv, in_=src), writes=[("xb", s)])

        def norm_tile(tb, j):
            s = tb % 2
            xv = xb[s][:, 0:4096].rearrange("p (j f) -> p j f", j=4)
            xs = (tb * 4 + j) % 2
            col = (tb * 4 + j) % 8
            sc.op("act", lambda e: e.activation(out=junk[:], in_=xv[:, j, :], func=AF.Square),
                  reads=[("xb", s)], writes=["junk"])
            sc.op("dve", lambda e: e.reduce_sum(out=ss[:, col:col + 1], in_=junk[:], axis=AX.X),
                  reads=["junk"], writes=[("ss", col)])
            sc.op("act", lambda e: e.activation(out=ss[:, col:col + 1], in_=ss[:, col:col + 1], func=AF.Sqrt,
                                                bias=eps_t[:, 0:1], scale=1.0 / 1024.0),
                  reads=[("ss", col), "eps_t"], writes=[("ss", col)])
            sc.op("dve", lambda e: e.reciprocal(out=ss[:, col:col + 1], in_=ss[:, col:col + 1]),
                  reads=[("ss", col)], writes=[("ss", col)])
            sc.op("dve", lambda e: e.tensor_scalar(out=xn[xs][:], in0=xv[:, j, :], scalar1=ss[:, col:col + 1], scalar2=None, op0=ALU.mult),
                  reads=[("xb", s), ("ss", col)], writes=[("xn", xs)])

        def transp_tile(tb, j):
            s = tb % 2
            xs = (tb * 4 + j) % 2
            for kc in range(8):
                sc.op("pe", lambda e, kc=kc: e.transpose(out=psT[xs][:, kc, :], in_=xn[xs][:, kc * 128:(kc + 1) * 128], identity=id_t[:]),
                      reads=[("xn", xs), "id_t"], writes=[("psT", xs)])
            sc.op("act", lambda e: e.copy(out=xT[s][:, :, j * 128:(j + 1) * 128], in_=psT[xs][:]),
                  reads=[("psT", xs)], writes=[("xT", s, j)])

        def feat_group(tb, cg):
            s = tb % 2
            pb = cg % 3
            for kc in range(8):
                sc.op("pe", lambda e, kc=kc: e.matmul(
                    psF[pb][:], lhsT=Wb[:, kc, cg * 128:(cg + 1) * 128], rhs=xT[s][:, kc, :],
                    start=(kc == 0), stop=(kc == 7)),
                    reads=[("xT", s, jj) for jj in range(4)] + [("Wb", kc)], writes=[("psF", pb)])
            sl = evac_box[0] % 8
            evac_box[0] += 1
            kind = (cg // 4) % 3
            if kind == 0:
                sc.op("act", lambda e: e.activation(out=stF[:, sl, :], in_=psF[pb][:], func=AF.Copy, scale=0.125),
                      reads=[("psF", pb)], writes=[("stF", sl)])
            elif kind == 1:
                sc.op("dve", lambda e: e.tensor_copy(out=stF[:, sl, :], in_=psF[pb][:]),
                      reads=[("psF", pb)], writes=[("stF", sl)])
            else:
                sc.op("act", lambda e: e.activation(out=stF[:, sl, :], in_=psF[pb][:], func=AF.Silu),
                      reads=[("psF", pb)], writes=[("stF", sl)])
            sc.dma("sp" if sl % 2 == 0 else "act", lambda e: e.dma_start(out=featT[cg * 128:(cg + 1) * 128, tb * 512:(tb + 1) * 512], in_=stF[:, sl, :]),
                   reads=[("stF", sl)])

        def v_group(tb, j, half):
            s = tb % 2
            pv = (j * 2 + half) % 2
            for kc in range(8):
                sc.op("pe", lambda e, kc=kc: e.matmul(
                    psV[pv][:], lhsT=xT[s][:, kc, j * 128:(j + 1) * 128],
                    rhs=Wb[:, kc, NFEAT + half * 512:NFEAT + (half + 1) * 512],
                    start=(kc == 0), stop=(kc == 7)),
                    reads=[("xT", s, j), ("Wb", kc)], writes=[("psV", pv)])
            sl = (j * 2 + half) % 4
            sc.op("dve", lambda e: e.tensor_copy(out=stV[:, sl, :], in_=psV[pv][:]),
                  reads=[("psV", pv)], writes=[("stV", sl)])
            r0 = tb * 512 + j * 128
            sc.dma("pool", lambda e: e.dma_start(out=vtok[r0:r0 + 128, half * 512:(half + 1) * 512], in_=stV[:, sl, :]),
                   reads=[("stV", sl)])

        load_x(0)
        if ntb > 1:
            load_x(1)
        for j in range(4):
            norm_tile(0, j)
            transp_tile(0, j)
        for tb in range(ntb):
            s = tb % 2
            nxt = tb + 1 < ntb
            if tb + 2 < ntb:
                load_x(tb + 2)
            groups = [(lambda cg=cg: feat_group(tb, cg)) for cg in range(NFEAT // 128)]
            groups += [(lambda j=j, half=half: v_group(tb, j, half)) for j in range(4) for half in range(2)]
            for gi, g in enumerate(groups):
                g()
                if nxt and gi % 8 == 1:
                    norm_tile(tb + 1, gi // 8)
                if nxt and gi % 8 == 5:
                    transp_tile(tb + 1, gi // 8)
            for kc in range(8):
                sc.op("pe", lambda e, kc=kc, s=s: e.matmul(
                    psf[:], lhsT=Wb[:, kc, NFEAT + NV:NFEAT + NV + 8], rhs=xT[s][:, kc, :],
                    start=(kc == 0), stop=(kc == 7)),
                    reads=[("xT", s, jj) for jj in range(4)] + [("Wb", kc)], writes=["psf"])
            fs = tb % 2
            fwv = fw[:, fs]
            csv = cs[:, fs]
            R_fw = ("fw", fs)
            sc.op("act", lambda e, fwv=fwv: e.activation(out=fwv[:, 0, :], in_=psf[:], func=AF.Exp, bias=nb_t[:, 0:1], scale=-1.0),
                  reads=["psf", "nb_t"], writes=[R_fw])
            sc.op("act", lambda e, fwv=fwv: e.activation(out=fwv[:, 1, :], in_=fwv[:, 0, :], func=AF.Ln, bias=one_t[0:8, 0:1], scale=1.0),
                  reads=[R_fw, "one_t"], writes=[R_fw])
            if tb == 0:
                init = 0.0
                rd = [R_fw, "ones8"]
            else:
                init = fw[:, 1 - fs, 2, 511:512]
                rd = [R_fw, "ones8", ("fw", 1 - fs)]
            sc.op("dve", lambda e, fwv=fwv, init=init: e.tensor_tensor_scan(
                out=fwv[:, 2, :], data0=ones8[:], data1=fwv[:, 1, :], initial=init, op0=ALU.mult, op1=ALU.subtract),
                reads=rd, writes=[R_fw])
            R_cs = ("cs", fs)
            sc.op("dve", lambda e, fwv=fwv, csv=csv: e.tensor_copy(out=csv[:, 0, :], in_=fwv[:, 2, :]), reads=[R_fw], writes=[R_cs])
            sc.op("dve", lambda e, fwv=fwv, csv=csv: e.tensor_tensor(out=fwv[:, 3, :], in0=fwv[:, 2, :], in1=csv[:, 0, :], op=ALU.subtract),
                  reads=[R_fw, R_cs], writes=[R_fw])
            sc.op("dve", lambda e, fwv=fwv, csv=csv: e.tensor_copy(out=csv[:, 1, :], in_=fwv[:, 3, :]), reads=[R_fw], writes=[R_cs])
            sc.op("dve", lambda e, fwv=fwv, csv=csv: e.tensor_tensor(out=fwv[:, 4, :], in0=fwv[:, 3, :], in1=csv[:, 1, :], op=ALU.subtract),
                  reads=[R_fw, R_cs], writes=[R_fw])
            sc.op("dve", lambda e, fwv=fwv, csv=csv: e.tensor_copy(out=csv[:, 2, :], in_=fwv[:, 4, :]), reads=[R_fw], writes=[R_cs])
            sc.op("dve", lambda e, csv=csv: e.tensor_scalar(out=csv[:, 3:6, :], in0=csv[:, 0:3, :], scalar1=-1.0, scalar2=None, op0=ALU.mult),
                  reads=[R_cs], writes=[R_cs])
            sc.dma("sp", lambda e, csv=csv, tb=tb: e.dma_start(out=caug[:, :, tb * 512:(tb + 1) * 512], in_=csv),
                   reads=[R_cs])
        sc.emit()


def dram_ap(ap, offset, dims):
    return bass.AP(ap.tensor, offset, [list(d) for d in dims])


def build_attn_phase(nc, sc, featT, vtok, caug, ident, maskD, yT, fox_heads=range(8), dil_heads=range(8), nqc=16, on_job_done=None):
    with ExitStack() as es:
        _UID[0] += 1
        _u = _UID[0]
        sb = lambda n, s, d: es.enter_context(nc.sbuf_tensor(f"{n}_u{_u}", s, d))
        ps = lambda n, s, d: es.enter_context(nc.psum_tensor(f"{n}_u{_u}", s, d))
        Qa = [sb(f"Qa{i}", [70, S], BF) for i in range(2)]
        Ka = [sb(f"Ka{i}", [70, S], BF) for i in range(2)]
        Vt = [sb(f"Vt{i}", [128, 3, 64, 65], BF) for i in range(2)]
        acc = sb("acc", [65, S], F32)
        pT2 = [sb(f"pT{i}", [128, 1024], BF) for i in range(2)]
        pT = [pT2[0][:, 0:512], pT2[0][:, 512:1024], pT2[1][:, 0:512], pT2[1][:, 512:1024]]
        Gc = [sb(f"Gc{i}", [64, 512], BF) for i in range(3)]
        rec = [sb(f"rec{i}", [65, 512], F32) for i in range(3)]
        bcs = [sb(f"bcs{i}", [64, 512], F32) for i in range(3)]
        obt = [sb(f"obt{i}", [64, 512], F32) for i in range(3)]
        ysb = [sb(f"ysb{i}", [64, 512], BF) for i in range(3)]
        ones_b = sb("ones_b", [65, 64], BF)
        rhi = [sb(f"rhi{i}", [65, 512], BF) for i in range(3)]
        rlo = [sb(f"rlo{i}", [65, 512], BF) for i in range(3)]
        id_t = sb("id_t2", [128, 128], BF)
        mk_t = sb("mk_t", [128, 256], BF)
        mk_m = sb("mk_m", [128, 256], BF)
        psS2 = [ps(f"psS{i}", [128, 1024], F32) for i in range(2)]
        psS = [psS2[0][:, 0:512], psS2[0][:, 512:1024], psS2[1][:, 0:512], psS2[1][:, 512:1024]]
        psO = [ps(f"psO{i}", [128, 512], F32) for i in range(2)]
        psB = ps("psB", [64, 512], F32)

        sc.dma("sp", lambda e: e.dma_start(out=id_t[:], in_=ident), writes=["id_t"])
        sc.dma("sp", lambda e: e.dma_start(out=mk_t[:], in_=maskD), writes=["mk_t"])
        sc.op("dve", lambda e: e.memset(ones_b[:], 1.0), writes=["ones_b"])
        sc.op("dve", lambda e: e.tensor_scalar(out=mk_m[:], in0=mk_t[:], scalar1=0.0, scalar2=None, op0=ALU.is_equal),
              reads=["mk_t"], writes=["mk_m"])
        for i in range(2):
            sc.op("dve", lambda e, i=i: e.memset(Qa[i][64:70, :], 1.0), writes=[("Qa", i)])
            sc.op("dve", lambda e, i=i: e.memset(Ka[i][64:70, :], 1.0), writes=[("Ka", i)])
            sc.op("pool", lambda e, i=i: e.memset(Vt[i][:, :, :, 64:65], 1.0), writes=[("Vt", i)])

        jobs = [("fox", h) for h in fox_heads] + [("dil", h) for h in dil_heads]

        def load(ji):
            kind, h = jobs[ji]
            s = ji % 2
            if kind == "fox":
                qrow, krow, vcol = h * 64, 512 + h * 64, h * 64
            else:
                qrow, krow, vcol = 1536 + h * 64, 2048 + h * 64, 512 + h * 64
            sc.dma("sp", lambda e: e.dma_start(out=Qa[s][0:64, :], in_=featT[qrow:qrow + 64, :]), writes=[("Qa", s)])
            sc.dma("sp", lambda e: e.dma_start(out=Ka[s][0:64, :], in_=featT[krow:krow + 64, :]), writes=[("Ka", s)])
            if kind == "fox":
                sc.dma("sp", lambda e: e.dma_start(out=Qa[s][64:67, :], in_=caug[h, 0:3, :]), writes=[("Qa", s)])
                sc.dma("sp", lambda e: e.dma_start(out=Ka[s][67:70, :], in_=caug[h, 3:6, :]), writes=[("Ka", s)])
                pats = [(0, 1)]
            else:
                pats = [(0, 1), (1, 4), (2, 16)]
            for pi, r in pats:
                nblk = S // r // 128
                for s_ in range(r):
                    step = 16 if nblk >= 16 else nblk
                    for j0 in range(0, nblk, step):
                        src = dram_ap(vtok, (s_ + r * 128 * j0) * 1024 + vcol,
                                      [[r * 1024, 128], [r * 128 * 1024, step], [1, 64]])
                        t0 = s_ * nblk + j0
                        sc.dma("sp", lambda e, src=src, pi=pi, t0=t0, step=step: e.dma_start(
                            out=Vt[s][:, pi, t0:t0 + step, 0:64], in_=src), writes=[("Vt", s)])

        NS = 3
        pending = []
        slot_ctr = [0]
        tick = [0]

        def normalize(kind, h, qc, src_ps, ob, job=0):
            pending.append({"state": 0, "kind": kind, "h": h, "qc": qc, "src_ps": src_ps, "ob": ob, "job": job})

        def _srcs(t):
            cols = slice(t["qc"] * 512, (t["qc"] + 1) * 512)
            if t["src_ps"]:
                return psO[t["ob"]][64:65, :], psO[t["ob"]][0:64, :], [("psO", t["ob"])], cols
            return acc[64:65, cols], acc[0:64, cols], [("acc", t["qc"])], cols

        def stage1(t):
            rs = slot_ctr[0] % NS
            slot_ctr[0] += 1
            t["rs"] = rs
            den, num, rd, cols = _srcs(t)
            grow = (1024 if t["kind"] == "fox" else 2560) + t["h"] * 64
            sc.dma("act", lambda e: e.dma_start(out=Gc[rs][:], in_=featT[grow:grow + 64, cols]), writes=[("Gc", rs)])
            sc.op("dve", lambda e: e.reciprocal(out=rec[rs][64:65, :], in_=den), reads=rd, writes=[("rec", rs)])
            sc.op("dve", lambda e: e.tensor_copy(out=rhi[rs][64:65, :], in_=rec[rs][64:65, :]), reads=[("rec", rs)], writes=[("rhi", rs)])
            sc.op("dve", lambda e: e.tensor_tensor(out=rlo[rs][64:65, :], in0=rec[rs][64:65, :], in1=rhi[rs][64:65, :], op=ALU.subtract),
                  reads=[("rec", rs), ("rhi", rs)], writes=[("rlo", rs)])
            t["state"] = 1
            t["t_issue"] = tick[0]

        def stage2(t):
            rs = t["rs"]
            den, num, rd, cols = _srcs(t)
            yrow = (0 if t["kind"] == "fox" else 512) + t["h"] * 64
            job = t["job"]
            sc.op("pe", lambda e: e.matmul(psB[:], lhsT=ones_b[64:65, 0:64], rhs=rhi[rs][64:65, :], start=True, stop=False),
                  reads=[("rhi", rs), "ones_b"], writes=["psB"])
            sc.op("pe", lambda e: e.matmul(psB[:], lhsT=ones_b[64:65, 0:64], rhs=rlo[rs][64:65, :], start=False, stop=True),
                  reads=[("rlo", rs), "ones_b"], writes=["psB"])
            if t["src_ps"]:
                sc.op("dve", lambda e: e.tensor_copy(out=bcs[rs][:], in_=psB[:]), reads=["psB"], writes=[("bcs", rs)])
                sc.op("dve", lambda e: e.tensor_tensor(out=obt[rs][:], in0=num, in1=bcs[rs][:], op=ALU.mult),
                      reads=rd + [("bcs", rs)], writes=[("obt", rs)])
            else:
                sc.op("dve", lambda e: e.tensor_tensor(out=obt[rs][:], in0=num, in1=psB[:], op=ALU.mult),
                      reads=rd + ["psB"], writes=[("obt", rs)])
            sc.op("pool", lambda e: e.tensor_tensor(out=ysb[rs][:], in0=obt[rs][:], in1=Gc[rs][:], op=ALU.mult),
                  reads=[("obt", rs), ("Gc", rs)], writes=[("ysb", rs)])
            sc.dma("pool", lambda e: e.dma_start(out=yT[yrow:yrow + 64, cols], in_=ysb[rs][:]), reads=[("ysb", rs)], writes=[("yTd", job)])
            pending.remove(t)

        def pump():
            tick[0] += 1
            for t in list(pending):
                if t["state"] == 1:
                    if tick[0] - t["t_issue"] >= 6:
                        stage2(t)
                    break
            if sum(1 for t in pending if t["state"] == 1) < NS:
                for t in pending:
                    if t["state"] == 0:
                        stage1(t)
                        break

        def flush(pred=lambda t: True):
            for t in list(pending):
                if pred(t):
                    for u in list(pending):
                        if u is t:
                            break
                        if u["state"] == 1:
                            stage2(u)
                    if t["state"] == 0:
                        stage1(t)
                    stage2(t)

        tile_ctr = [0]

        def run_tiles(tiles):
            LA = 3
            base = tile_ctr[0]
            n = len(tiles)
            for i in range(min(LA, n)):
                tiles[i]["s_fn"]((base + i) % 4)
            for i in range(n):
                if i + LA < n:
                    tiles[i + LA]["s_fn"]((base + i + LA) % 4)
                b = (base + i) % 4
                c0, c1 = tiles[i]["cr"]
                sc.op("act", lambda e, b=b, c0=c0, c1=c1: e.activation(out=pT[b][:, c0:c1], in_=psS[b][:, c0:c1], func=AF.Exp),
                      reads=[("psS", b)], writes=[("pT", b)])
                if "mask" in tiles[i]:
                    m0 = tiles[i]["mask"]
                    sc.op("dve", lambda e, b=b, c0=c0, c1=c1, m0=m0: e.tensor_tensor(
                        out=pT[b][:, c0:c1], in0=pT[b][:, c0:c1], in1=mk_m[:, m0:m0 + (c1 - c0)], op=ALU.mult),
                        reads=[("pT", b), "mk_m"], writes=[("pT", b)])
                tiles[i]["pv_fn"](b)
                if tiles[i].get("post"):
                    tiles[i]["post"]()
                pump()
            tile_ctr[0] = base + n

        ob_box = [0]

        def do_job(ji, kind, h):
            s = ji % 2
            ob_ctr = ob_box[0]
            RQ, RK, RV = ("Qa", s), ("Ka", s), ("Vt", s)
            tiles = []
            if kind == "fox":
                for qc in range(nqc):
                    ob = ob_ctr % 2
                    ob_ctr += 1
                    nk = 4 * qc + 4
                    for kt in range(nk):
                        j = kt - 4 * qc
                        c0 = 128 * max(j, 0)

                        def s_fn(b, kt=kt, qc=qc, j=j, c0=c0):
                            kap = Ka[s][0:70, kt * 128:(kt + 1) * 128]
                            if j < 0:
                                sc.op("pe", lambda e: e.matmul(psS[b][:, 0:512], lhsT=kap, rhs=Qa[s][0:70, qc * 512:(qc + 1) * 512],
                                                               start=True, stop=True),
                                      reads=[RQ, RK], writes=[("psS", b)])
                            else:
                                q0 = qc * 512 + c0
                                sc.op("pe", lambda e: e.matmul(psS[b][:, c0:c0 + 128], lhsT=id_t[:], rhs=mk_t[:, 0:128],
                                                               start=True, stop=False),
                                      reads=["id_t", "mk_t"], writes=[("psS", b)])
                                sc.op("pe", lambda e: e.matmul(psS[b][:, c0:c0 + 128], lhsT=kap, rhs=Qa[s][0:70, q0:q0 + 128],
                                                               start=False, stop=True),
                                      reads=[RQ, RK], writes=[("psS", b)])
                                if c0 + 128 < 512:
                                    sc.op("pe", lambda e: e.matmul(psS[b][:, c0 + 128:512], lhsT=kap,
                                                                   rhs=Qa[s][0:70, q0 + 128:(qc + 1) * 512], start=True, stop=True),
                                          reads=[RQ, RK], writes=[("psS", b)])

                        def pv_fn(b, kt=kt, c0=c0, ob=ob, nk=nk):
                            if kt == 0:
                                flush(lambda t: t["src_ps"] and t["ob"] == ob)
                            sc.op("pe", lambda e: e.matmul(psO[ob][0:65, c0:512], lhsT=Vt[s][:, 0, kt, 0:65], rhs=pT[b][:, c0:512],
                                                           start=(kt == 0), stop=(kt == nk - 1), skip_group_check=True),
                                  reads=[("pT", b), RV], writes=[("psO", ob)])

                        t = {"s_fn": s_fn, "cr": (c0, 512), "pv_fn": pv_fn}
                        if kt == nk - 1:
                            t["post"] = (lambda qc=qc, ob=ob: normalize("fox", h, qc, True, ob, ji))
                        tiles.append(t)
                run_tiles_pairs(tiles)
            else:
                for pi, r in [(0, 1), (1, 4), (2, 16)]:
                    nblk = S // r // 128
                    for s_ in range(r):
                        for c in range(nblk // 4):
                            n0 = 4 * c
                            ob = ob_ctr % 2
                            ob_ctr += 1
                            qbase = s_ + r * 128 * n0
                            js = ([n0 - 1] if n0 > 0 else []) + list(range(n0, n0 + 4))
                            for j in js:
                                b_lo, b_hi = max(j, n0), min(j + 1, n0 + 3)
                                c0, c1 = (b_lo - n0) * 128, (b_hi - n0 + 1) * 128
                                m0 = 0 if b_lo == j else 128

                                def s_fn(b, j=j, c0=c0, c1=c1, m0=m0, r=r, s_=s_, qbase=qbase):
                                    kb = s_ + r * 128 * j
                                    kap = Ka[s][0:64, kb:kb + r * 127 + 1:r]
                                    qap = Qa[s][0:64, qbase + r * c0:qbase + r * (c1 - 1) + 1:r]
                                    sc.op("pe", lambda e: e.matmul(psS[b][:, c0:c1], lhsT=id_t[:], rhs=mk_t[:, m0:m0 + (c1 - c0)],
                                                                   start=True, stop=False),
                                          reads=["id_t", "mk_t"], writes=[("psS", b)])
                                    sc.op("pe", lambda e: e.matmul(psS[b][:, c0:c1], lhsT=kap, rhs=qap, start=False, stop=True),
                                          reads=[RQ, RK], writes=[("psS", b)])

                                def pv_fn(b, j=j, b_lo=b_lo, b_hi=b_hi, n0=n0, ob=ob, pi=pi, s_=s_, nblk=nblk):
                                    flush(lambda t: t["src_ps"] and t["ob"] == ob)
                                    for bb in range(b_lo, b_hi + 1):
                                        cb = (bb - n0) * 128
                                        st = (j == bb - 1) or (bb == 0 and j == 0)
                                        sc.op("pe", lambda e, cb=cb, st=st, bb=bb: e.matmul(
                                            psO[ob][0:65, cb:cb + 128], lhsT=Vt[s][:, pi, s_ * nblk + j, 0:65], rhs=pT[b][:, cb:cb + 128],
                                            start=st, stop=(j == bb), skip_group_check=True),
                                            reads=[("pT", b), RV], writes=[("psO", ob)])

                                t = {"s_fn": s_fn, "cr": (c0, c1), "pv_fn": pv_fn}
                                if j == js[-1]:
                                    def post(ob=ob, pi=pi, r=r, qbase=qbase, c=c):
                                        av = acc[0:65, qbase:qbase + r * 511 + 1:r]
                                        ares = [("acc", k) for k in range(r * c, r * c + r)]
                                        flush(lambda t: (not t["src_ps"]) and t["qc"] in range(r * c, r * c + r))
                                        if pi == 0:
                                            sc.op("dve", lambda e: e.tensor_copy(out=av, in_=psO[ob][0:65, :]),
                                                  reads=[("psO", ob)], writes=ares)
                                        else:
                                            sc.op("dve", lambda e: e.tensor_tensor(out=av, in0=av, in1=psO[ob][0:65, :], op=ALU.add),
                                                  reads=[("psO", ob)] + ares, writes=ares)
                                    t["post"] = post
                                tiles.append(t)
                run_tiles(tiles)
                for qc in range(nqc):
                    normalize("dil", h, qc, False, 0, ji)
            ob_box[0] = ob_ctr

        def run_tiles_pairs(tiles):
            if tile_ctr[0] % 2:
                tile_ctr[0] += 1
            base = tile_ctr[0]
            n = len(tiles)
            pairs = [list(range(i, min(i + 2, n))) for i in range(0, n, 2)]

            def issue_s(p):
                for i in pairs[p]:
                    tiles[i]["s_fn"]((base + i) % 4)
            issue_s(0)
            for p in range(len(pairs)):
                if p + 1 < len(pairs):
                    issue_s(p + 1)
                for _ in pairs[p]:
                    pump()
                idx = pairs[p]
                b2 = ((base + idx[0]) % 4) // 2
                lo = tiles[idx[0]]["cr"][0]
                hi = 512 * (len(idx) - 1) + tiles[idx[-1]]["cr"][1]
                bs = [(base + i) % 4 for i in idx]
                if len(idx) == 2 and tiles[idx[1]]["cr"][0] > 0:
                    for i, b in zip(idx, bs):
                        c0, c1 = tiles[i]["cr"]
                        sc.op("act", lambda e, b=b, c0=c0, c1=c1: e.activation(out=pT[b][:, c0:c1], in_=psS[b][:, c0:c1], func=AF.Exp),
                              reads=[("psS", b)], writes=[("pT", b)])
                else:
                    sc.op("act", lambda e, b2=b2, lo=lo, hi=hi: e.activation(out=pT2[b2][:, lo:hi], in_=psS2[b2][:, lo:hi], func=AF.Exp),
                          reads=[("psS", b) for b in bs], writes=[("pT", b) for b in bs])
                for i in idx:
                    tiles[i]["pv_fn"]((base + i) % 4)
                    if tiles[i].get("post"):
                        tiles[i]["post"]()
            tile_ctr[0] = base + n

        done_box = [0]

        def notify_done():
            while done_box[0] < len(jobs) and done_box[0] < cur_job[0] + 0 and not any(t["job"] == done_box[0] for t in pending):
                if on_job_done is not None:
                    on_job_done(done_box[0])
                done_box[0] += 1

        cur_job = [0]
        _pump0 = pump

        def pump():
            _pump0()
            notify_done()

        load(0)
        for ji, (kind, h) in enumerate(jobs):
            cur_job[0] = ji
            if ji + 1 < len(jobs):
                load(ji + 1)
            do_job(ji, kind, h)
        cur_job[0] = len(jobs)
        flush()
        notify_done()
        sc.emit()
import numpy as np
import concourse.bass as bass
import concourse.mybir as mybir
from contextlib import ExitStack

F32 = mybir.dt.float32
BF = mybir.dt.bfloat16
AF = mybir.ActivationFunctionType
ALU = mybir.AluOpType
AX = mybir.AxisListType
S = 8192
_UID = [0]


def build_outproj(nc, sc, yT, wout, xres, ident, gvec, out_main, out_xT, final, ntok=4096):
    with ExitStack() as es:
        _UID[0] += 1
        _u = _UID[0]
        sb = lambda n, s, d: es.enter_context(nc.sbuf_tensor(f"{n}_u{_u}", s, d))
        ps = lambda n, s, d: es.enter_context(nc.psum_tensor(f"{n}_u{_u}", s, d))
        Wo = sb("Wo", [128, 16, 1024], BF)
        wst = [sb(f"wst{i}", [128, 1024], F32) for i in range(2)]
        yt = [sb(f"yt{i}", [128, 16, 512], BF) for i in range(2)]
        xt = [sb(f"xt{i}", [128, 1024], F32) for i in range(2)]
        x1 = [sb(f"x1{i}", [128, 1024], F32) for i in range(2)]
        junk = sb("junkb", [128, 1024], F32)
        ss = sb("ssb", [128, 8], F32)
        eps_t = sb("epsb", [128, 1], F32)
        id_t = sb("idb", [128, 128], BF)
        xn = [sb(f"xnb{i}", [128, 1024], BF) for i in range(2)]
        xTs = [sb(f"xTs{i}", [128, 8, 512], BF) for i in range(2)]
        gft = sb("gft", [128, 1024], F32)
        psA = [ps(f"psA{i}", [128, 512], F32) for i in range(4)]
        psT = [ps(f"psTb{i}", [128, 8, 128], BF) for i in range(2)]
        sc.dma("sp", lambda e: e.dma_start(out=id_t[:], in_=ident), writes=["id_t"])
        sc.op("dve", lambda e: e.memset(eps_t[:], 1e-6), writes=["eps_t"])
        if final:
            gsrc = bass.AP(gvec.tensor, 0, [[0, 128], [1, 1024]])
            sc.dma("sp", lambda e: e.dma_start(out=gft[:], in_=gsrc), writes=["gft"])
        for kc in range(16):
            s = kc % 2
            sc.dma("sp" if s == 0 else "pool", lambda e, kc=kc, s=s: e.dma_start(out=wst[s][:], in_=wout[kc * 128:(kc + 1) * 128, :]),
                   writes=[("wst", s)])
            sc.op("dve" if s == 0 else "pool", lambda e, kc=kc, s=s: e.tensor_copy(out=Wo[:, kc, :], in_=wst[s][:]),
                  reads=[("wst", s)], writes=[("Wo", kc)])
        ti = 0
        for tb in range(ntok // 512):
            s = tb % 2
            ysrc = yT[:, tb * 512:(tb + 1) * 512].rearrange("(kc p) t -> p kc t", p=128)
            sc.dma("sp", lambda e, s=s, ysrc=ysrc: e.dma_start(out=yt[s][:], in_=ysrc), writes=[("yt", s)])
            for j in range(4):
                xs = ti % 2
                col = ti % 8
                ti += 1
                r0 = tb * 512 + j * 128
                sc.dma("pool", lambda e, xs=xs, r0=r0: e.dma_start(out=xt[xs][:], in_=xres[r0:r0 + 128, :]), writes=[("xt", xs)])
                for half in range(2):
                    pb = (xs * 2 + half)
                    for kc in range(16):
                        sc.op("pe", lambda e, pb=pb, kc=kc, s=s, j=j, half=half: e.matmul(
                            psA[pb][:], lhsT=yt[s][:, kc, j * 128:(j + 1) * 128], rhs=Wo[:, kc, half * 512:(half + 1) * 512],
                            start=(kc == 0), stop=(kc == 15)),
                            reads=[("yt", s), ("Wo", kc)], writes=[("psA", pb)])
                    sc.op("dve", lambda e, pb=pb, xs=xs, half=half: e.tensor_tensor(
                        out=x1[xs][:, half * 512:(half + 1) * 512], in0=xt[xs][:, half * 512:(half + 1) * 512], in1=psA[pb][:], op=ALU.add),
                        reads=[("psA", pb), ("xt", xs)], writes=[("x1", xs)])
                if not final:
                    sc.dma("sp", lambda e, xs=xs, r0=r0: e.dma_start(out=out_main[r0:r0 + 128, :], in_=x1[xs][:]), reads=[("x1", xs)])
                sc.op("act", lambda e, xs=xs: e.activation(out=junk[:], in_=x1[xs][:], func=AF.Square), reads=[("x1", xs)], writes=["junk"])
                sc.op("dve", lambda e, col=col: e.reduce_sum(out=ss[:, col:col + 1], in_=junk[:], axis=AX.X), reads=["junk"], writes=[("ss", col)])
                sc.op("act", lambda e, col=col: e.activation(out=ss[:, col:col + 1], in_=ss[:, col:col + 1], func=AF.Sqrt,
                                                             bias=eps_t[:, 0:1], scale=1.0 / 1024.0),
                      reads=[("ss", col), "eps_t"], writes=[("ss", col)])
                sc.op("dve", lambda e, col=col: e.reciprocal(out=ss[:, col:col + 1], in_=ss[:, col:col + 1]), reads=[("ss", col)], writes=[("ss", col)])
                if final:
                    sc.op("dve", lambda e, xs=xs, col=col: e.scalar_tensor_tensor(
                        out=xt[xs][:], in0=x1[xs][:], scalar=ss[:, col:col + 1], in1=gft[:], op0=ALU.mult, op1=ALU.mult),
                        reads=[("x1", xs), ("ss", col), "gft"], writes=[("xt", xs)])
                    sc.dma("sp", lambda e, xs=xs, r0=r0: e.dma_start(out=out_main[r0:r0 + 128, :], in_=xt[xs][:]), reads=[("xt", xs)])
                else:
                    sc.op("dve", lambda e, xs=xs, col=col: e.tensor_scalar(out=xn[xs][:], in0=x1[xs][:], scalar1=ss[:, col:col + 1],
                                                                         scalar2=None, op0=ALU.mult),
                          reads=[("x1", xs), ("ss", col)], writes=[("xn", xs)])
                    for kc in range(8):
                        sc.op("pe", lambda e, xs=xs, kc=kc: e.transpose(out=psT[xs][:, kc, :], in_=xn[xs][:, kc * 128:(kc + 1) * 128], identity=id_t[:]),
                              reads=[("xn", xs), "id_t"], writes=[("psT", xs)])
                    sc.op("act", lambda e, xs=xs, s=s, j=j: e.copy(out=xTs[s][:, :, j * 128:(j + 1) * 128], in_=psT[xs][:]),
                          reads=[("psT", xs)], writes=[("xTs", s)])
            if not final:
                dst = out_xT[:, tb * 512:(tb + 1) * 512].rearrange("(kc p) t -> p kc t", p=128)
                sc.dma("sp", lambda e, s=s, dst=dst: e.dma_start(out=dst, in_=xTs[s][:]), reads=[("xTs", s)])
        sc.emit()


NW1 = 3072


def build_retention(nc, sc, xsrc_fn, w1, gn, ident, cosT, sinT, maskR, qdT, kdec, cd, ydst_fn, ntb=16, on_block_done=None):
    with ExitStack() as es:
        _UID[0] += 1
        _u = _UID[0]
        sb = lambda n, s, d: es.enter_context(nc.sbuf_tensor(f"{n}_u{_u}", s, d))
        ps = lambda n, s, d: es.enter_context(nc.psum_tensor(f"{n}_u{_u}", s, d))
        Wb = sb("W1b", [128, 8, NW1], BF)
        wst = [sb(f"w1st{i}", [128, NW1], F32) for i in range(2)]
        gn_t = sb("gn1", [128, 8], F32)
        gn16 = sb("gn16", [128, 8], F32)
        id_t = sb("idc", [128, 128], BF)
        eps_t = sb("epsc", [128, 1], F32)
        mk = sb("mkR", [128, 2, 128], F32)
        qd = sb("qd_sb", [128, 2, 128], F32)
        kd = sb("kd_sb", [128, 2], F32)
        xT = [sb(f"xTc{i}", [128, 8, 512], BF) for i in range(2)]
        cs_t = [sb(f"cos{i}", [128, 512], F32) for i in range(2)]
        sn_t = [sb(f"sin{i}", [128, 512], F32) for i in range(2)]
        tm = [sb(f"tm{i}", [128, 4, 512], F32) for i in range(2)]
        QT = [sb(f"QT{i}", [128, 2, 2, 512], BF) for i in range(2)]
        QdT = [sb(f"QdT{i}", [128, 2, 2, 512], BF) for i in range(2)]
        KT = [sb(f"KT{i}", [128, 2, 2, 512], BF) for i in range(2)]
        Ktok = [sb(f"Ktok{i}", [128, 4, 2, 256], BF) for i in range(2)]
        Vt = [sb(f"Vc{i}", [128, 4, 2, 512], BF) for i in range(2)]
        Gt = [sb(f"Gc{i}", [128, 4, 2, 512], BF) for i in range(2)]
        St = sb("St", [128, 2, 2, 512], F32)
        Stb = [sb(f"Stb{i}", [128, 2, 2, 512], BF) for i in range(2)]
        Sm = [sb(f"Sm{i}", [128, 128], BF) for i in range(2)]
        stats = sb("stats", [128, 2, 2, 6], F32)
        mv = sb("mv", [128, 2, 2, 2], F32)
        yn = [sb(f"yn{i}", [128, 512], F32) for i in range(2)]
        y2 = [sb(f"y2{i}", [128, 1024], BF) for i in range(2)]
        y2s = [sb(f"y2s{i}", [128, 8, 128], BF) for i in range(2)]
        psF = [ps(f"pcF{i}", [128, 512], F32) for i in range(2)]
        psS = ps("pcS", [128, 2, 128], F32)
        psO = [ps(f"pcO{i}", [128, 512], F32) for i in range(2)]
        psU = [ps(f"pcU{i}", [128, 512], F32) for i in range(2)]
        psT = ps("pcT", [128, 8, 128], BF)

        for (t, src, nm) in ((gn_t, gn, "gn_t"), (id_t, ident, "id_t"), (mk, maskR, "mk"), (qd, qdT, "qd"), (kd, kdec, "kd")):
            sc.dma("sp", lambda e, t=t, src=src: e.dma_start(out=t[:], in_=src), writes=[nm])
        cdt = sb("cdt", [128, 2], F32)
        cdsrc = bass.AP(cd.tensor, 0, [[0, 128], [1, 2]])
        sc.dma("sp", lambda e: e.dma_start(out=cdt[:], in_=cdsrc), writes=["cdt"])
        sc.op("dve", lambda e: e.memset(eps_t[:], 1e-6), writes=["eps_t"])
        sc.op("dve", lambda e: e.memset(St[:], 0.0), writes=[("St", a, b) for a in range(2) for b in range(2)])
        sc.op("dve", lambda e: e.memset(Stb[0][:], 0.0), writes=[("Stb", 0, a, b) for a in range(2) for b in range(2)])
        sc.op("dve", lambda e: e.tensor_scalar(out=gn16[:], in0=gn_t[:], scalar1=1.0 / 16.0, scalar2=None, op0=ALU.mult),
              reads=["gn_t"], writes=["gn16"])
        for kc in range(8):
            s = kc % 2
            sc.dma("sp" if s == 0 else "pool", lambda e, kc=kc, s=s: e.dma_start(out=wst[s][:], in_=w1[kc * 128:(kc + 1) * 128, :]),
                   writes=[("wst", s)])
            eng = "dve" if s == 0 else "pool"
            for (c0, c1, gt, gname) in ((0, 512, gn_t, "gn_t"), (512, 1024, gn16, "gn16"), (1024, NW1, gn_t, "gn_t")):
                sc.op(eng, lambda e, kc=kc, s=s, c0=c0, c1=c1, gt=gt: e.tensor_scalar(
                    out=Wb[:, kc, c0:c1], in0=wst[s][:, c0:c1], scalar1=gt[:, kc:kc + 1], scalar2=None, op0=ALU.mult),
                    reads=[("wst", s), gname], writes=[("Wb", kc, c0)])
        WR = lambda kc: [("Wb", kc, 0), ("Wb", kc, 512), ("Wb", kc, 1024)]
        chunk_i = 0
        deferred = [None]
        gn_def = [None]
        kdefer = []
        def load_blk(tb):
            s = tb % 2
            t0 = tb * 512
            src = xsrc_fn(tb).rearrange("(kc p) t -> p kc t", p=128)
            sc.dma("sp", lambda e: e.dma_start(out=xT[s][:, 0:4, :], in_=src[:, 0:4, :]), writes=[("xT", s)])
            sc.dma("act", lambda e: e.dma_start(out=xT[s][:, 4:8, :], in_=src[:, 4:8, :]), writes=[("xT", s)])
            sc.dma("pool", lambda e: e.dma_start(out=cs_t[s][:], in_=cosT[:, t0:t0 + 512]), writes=[("cos", s)])
            sc.dma("pool", lambda e: e.dma_start(out=sn_t[s][:], in_=sinT[:, t0:t0 + 512]), writes=[("sin", s)])

        load_blk(0)
        for tb in range(ntb):
            s = tb % 2
            t0 = tb * 512
            if tb + 1 < ntb:
                load_blk(tb + 1)
            for gi in range(4):
                isk, h = gi // 2, gi % 2
                fb = [psF[0], psF[1]] if gi % 2 == 0 else [psO[0], psO[1]]
                fr = [("psF", 0), ("psF", 1)] if gi % 2 == 0 else [("psO", 0), ("psO", 1)]
                for eo in range(2):
                    cg = gi * 2 + eo
                    for kc in range(8):
                        sc.op("pe", lambda e, eo=eo, kc=kc, cg=cg, s=s, fb=fb: e.matmul(
                            fb[eo][:], lhsT=Wb[:, kc, cg * 128:(cg + 1) * 128], rhs=xT[s][:, kc, :], start=(kc == 0), stop=(kc == 7)),
                            reads=[("xT", s)] + WR(kc), writes=[fr[eo]])
                ts = gi % 2
                RT = ("tm", ts)
                sc.op("dve", lambda e, ts=ts, s=s, fb=fb: e.tensor_tensor(out=tm[ts][:, 0, :], in0=fb[0][:], in1=cs_t[s][:], op=ALU.mult),
                      reads=[fr[0], ("cos", s)], writes=[RT])
                sc.op("dve", lambda e, ts=ts, s=s, fb=fb: e.tensor_tensor(out=tm[ts][:, 1, :], in0=fb[1][:], in1=sn_t[s][:], op=ALU.mult),
                      reads=[fr[1], ("sin", s)], writes=[RT])
                sc.op("dve", lambda e, ts=ts, s=s, fb=fb: e.tensor_tensor(out=tm[ts][:, 2, :], in0=fb[0][:], in1=sn_t[s][:], op=ALU.mult),
                      reads=[fr[0], ("sin", s)], writes=[RT])
                sc.op("dve", lambda e, ts=ts, s=s, fb=fb: e.tensor_tensor(out=tm[ts][:, 3, :], in0=fb[1][:], in1=cs_t[s][:], op=ALU.mult),
                      reads=[fr[1], ("cos", s)], writes=[RT])
                dst = (KT if isk else QT)[s]
                RD = ("KT" if isk else "QT", s)
                sc.op("pool", lambda e, ts=ts, dst=dst, h=h: e.tensor_tensor(out=dst[:, h, 0, :], in0=tm[ts][:, 0, :], in1=tm[ts][:, 1, :], op=ALU.subtract),
                      reads=[RT], writes=[RD])
                sc.op("pool", lambda e, ts=ts, dst=dst, h=h: e.tensor_tensor(out=dst[:, h, 1, :], in0=tm[ts][:, 2, :], in1=tm[ts][:, 3, :], op=ALU.add),
                      reads=[RT], writes=[RD])
                if not isk:
                    for eo in range(2):
                        qv = QT[s][:, h, eo, :].rearrange("p (c t) -> p c t", c=4)
                        ov = QdT[s][:, h, eo, :].rearrange("p (c t) -> p c t", c=4)
                        base = qd[:, h, :]
                        dv_ = bass.AP(base.tensor, base.offset, [list(base.ap[0]), [0, 4], [1, 128]])
                        sc.op("dve", lambda e, qv=qv, ov=ov, dv_=dv_: e.tensor_tensor(out=ov, in0=qv, in1=dv_, op=ALU.mult),
                              reads=[RD, "qd"], writes=[("QdT", s)])
                else:
                    def ktrans(s=s, h=h, RD=RD):
                        for c in range(4):
                            for eo in range(2):
                                sc.op("pe", lambda e, c=c, eo=eo: e.transpose(
                                    out=psT[:, c * 2 + eo, :], in_=KT[s][:, h, eo, c * 128:(c + 1) * 128], identity=id_t[:]),
                                    reads=[RD, "id_t"], writes=["psT"])
                        kv = Ktok[s][:, :, h, :].rearrange("p c (eo i) -> p c eo i", eo=2)
                        pv = psT[:].rearrange("p (c eo) i -> p c eo i", eo=2)
                        sc.op("act", lambda e: e.activation(out=kv, in_=pv, func=AF.Copy, scale=kd[:, h:h + 1]),
                              reads=["psT", "kd"], writes=[("Ktok", s)])
                    kdefer.append(ktrans)
            for j in range(4):
                for grp in range(4):
                    pb = grp % 2
                    for kc in range(8):
                        sc.op("pe", lambda e, pb=pb, kc=kc, j=j, grp=grp, s=s: e.matmul(
                            psU[pb][:], lhsT=xT[s][:, kc, j * 128:(j + 1) * 128], rhs=Wb[:, kc, 1024 + grp * 512:1024 + (grp + 1) * 512],
                            start=(kc == 0), stop=(kc == 7)),
                            reads=[("xT", s)] + WR(kc), writes=[("psU", pb)])
                    if grp < 2:
                        sc.op("act", lambda e, pb=pb, j=j, grp=grp, s=s: e.copy(out=Vt[s][:, j, grp, :], in_=psU[pb][:]),
                              reads=[("psU", pb)], writes=[("Vt", s)])
                    else:
                        sc.op("act", lambda e, pb=pb, j=j, grp=grp, s=s: e.activation(out=Gt[s][:, j, grp - 2, :], in_=psU[pb][:], func=AF.Silu),
                              reads=[("psU", pb)], writes=[("Gt", s)])
            for f in kdefer:
                f()
            kdefer.clear()
            for c in range(4):
                cur = chunk_i % 2
                nxt = 1 - cur
                ys = chunk_i % 2
                cc = slice(c * 128, (c + 1) * 128)
                oset = chunk_i % 2
                ob = [psO[0], psO[1]] if oset == 0 else [psF[0], psF[1]]
                orr = [("psO", 0), ("psO", 1)] if oset == 0 else [("psF", 0), ("psF", 1)]
                for h in range(2):
                    for eo in range(2):
                        sc.op("pe", lambda e, h=h, eo=eo, s=s, cc=cc: e.matmul(
                            psS[:, h, :], lhsT=KT[s][:, h, eo, cc], rhs=QT[s][:, h, eo, cc], start=(eo == 0), stop=(eo == 1)),
                            reads=[("KT", s), ("QT", s)], writes=["psSbank"])
                    sc.op("dve", lambda e, h=h: e.tensor_tensor(out=Sm[h][:], in0=psS[:, h, :], in1=mk[:, h, :], op=ALU.mult),
                          reads=["psSbank", "mk"], writes=[("Sm", h)])
                if gn_def[0] is not None:
                    gn_def[0]()
                    gn_def[0] = None
                for h in range(2):
                    sc.op("pe", lambda e, h=h, s=s, c=c, ob=ob: e.matmul(ob[h][:], lhsT=Sm[h][:], rhs=Vt[s][:, c, h, :], start=True, stop=False),
                          reads=[("Sm", h), ("Vt", s)], writes=[orr[h]])
                    for eo in range(2):
                        sc.op("pe", lambda e, h=h, eo=eo, s=s, cc=cc, cur=cur, ob=ob: e.matmul(
                            ob[h][:], lhsT=QdT[s][:, h, eo, cc], rhs=Stb[cur][:, eo, h, :], start=False, stop=(eo == 1)),
                            reads=[("QdT", s), ("Stb", cur, eo, h)], writes=[orr[h]])
                    for half in range(2):
                        sc.op("pe", lambda e, h=h, half=half, s=s, c=c: e.matmul(
                            psU[half][:], lhsT=Ktok[s][:, c, h, half * 128:(half + 1) * 128], rhs=Vt[s][:, c, h, :], start=True, stop=True),
                            reads=[("Ktok", s), ("Vt", s)], writes=[("psU", half)])
                        sc.op("dve", lambda e, h=h, half=half: e.scalar_tensor_tensor(
                            out=St[:, half, h, :], in0=St[:, half, h, :], scalar=cdt[:, h:h + 1], in1=psU[half][:], op0=ALU.mult, op1=ALU.add),
                            reads=[("St", half, h), ("psU", half), "cdt"], writes=[("St", half, h)])
                        sc.op("act", lambda e, h=h, half=half, nxt=nxt: e.copy(out=Stb[nxt][:, half, h, :], in_=St[:, half, h, :]),
                              reads=[("St", half, h)], writes=[("Stb", nxt, half, h)])

                def groupnorm(ys=ys, ob=ob, orr=orr, s=s, c=c):
                    for h in range(2):
                        sr = ("stats", ys, h)
                        sc.op("dve", lambda e, h=h: e.bn_stats(out=stats[:, ys, h, :], in_=ob[h][:]), reads=[orr[h]], writes=[sr])
                        sc.op("dve", lambda e, h=h: e.bn_aggr(out=mv[:, ys, h, :], in_=stats[:, ys, h, :]), reads=[sr], writes=[("mv", ys, h)])
                        sc.op("act", lambda e, h=h: e.activation(out=mv[:, ys, h, 1:2], in_=mv[:, ys, h, 1:2], func=AF.Sqrt, bias=eps_t[:, 0:1], scale=1.0),
                              reads=[("mv", ys, h), "eps_t"], writes=[("mv", ys, h)])
                    for h in range(2):
                        sc.op("dve", lambda e, h=h: e.reciprocal(out=mv[:, ys, h, 1:2], in_=mv[:, ys, h, 1:2]), reads=[("mv", ys, h)], writes=[("mv", ys, h)])
                        sc.op("dve", lambda e, h=h: e.tensor_scalar(out=yn[h][:], in0=ob[h][:], scalar1=mv[:, ys, h, 0:1], scalar2=mv[:, ys, h, 1:2],
                                                                    op0=ALU.subtract, op1=ALU.mult),
                              reads=[orr[h], ("mv", ys, h)], writes=[("yn", h)])
                        sc.op("pool", lambda e, h=h: e.tensor_tensor(out=y2[ys][:, h * 512:(h + 1) * 512], in0=yn[h][:], in1=Gt[s][:, c, h, :], op=ALU.mult),
                              reads=[("yn", h), ("Gt", s)], writes=[("y2", ys, h)])
                gn_def[0] = groupnorm

                def epilogue(ys=ys, tb=tb, c=c):
                    for kc in range(8):
                        sc.op("pe", lambda e, kc=kc: e.transpose(out=psT[:, kc, :], in_=y2[ys][:, kc * 128:(kc + 1) * 128], identity=id_t[:]),
                              reads=[("y2", ys, kc // 4), "id_t"], writes=["psT"])
                    sc.op("act", lambda e: e.copy(out=y2s[ys][:], in_=psT[:]), reads=["psT"], writes=[("y2s", ys)])
                    dst = ydst_fn(tb, c).rearrange("(kc p) t -> p kc t", p=128)
                    sc.dma("sp", lambda e: e.dma_start(out=dst, in_=y2s[ys][:]), reads=[("y2s", ys)], writes=[("y2d", tb)])
                    if c == 3 and on_block_done is not None:
                        on_block_done(tb)

                if deferred[0] is not None:
                    deferred[0]()
                deferred[0] = epilogue
                chunk_i += 1
            if gn_def[0] is not None:
                gn_def[0]()
                gn_def[0] = None

        if gn_def[0] is not None:
            gn_def[0]()
            gn_def[0] = None
        if deferred[0] is not None:
            deferred[0]()
        sc.emit()
import numpy as np
import concourse.bass as bass
import concourse.mybir as mybir
from contextlib import ExitStack

F32 = mybir.dt.float32
BF = mybir.dt.bfloat16
AF = mybir.ActivationFunctionType
ALU = mybir.AluOpType
AX = mybir.AxisListType
S = 8192
_UID = [0]


def build_outproj_p1(nc, sc, ysrc_fn, wout, xres, xout, ssloc):
    with ExitStack() as es:
        _UID[0] += 1
        _u = _UID[0]
        sb = lambda n, s, d: es.enter_context(nc.sbuf_tensor(f"{n}_u{_u}", s, d))
        ps = lambda n, s, d: es.enter_context(nc.psum_tensor(f"{n}_u{_u}", s, d))
        Wo = sb("Wo", [128, 16, 512], BF)
        wst = [sb(f"wst{i}", [128, 512], F32) for i in range(2)]
        yt = [sb(f"yt{i}", [128, 16, 512], BF) for i in range(2)]
        xt = [sb(f"xt{i}", [128, 512], F32) for i in range(2)]
        x1 = [sb(f"x1{i}", [128, 512], F32) for i in range(2)]
        junk = sb("junkb", [128, 512], F32)
        ssp = sb("ssp", [128, 64], F32)
        psA = [ps(f"psA{i}", [128, 512], F32) for i in range(4)]
        for kc in range(16):
            s = kc % 2
            sc.dma("sp" if s == 0 else "pool", lambda e, kc=kc, s=s: e.dma_start(out=wst[s][:], in_=wout[kc * 128:(kc + 1) * 128, :]),
                   writes=[("wst", s)])
            sc.op("dve" if s == 0 else "pool", lambda e, kc=kc, s=s: e.tensor_copy(out=Wo[:, kc, :], in_=wst[s][:]),
                  reads=[("wst", s)], writes=[("Wo", kc)])
        ti = 0

        def load_y(tb):
            s = tb % 2
            ysrc = ysrc_fn(tb).rearrange("(kc p) t -> p kc t", p=128)
            for q4, qn in enumerate(("sp", "act", "sp", "act")):
                sc.dma(qn, lambda e, q4=q4: e.dma_start(out=yt[s][:, q4 * 4:(q4 + 1) * 4, :], in_=ysrc[:, q4 * 4:(q4 + 1) * 4, :]),
                       writes=[("yt", s, q4)])

        load_y(0)
        for tb in range(S // 512):
            s = tb % 2
            if tb + 1 < S // 512:
                load_y(tb + 1)
            for j in range(4):
                xs = ti % 2
                pb = ti % 4
                tile = ti
                ti += 1
                r0 = tb * 512 + j * 128
                sc.dma("pool", lambda e, xs=xs, r0=r0: e.dma_start(out=xt[xs][:], in_=xres[r0:r0 + 128, :]), writes=[("xt", xs)])
                for kc in range(16):
                    sc.op("pe", lambda e, pb=pb, kc=kc, s=s, j=j: e.matmul(
                        psA[pb][:], lhsT=yt[s][:, kc, j * 128:(j + 1) * 128], rhs=Wo[:, kc, :], start=(kc == 0), stop=(kc == 15)),
                        reads=[("yt", s, kc // 4), ("Wo", kc)], writes=[("psA", pb)])
                sc.op("dve", lambda e, pb=pb, xs=xs: e.tensor_tensor(out=x1[xs][:], in0=xt[xs][:], in1=psA[pb][:], op=ALU.add),
                      reads=[("psA", pb), ("xt", xs)], writes=[("x1", xs)])
                sc.dma("sp", lambda e, xs=xs, r0=r0: e.dma_start(out=xout[r0:r0 + 128, :], in_=x1[xs][:]), reads=[("x1", xs)])
                sc.op("act", lambda e, xs=xs: e.activation(out=junk[:], in_=x1[xs][:], func=AF.Square), reads=[("x1", xs)], writes=["junk"])
                sc.op("dve", lambda e, tile=tile: e.reduce_sum(out=ssp[:, tile:tile + 1], in_=junk[:], axis=AX.X),
                      reads=["junk"], writes=["ssp"])
        sc.dma("sp", lambda e: e.dma_start(out=ssloc, in_=ssp[:]), reads=["ssp"])
        sc.emit()


def build_outproj_p2(nc, sc, xin, ssall, ident, gvec, out_main, xdst_fn, final, on_block_done=None):
    with ExitStack() as es:
        _UID[0] += 1
        _u = _UID[0]
        sb = lambda n, s, d: es.enter_context(nc.sbuf_tensor(f"{n}_u{_u}", s, d))
        ps = lambda n, s, d: es.enter_context(nc.psum_tensor(f"{n}_u{_u}", s, d))
        ssa = sb("ssa", [128, 2, 64], F32)
        rstd = sb("rstd", [128, 64], F32)
        eps_t = sb("epsb", [128, 1], F32)
        id_t = sb("idb", [128, 128], BF)
        gft = sb("gft", [128, 512], F32)
        xt = [sb(f"xq{i}", [128, 512], F32) for i in range(4)]
        xo = [sb(f"xo{i}", [128, 512], F32) for i in range(4)]
        xn = [sb(f"xnb{i}", [128, 512], BF) for i in range(4)]
        xTs = [sb(f"xTs{i}", [128, 4, 512], BF) for i in range(2)]
        psT = [ps(f"psTb{i}", [128, 4, 128], BF) for i in range(4)]
        sc.dma("sp", lambda e: e.dma_start(out=id_t[:], in_=ident), writes=["id_t"])
        sc.op("dve", lambda e: e.memset(eps_t[:], 1e-6), writes=["eps_t"])
        sc.dma("sp", lambda e: e.dma_start(out=ssa[:], in_=ssall.rearrange("(r p) t -> p r t", p=128)), reads=["ssall"], writes=["ssa"])
        if final:
            gsrc = bass.AP(gvec.tensor, 0, [[0, 128], [1, 512]])
            sc.dma("sp", lambda e: e.dma_start(out=gft[:], in_=gsrc), writes=["gft"])
        sc.op("dve", lambda e: e.tensor_tensor(out=rstd[:], in0=ssa[:, 0, :], in1=ssa[:, 1, :], op=ALU.add), reads=["ssa"], writes=["rstd"])
        sc.op("act", lambda e: e.activation(out=rstd[:], in_=rstd[:], func=AF.Sqrt, bias=eps_t[:, 0:1], scale=1.0 / 1024.0),
              reads=["rstd", "eps_t"], writes=["rstd"])
        sc.op("dve", lambda e: e.reciprocal(out=rstd[:], in_=rstd[:]), reads=["rstd"], writes=["rstd"])
        ti = 0
        for tb in range(S // 512):
            s = tb % 2
            for j in range(4):
                xs = ti % 4
                tile = ti
                ti += 1
                r0 = tb * 512 + j * 128
                sc.dma("sp" if xs % 2 == 0 else "pool", lambda e, xs=xs, r0=r0: e.dma_start(out=xt[xs][:], in_=xin[r0:r0 + 128, :]), writes=[("xt", xs)])
                if final:
                    sc.op("dve", lambda e, xs=xs, tile=tile: e.scalar_tensor_tensor(
                        out=xo[xs][:], in0=xt[xs][:], scalar=rstd[:, tile:tile + 1], in1=gft[:], op0=ALU.mult, op1=ALU.mult),
                        reads=[("xt", xs), "rstd", "gft"], writes=[("xo", xs)])
                    sc.dma("sp", lambda e, xs=xs, r0=r0: e.dma_start(out=out_main[r0:r0 + 128, :], in_=xo[xs][:]), reads=[("xo", xs)])
                else:
                    sc.op("dve", lambda e, xs=xs, tile=tile: e.tensor_scalar(out=xn[xs][:], in0=xt[xs][:], scalar1=rstd[:, tile:tile + 1],
                                                                           scalar2=None, op0=ALU.mult),
                          reads=[("xt", xs), "rstd"], writes=[("xn", xs)])
                    for kc in range(4):
                        sc.op("pe", lambda e, xs=xs, kc=kc: e.transpose(out=psT[xs][:, kc, :], in_=xn[xs][:, kc * 128:(kc + 1) * 128], identity=id_t[:]),
                              reads=[("xn", xs), "id_t"], writes=[("psT", xs)])
                    sc.op("act", lambda e, xs=xs, s=s, j=j: e.copy(out=xTs[s][:, :, j * 128:(j + 1) * 128], in_=psT[xs][:]),
                          reads=[("psT", xs)], writes=[("xTs", s)])
            if not final:
                dst = xdst_fn(tb).rearrange("(kc p) t -> p kc t", p=128)
                sc.dma("sp", lambda e, s=s, dst=dst: e.dma_start(out=dst, in_=xTs[s][:]), reads=[("xTs", s)], writes=[("xnd", tb)])
                if on_block_done is not None:
                    on_block_done(tb)
        sc.emit()
import ml_dtypes
import numpy as np, ml_dtypes
bf = ml_dtypes.bfloat16
S = 8192
def consts_common():
    kk = np.arange(128)[:, None]; qq = np.arange(128)[None, :]
    maskD = np.concatenate([np.where(qq >= kk, 0.0, -30000.0), np.where(kk >= qq, 0.0, -30000.0)], axis=1).astype(bf)
    return {"ident": np.eye(128, dtype=bf), "maskD": maskD}
def rot_tables():
    inv = (1.0 / (np.float32(10000.0) ** np.linspace(0.0, 1.0, 128, dtype=np.float32))).astype(np.float32)
    ang = (np.arange(S, dtype=np.float32)[:, None] * inv[None, :]).astype(np.float32)
    return np.ascontiguousarray(np.cos(ang.astype(np.float64)).T.astype(np.float32)), np.ascontiguousarray(np.sin(ang.astype(np.float64)).T.astype(np.float32))
def decay_tables(hp):
    Hs = [2 * hp, 2 * hp + 1]
    lg = [float(np.log1p(-np.float32(2.0) ** np.float32(-5.0 - H))) for H in Hs]
    pos = np.arange(128, dtype=np.float64)
    maskR = np.zeros((128, 2, 128), np.float32); qdT = np.zeros((128, 2, 128), np.float32); kdec = np.zeros((128, 2), np.float32); cd = []
    for i, l in enumerate(lg):
        rel = pos[None, :] - pos[:, None]
        maskR[:, i, :] = np.where(rel >= 0, np.exp(l * np.maximum(rel, 0)), 0.0)
        qdT[:, i, :] = np.exp(l * (pos + 1.0))[None, :]
        kdec[:, i] = np.exp(l * (127.0 - pos))
        cd.append(float(np.exp(l * 128.0)))
    return maskR, qdT, kdec, cd
def w1_core(Wodd, hp):
    Hs = [2 * hp, 2 * hp + 1]
    cols = []
    for base in (0, 1024):
        for H in Hs:
            cols.append(Wodd[:, base + H * 256: base + (H + 1) * 256][:, 0::2])
            cols.append(Wodd[:, base + H * 256: base + (H + 1) * 256][:, 1::2])
    for base in (2048, 4096):
        for H in Hs:
            cols.append(Wodd[:, base + H * 512: base + (H + 1) * 512])
    return np.ascontiguousarray(np.concatenate(cols, axis=1))


from concourse.bass_utils import run_bass_kernel_spmd

NCORES = 8
GROUPS = [[0, 1], [2, 3], [4, 5], [6, 7]]


def _dt(a):
    return BF if a.dtype == bf else F32


def _core_inputs(inputs, core):
    cc = consts_common()
    b, r = core // 2, core % 2
    W = inputs["even_w_in"][0]
    hs = slice(r * 512, (r + 1) * 512)
    o = 4112
    parts = [W[:, 0:1024][:, hs], W[:, 1024:2048][:, hs], W[:, 3072:4096][:, hs],
             W[:, o:o + 1024][:, hs], W[:, o + 1024:o + 2048][:, hs], W[:, o + 3072:o + 4096][:, hs],
             W[:, 2048:3072][:, hs], W[:, o + 2048:o + 3072][:, hs], W[:, 4096 + r * 8:4096 + (r + 1) * 8]]
    perm = []
    for job in range(16):
        for rk in range(2):
            base = (rk * 8 + job) * 64 if job < 8 else 1024 + (rk * 8 + job - 8) * 64
            perm.extend(range(base, base + 64))
    perm = np.array(perm)
    cosT, sinT = rot_tables()
    maskR, qdT, kdec, cd = decay_tables(r)
    x = inputs["x"][b]
    return {"x": np.ascontiguousarray(x),
            "w": np.ascontiguousarray(np.concatenate(parts, axis=1)),
            "gn": np.ascontiguousarray(inputs["even_norm"][0].reshape(8, 128).T),
            "bfv": np.ascontiguousarray(inputs["even_b_f"][0][r * 8:(r + 1) * 8].reshape(8, 1)),
            "ident": cc["ident"], "maskD": cc["maskD"],
            "wo0": np.ascontiguousarray(inputs["even_w_out"][0][perm][:, hs]),
            "xres0": np.ascontiguousarray(x[:, hs]),
            "w1": w1_core(inputs["odd_w_in"][0], r),
            "gn1": np.ascontiguousarray(inputs["odd_norm"][0].reshape(8, 128).T),
            "cosT": cosT, "sinT": sinT, "maskR": maskR, "qdT": qdT, "kdec": kdec,
            "cdv": np.array(cd, np.float32).reshape(1, 2),
            "wo1": np.ascontiguousarray(inputs["odd_w_out"][0][:, hs]),
            "gfin": np.ascontiguousarray(inputs["final_norm"][hs].reshape(1, 512))}


def _build(sample):
    nc = bass.Bass("TRN2", target_bir_lowering=False)
    d = {k: nc.dram_tensor(k, list(v.shape), _dt(v), kind="ExternalInput").ap() for k, v in sample.items()}
    out = nc.dram_tensor("out", [S, 512], F32, kind="ExternalOutput").ap()
    scr = lambda n, shp, dt: nc.dram_tensor(n, shp, dt).ap()
    featT = scr("featT", [NFEAT, S], BF)
    vtok = scr("vtok", [S, 1024], BF)
    caug = scr("caug", [8, 6, S], BF)
    yT = scr("yT", [1024, S], BF)
    yAll = scr("yAll", [2048, S], BF)
    x1c = scr("x1c", [S, 512], F32)
    ssl1 = scr("ssl1", [128, 64], F32)
    ssa1 = scr("ssa1", [256, 64], F32)
    xnblk = scr("xnblk", [16, 512, 512], BF)
    xnAllb = scr("xnAllb", [16, 1024, 512], BF)
    y2blk = scr("y2blk", [16, 1024, 512], BF)
    y2Allb = scr("y2Allb", [16, 2048, 512], BF)
    x2c = scr("x2c", [S, 512], F32)
    ssl2 = scr("ssl2", [128, 64], F32)
    ssa2 = scr("ssa2", [256, 64], F32)
    sc = Sched(nc)
    build_proj_phase(nc, sc, d["x"], d["w"], d["gn"], d["bfv"], d["ident"], featT, vtok, caug)
    build_attn_phase(nc, sc, featT, vtok, caug, d["ident"], d["maskD"], yT,
                     on_job_done=lambda job: sc.coll(yT[job * 64:(job + 1) * 64, :], yAll[job * 128:(job + 1) * 128, :], GROUPS,
                                                     reads=[("yTd", job)]))
    build_outproj_p1(nc, sc, lambda tb: yAll[:, tb * 512:(tb + 1) * 512], d["wo0"], d["xres0"], x1c, ssl1)
    sc.coll(ssl1, ssa1, GROUPS, writes=["ssall"])
    build_outproj_p2(nc, sc, x1c, ssa1, d["ident"], None, None, lambda tb: xnblk[tb], False,
                     on_block_done=lambda tb: sc.coll(xnblk[tb], xnAllb[tb], GROUPS, reads=[("xnd", tb)]))
    build_retention(nc, sc, lambda tb: xnAllb[tb], d["w1"], d["gn1"], d["ident"], d["cosT"], d["sinT"], d["maskR"], d["qdT"], d["kdec"],
                    d["cdv"], lambda tb, c: y2blk[tb][:, c * 128:(c + 1) * 128],
                    on_block_done=lambda tb: sc.coll(y2blk[tb], y2Allb[tb], GROUPS, reads=[("y2d", tb)]))
    build_outproj_p1(nc, sc, lambda tb: y2Allb[tb], d["wo1"], x1c, x2c, ssl2)
    sc.coll(ssl2, ssa2, GROUPS, writes=["ssall"])
    build_outproj_p2(nc, sc, x2c, ssa2, d["ident"], d["gfin"], out, None, True)
    return nc


def kernel(**inputs):
    inputs = {k: np.asarray(v) for k, v in inputs.items()}
    maps = [_core_inputs(inputs, c) for c in range(NCORES)]
    nc = _build(maps[0])
    res = run_bass_kernel_spmd(nc, maps, core_ids=list(range(NCORES))).results
    out = np.empty((4, S, 1024), np.float32)
    for core in range(NCORES):
        b, r = core // 2, core % 2
        out[b, :, r * 512:(r + 1) * 512] = np.asarray(res[core]["out"])
    return out
```
